# Optimizing a Trainium2 kernel written in Bass

```python
import jax, jax.numpy as jnp
from jax import lax
import numpy as np

D_MODEL = 2048
BATCH = 8
SEQ = 2048
DEPTH = 4

MEM_LEN = 256
BRANCH_W = 1024
N_BRANCH = 4
CHUNK = 128
GMLP_GROUPS = 8
GMLP_GROUP_W = BRANCH_W // GMLP_GROUPS
CONV_W = 31
POOL_WINDOWS = (2, 4, 8, 16)
POOL_GROUPS = 4
POOL_GROUP_W = BRANCH_W // POOL_GROUPS
MEM_HEADS = 4
MEM_HEAD_DIM = BRANCH_W // MEM_HEADS
EPS = 1e-6

SPLIT_SIZES = (BRANCH_W, BRANCH_W, BRANCH_W,
               2 * BRANCH_W, BRANCH_W,
               BRANCH_W, BRANCH_W,
               BRANCH_W, BRANCH_W,
               N_BRANCH * D_MODEL)
IN_COLS = sum(SPLIT_SIZES)

kernel_name = "hybrid_gmlp_conv_pool_memattn_trunk"


def rmsnorm(x, g):
    xf = x.astype(jnp.float32)
    y = xf * lax.rsqrt(jnp.mean(xf * xf, axis=-1, keepdims=True) + EPS)
    return (y * g.astype(jnp.float32)).astype(x.dtype)


def layernorm(x, g, b):
    xf = x.astype(jnp.float32)
    mu = jnp.mean(xf, axis=-1, keepdims=True)
    var = jnp.mean(jnp.square(xf - mu), axis=-1, keepdims=True)
    y = (xf - mu) * lax.rsqrt(var + EPS)
    return (y * g.astype(jnp.float32) + b.astype(jnp.float32)).astype(x.dtype)


def gmlp_spatial(u, v, ln_g, ln_b, w_s, b_s):
    bn, t, w = v.shape
    u = jax.nn.gelu(u)
    v = layernorm(jax.nn.gelu(v), ln_g, ln_b)
    mask = jnp.tril(jnp.ones((CHUNK, CHUNK), dtype=bool))
    ws = jnp.where(mask[None], w_s, 0).astype(v.dtype)
    vc = v.reshape(bn, t // CHUNK, CHUNK, GMLP_GROUPS, GMLP_GROUP_W)
    s = jnp.einsum('gts,bcsgd->bctgd', ws, vc) + b_s.T[:, :, None].astype(v.dtype)
    return u * s.reshape(bn, t, w)


def conformer_conv(z, dw_w, dw_b, ln_g, ln_b):
    a, b = jnp.split(z, 2, axis=-1)
    y = a * jax.nn.sigmoid(b)
    y = lax.conv_general_dilated(
        y, dw_w[:, None, :].astype(y.dtype), window_strides=(1,),
        padding=[(CONV_W - 1, 0)], dimension_numbers=('NWC', 'WIO', 'NWC'),
        feature_group_count=BRANCH_W) + dw_b
    y = layernorm(y, ln_g, ln_b)
    return jax.nn.silu(y)


def multiscale_pool(c, w_pool, scale):
    bn, t, w = c.shape
    cg = c.reshape(bn, t, POOL_GROUPS, POOL_GROUP_W).astype(jnp.float32)
    cs = jnp.cumsum(cg, axis=1)
    t1 = jnp.arange(1, t + 1)
    pooled = []
    for g, win in enumerate(POOL_WINDOWS):
        csg = cs[:, :, g]
        lag = jnp.pad(csg, ((0, 0), (win, 0), (0, 0)))[:, :t]
        cnt = jnp.minimum(t1, win).astype(jnp.float32)[None, :, None]
        pooled.append((csg - lag) / cnt)
    pooled = jnp.stack(pooled, axis=2)
    diff = (pooled - cg).astype(c.dtype)
    y = jnp.einsum('btgc,gcd->btgd', diff, w_pool)
    return y.reshape(bn, t, w) * scale


def memory_attention(q, mem_n, w_kv):
    bn, t, w = q.shape
    kv = jnp.einsum('bmd,dk->bmk', mem_n, w_kv)
    k, v = jnp.split(kv, 2, axis=-1)
    qh = q.reshape(bn, t, MEM_HEADS, MEM_HEAD_DIM) * (MEM_HEAD_DIM ** -0.5)
    kh = k.reshape(bn, -1, MEM_HEADS, MEM_HEAD_DIM)
    vh = v.reshape(bn, -1, MEM_HEADS, MEM_HEAD_DIM)
    s = jnp.einsum('bthd,bmhd->bhtm', qh, kh).astype(jnp.float32)
    p = jax.nn.softmax(s, axis=-1).astype(vh.dtype)
    o = jnp.einsum('bhtm,bmhd->bthd', p, vh)
    return o.reshape(bn, t, w)


def setup_inputs(seed: int = 0) -> dict:
    key = jax.random.key(seed)
    ks = jax.random.split(key, 20)
    f32 = jnp.float32
    nrm = lambda k, shp, sc: jax.random.normal(k, shp, f32) * sc
    L, D, W = DEPTH, D_MODEL, BRANCH_W
    return {
        "x": jax.random.normal(ks[0], (BATCH, SEQ, D), f32),
        "mem": jax.random.normal(ks[1], (BATCH, MEM_LEN, D), f32),
        "norm_g": 1.0 + nrm(ks[2], (L, D), 0.02),
        "w_in": nrm(ks[3], (L, D, IN_COLS), D ** -0.5),
        "gmlp_ln_g": 1.0 + nrm(ks[4], (L, W), 0.02),
        "gmlp_ln_b": nrm(ks[5], (L, W), 0.02),
        "gmlp_ws": nrm(ks[6], (L, GMLP_GROUPS, CHUNK, CHUNK), CHUNK ** -0.5),
        "gmlp_bs": 1.0 + nrm(ks[7], (L, GMLP_GROUPS, CHUNK), 0.02),
        "conv_w": nrm(ks[8], (L, CONV_W, W), CONV_W ** -0.5),
        "conv_b": nrm(ks[9], (L, W), 0.02),
        "conv_ln_g": 1.0 + nrm(ks[10], (L, W), 0.02),
        "conv_ln_b": nrm(ks[11], (L, W), 0.02),
        "pool_w": nrm(ks[12], (L, POOL_GROUPS, POOL_GROUP_W, POOL_GROUP_W), POOL_GROUP_W ** -0.5),
        "pool_scale": 1.0 + nrm(ks[13], (L, W), 0.02),
        "mem_norm_g": 1.0 + nrm(ks[14], (L, D), 0.02),
        "w_kv": nrm(ks[15], (L, D, 2 * W), D ** -0.5),
        "w_branch": nrm(ks[16], (L, N_BRANCH, W, D), W ** -0.5),
        "w_out": nrm(ks[17], (L, D, D), D ** -0.5),
        "final_g": 1.0 + nrm(ks[18], (D,), 0.02),
    }


def reference(x, mem, norm_g, w_in, gmlp_ln_g, gmlp_ln_b, gmlp_ws, gmlp_bs, conv_w, conv_b,
              conv_ln_g, conv_ln_b, pool_w, pool_scale, mem_norm_g, w_kv, w_branch, w_out, final_g):
    bn, t, d = x.shape
    split_points = [int(p) for p in np.cumsum(SPLIT_SIZES)[:-1]]
    for l in range(DEPTH):
        h = rmsnorm(x, norm_g[l])
        z = jnp.einsum('btd,dk->btk', h, w_in[l])
        (a_u, a_v, a_g, b_in, b_g, c_in, c_g, m_q, m_g, gates) = jnp.split(z, split_points, axis=-1)
        y_a = gmlp_spatial(a_u, a_v, gmlp_ln_g[l], gmlp_ln_b[l], gmlp_ws[l], gmlp_bs[l]) * jax.nn.silu(a_g)
        y_b = conformer_conv(b_in, conv_w[l], conv_b[l], conv_ln_g[l], conv_ln_b[l]) * jax.nn.silu(b_g)
        y_c = multiscale_pool(c_in, pool_w[l], pool_scale[l]) * jax.nn.silu(c_g)
        mem_n = rmsnorm(mem, mem_norm_g[l])
        y_m = memory_attention(m_q, mem_n, w_kv[l]) * jax.nn.silu(m_g)
        ys = jnp.stack([y_a, y_b, y_c, y_m], axis=2)
        proj = jnp.einsum('btnw,nwd->btnd', ys, w_branch[l])
        g = jax.nn.sigmoid(gates.reshape(bn, t, N_BRANCH, d))
        merged = jnp.sum(g * proj, axis=2)
        x = x + jnp.einsum('btd,de->bte', merged, w_out[l])
    return rmsnorm(x, final_g)
```

```python
import contextlib
import numpy as np
import concourse.bass as bass
import concourse.mybir as mybir
from concourse.bass_utils import run_bass_kernel_spmd

F32 = mybir.dt.float32
BF16 = mybir.dt.bfloat16
AF = mybir.ActivationFunctionType
ALU = mybir.AluOpType

EPOCH_ENG = 8000
EPOCH_DMA = 500

D_MODEL = 2048
W = 1024
IN_COLS = 18432
TT = 512
EPS = 1e-6
A_U, A_V, A_G, B_A, B_B, B_G, C_IN, C_G, M_Q, M_G, GATES = (
    0, 1024, 2048, 3072, 4096, 5120, 6144, 7168, 8192, 9216, 10240)
POOL_WINDOWS = (2, 4, 8, 16)
NCOLV = 312
CV_NG, CV_MG, CV_CB, CV_CLG, CV_CLB, CV_PS, CV_CW = 0, 16, 32, 40, 48, 56, 64


class _Op:
    __slots__ = ("id", "eng", "fn", "deps", "dma", "need_ev", "ev")

    def __init__(self, id, eng, fn, dma):
        self.id = id
        self.eng = eng
        self.fn = fn
        self.dma = dma
        self.deps = set()
        self.need_ev = False
        self.ev = None


class Sched:
    def __init__(self):
        self.ops = []
        self.lastw = {}
        self.readers = {}

    def add(self, eng, fn, reads=(), writes=(), dma=None):
        op = _Op(len(self.ops), eng, fn, dma)
        lastw, readers = self.lastw, self.readers
        deps = set()
        raw = set()
        for k in reads:
            w = lastw.get(k)
            if w is not None:
                deps.add(w)
                raw.add(w)
        for k in writes:
            w = lastw.get(k)
            if w is not None:
                deps.add(w)
            for r in readers.get(k, ()):
                deps.add(r)
        ops = self.ops
        for d in deps:
            p = ops[d]
            if p.dma is None and dma is None and p.eng == eng:
                if eng == "pe" or d not in raw:
                    continue
            op.deps.add(d)
        for k in reads:
            readers.setdefault(k, []).append(op.id)
        for k in writes:
            lastw[k] = op.id
            readers[k] = []
        ops.append(op)
        return op.id

    def emit(self, nc, stack):
        ops = self.ops
        for op in ops:
            for d in op.deps:
                ops[d].need_ev = True
        counters = {}
        semkeys = set()
        for op in ops:
            if not op.need_ev:
                continue
            if op.dma is not None:
                key, inc, ep = ("dma", op.dma), 16, EPOCH_DMA
            else:
                key, inc, ep = ("eng", op.eng), 1, EPOCH_ENG
            c = counters.get(key, 0)
            counters[key] = c + 1
            epoch, idx = divmod(c, ep)
            op.ev = (key, epoch, (idx + 1) * inc)
            semkeys.add((key, epoch))
        sems = {}
        for i, sk in enumerate(sorted(semkeys, key=str)):
            sems[sk] = stack.enter_context(nc.semaphore("s%d" % i))
        self.nsems = len(sems)
        by_eng = {}
        for op in ops:
            by_eng.setdefault(op.eng, []).append(op)

        def run(eng_name, eng):
            known = {}
            for op in by_eng.get(eng_name, ()):
                need = {}
                for d in op.deps:
                    key, epoch, val = ops[d].ev
                    kn = known.get(key)
                    if kn is not None and kn >= (epoch, val):
                        continue
                    cur = need.get(key)
                    if cur is None or cur < (epoch, val):
                        need[key] = (epoch, val)
                for key, (epoch, val) in need.items():
                    eng.wait_ge(sems[(key, epoch)], val)
                    known[key] = (epoch, val)
                inst = op.fn(eng)
                if op.need_ev:
                    key, epoch, val = op.ev
                    inst.then_inc(sems[(key, epoch)], 16 if op.dma is not None else 1)

        with nc.Block() as block:
            @block.sync
            def _(e):
                run("sp", e)

            @block.tensor
            def _(e):
                run("pe", e)

            @block.scalar
            def _(e):
                run("act", e)

            @block.vector
            def _(e):
                run("dve", e)

            @block.gpsimd
            def _(e):
                run("pool", e)


def build_program(NL, NT, final_norm=True):
    CW = 512
    NS = 6
    NTMP = 6
    R = CW // 128
    nc = bass.Bass("TRN2", target_bir_lowering=False)

    def din(name, shape, dt=F32):
        return nc.dram_tensor(name, shape, dt, kind="ExternalInput").ap()

    x_d = din("x", [NT * TT, D_MODEL])
    mem_d = din("mem", [256, D_MODEL])
    w_in_d = din("w_in", [NL, D_MODEL, IN_COLS])
    w_kv_d = din("w_kv", [NL, D_MODEL, 2 * W])
    w_br_d = din("w_branch", [NL, 4, W, D_MODEL])
    w_out_d = din("w_out", [NL, D_MODEL, D_MODEL])
    pool_w_d = din("pool_w", [NL, 4, 256, 256])
    colv_d = din("colv", [NL, 128, NCOLV])
    fing_d = din("fing", [128, 16])
    wsT_d = din("wsT", [NL, 128, 1024])
    rows3_d = din("rows3", [NL, 128, 3072])
    cst_d = din("cst", [128, 320])
    out_d = nc.dram_tensor("out", [NT * TT, D_MODEL], F32, kind="ExternalOutput").ap()
    kvs_d = nc.dram_tensor("kvs", [NL, 128, 4096], BF16).ap()
    dgs_d = nc.dram_tensor("dgs", [NL, 8, 128, 31 * 128], BF16).ap()

    S = Sched()
    off = [16512]

    def alloc(name, shape, dt, at=None):
        esz = 4 if dt == F32 else 2
        n = esz
        for s in shape[1:]:
            n *= s
        if at is None:
            at = off[0]
            off[0] += (n + 63) // 64 * 64
        return nc.alloc_sbuf_tensor_at(name, list(shape), dt, offset=at)

    xT = alloc("xT", [128, 16, TT], F32)
    hT_off = off[0]
    hT = alloc("hT", [128, 16, TT], BF16)
    memhT = alloc("memhT", [128, 16, 256], F32, at=hT_off)
    yT_off = off[0]
    yT = alloc("yT", [128, 32, TT], BF16)
    rows3 = alloc("rows3", [128, 3, 1024], F32, at=yT_off + 8 * 1024)
    wb = [alloc("wb%d" % i, [128, 8, CW], BF16) for i in range(NS)]
    kvb = alloc("kvb", [128, 4096], BF16)
    A1_off = off[0]
    gv = alloc("gv", [128, 4, 1024], F32)
    cv = alloc("cv", [128, 8, TT], F32, at=A1_off)
    mT = alloc("mT", [128, 16, TT], BF16, at=A1_off)
    xin = alloc("xin", [128, 2, D_MODEL], F32, at=A1_off)
    A2_off = off[0]
    vn = alloc("vn", [128, 4, 1024], BF16)
    diff = alloc("diff", [128, 8, TT], BF16, at=A2_off)
    qT = alloc("qT", [128, 8, TT], BF16, at=A2_off)
    memnT = alloc("memnT", [128, 16, 256], BF16, at=A2_off)
    TMP = [alloc("tmp%d" % i, [128, 544], F32) for i in range(NTMP)]
    DD = [alloc("dd%d" % i, [128, TT], F32) for i in range(4)]
    expT = [alloc("expT%d" % i, [128, 2, TT], BF16) for i in range(2)]
    convh = alloc("convh", [128, NL, 8, 30], BF16)
    hbb = [alloc("hbb%d" % i, [128, 544], BF16) for i in range(3)]
    identb = alloc("identb", [128, 128], BF16)
    poolh = alloc("poolh", [128, NL, 8, 15], F32)
    colv = alloc("colv", [128, NL, NCOLV], F32)
    fing = alloc("fing", [128, 16], F32)
    wsTb = alloc("wsTb", [128, 8, 128], BF16)
    poolw = alloc("poolw", [128, 4, 2, 256], BF16)
    cst = alloc("cst", [128, 320], F32)
    onesb = alloc("onesb", [128, 128], BF16)
    onesf = alloc("onesf", [128, 128], F32)
    stat = alloc("stat", [128, 64], F32)
    assert off[0] <= 16512 + 212800, off[0]
    print("sbuf used", off[0] - 16512, "of 212863")
    PS = [nc.alloc_psum_tensor("ps%d" % i, [128, TT], F32) for i in range(8)]

    ident = cst[:, 0:128]
    maskT = cst[:, 128:256]

    bk = [0]

    def bank():
        b = bk[0] % 8
        bk[0] += 1
        return b

    tm = [0]

    def tmp():
        i = tm[0] % NTMP
        tm[0] += 1
        return i

    slot_ctr = [0]

    def wload_half(src, eng="pool", extra_reads=(), slot=None):
        if slot is None:
            s = slot_ctr[0] % NS
            slot_ctr[0] += 1
        else:
            s = slot
        if len(src.shape) == 2:
            dst = wb[s][:].rearrange("p k c -> p (k c)")[:, 0:src.shape[1]]
        else:
            dst = wb[s][:]
        S.add(eng, lambda e: e.dma_start(out=dst, in_=src), reads=list(extra_reads), writes=[("wb", s)],
              dma="wb%d" % s)
        return s

    def wload(src, kdim=16, width=CW):
        if kdim == 8:
            return (wload_half(src),)
        return (wload_half(src[:, 0:8, :]), wload_half(src[:, 8:16, :]))

    def wsl(s, k, lo, hi):
        return wb[s[k // 8]][:, k % 8, lo:hi]

    def wkeys(s):
        return [("wb", i) for i in s]

    def mm(b_ap, pairs, reads, writes):
        def fn(e):
            n = len(pairs)
            r = None
            for i, (a, b) in enumerate(pairs):
                r = e.matmul(b_ap, a, b, start=(i == 0), stop=(i == n - 1))
            return r
        S.add("pe", fn, reads=reads, writes=writes)

    def act(out, in_, func, reads, writes, **kw):
        S.add("act", lambda e: e.activation(out=out, in_=in_, func=func, **kw),
              reads=list(reads) + ["cst"], writes=writes)

    def dve(fn, reads, writes):
        S.add("dve", fn, reads=reads, writes=writes)

    def A1k(lo, hi):
        return [("A1", i) for i in range(lo, hi)]

    def A2k(lo, hi):
        return [("A2", i) for i in range(lo, hi)]

    HT_ALL = [("hT", k) for k in range(16)]
    XT_ALL = [("xT", k) for k in range(16)]

    def w_in_v(l):
        return w_in_d[l].rearrange("(k p) c -> p k c", p=128)

    S.add("sp", lambda e: e.dma_start(out=cst[:], in_=cst_d), writes=["cst"], dma="c0")
    S.add("sp", lambda e: e.dma_start(out=colv[:], in_=colv_d.rearrange("l p c -> p l c")),
          writes=["colv"], dma="c1")
    S.add("sp", lambda e: e.dma_start(out=fing[:], in_=fing_d), writes=["fing"], dma="c2")
    dve(lambda e: e.memset(onesf[:], 1.0), [], ["onesf"])
    dve(lambda e: e.memset(onesb[:], 1.0), [], ["onesb"])
    dve(lambda e: e.memset(convh[:], 0.0), [], ["convh"])
    dve(lambda e: e.memset(poolh[:], 0.0), [], ["poolh"])
    dve(lambda e: e.memset(stat[:], 0.0), [], [("stat", i) for i in range(4)])

    for mb in range(2):
        S.add("sp", lambda e, mb=mb: e.dma_start(out=xin[:, mb, :], in_=mem_d[mb * 128:(mb + 1) * 128, :]),
              writes=A1k(4 * mb, 4 * mb + 4), dma="xin%d" % mb)
        ti = tmp()
        for hh in range(4):
            act(TMP[ti][:, 0:512], xin[:, mb, hh * 512:(hh + 1) * 512], AF.Square,
                A1k(4 * mb, 4 * mb + 4), [("tmp", ti), ("stat", hh)],
                accum_out=stat[:, hh * 16 + mb:hh * 16 + mb + 1])
        c0 = 8 + mb
        dve(lambda e, mb=mb, c0=c0: e.tensor_tensor(out=stat[:, c0:c0 + 1], in0=stat[:, mb:mb + 1],
                                                    in1=stat[:, 16 + mb:17 + mb], op=ALU.add),
            [("stat", 0), ("stat", 1)], [("stat", "m")])
        dve(lambda e, mb=mb, c0=c0: e.tensor_tensor(out=stat[:, c0 + 2:c0 + 3], in0=stat[:, 32 + mb:33 + mb],
                                                    in1=stat[:, 48 + mb:49 + mb], op=ALU.add),
            [("stat", 2), ("stat", 3)], [("stat", "m2")])
        dve(lambda e, c0=c0: e.tensor_tensor(out=stat[:, c0:c0 + 1], in0=stat[:, c0:c0 + 1],
                                             in1=stat[:, c0 + 2:c0 + 3], op=ALU.add),
            [("stat", "m"), ("stat", "m2")], [("stat", "m3")])
        act(stat[:, c0 + 4:c0 + 5], stat[:, c0:c0 + 1], AF.Sqrt, [("stat", "m3")], [("stat", "m4")],
            scale=1.0 / D_MODEL, bias=cst[:, 319:320])
        dve(lambda e, c0=c0: e.reciprocal(out=stat[:, c0 + 6:c0 + 7], in_=stat[:, c0 + 4:c0 + 5]),
            [("stat", "m4")], [("stat", "m5")])
        dve(lambda e, mb=mb, c0=c0: e.tensor_scalar(out=xin[:, mb, :], in0=xin[:, mb, :],
                                                    scalar1=stat[:, c0 + 6:c0 + 7], scalar2=None, op0=ALU.mult),
            [("stat", "m5")] + A1k(4 * mb, 4 * mb + 4), A1k(4 * mb, 4 * mb + 4))
        for k4 in range(4):
            b = bank()

            def tr(e, mb=mb, k4=k4, b=b):
                r = None
                for j in range(4):
                    k = k4 * 4 + j
                    r = e.transpose(out=PS[b][:, j * 128:(j + 1) * 128], in_=xin[:, mb, k * 128:(k + 1) * 128],
                                    identity=ident)
                return r
            S.add("pe", tr, reads=A1k(4 * mb, 4 * mb + 4) + ["cst"], writes=[("ps", b)])
            act(memhT[:, k4 * 4:k4 * 4 + 4, mb * 128:(mb + 1) * 128],
                PS[b][:].rearrange("p (j t) -> p j t", j=4), AF.Copy, [("ps", b)], HT_ALL)
    for l in range(NL):
        for k in range(16):
            dve(lambda e, l=l, k=k: e.tensor_scalar(out=memnT[:, k, :], in0=memhT[:, k, :],
                                                    scalar1=colv[:, l, CV_MG + k:CV_MG + k + 1], scalar2=None,
                                                    op0=ALU.mult),
                HT_ALL + ["colv"], A2k(0, 4))
        wkv = w_kv_d[l].rearrange("(k p) c -> p k c", p=128)
        for half in range(2):
            s = wload(wkv[:, :, half * CW:(half + 1) * CW])
            b = bank()
            for r in range(R):
                fb = half * R + r
                pairs = [(wsl(s, k, r * 128, (r + 1) * 128), memnT[:, k, :]) for k in range(16)]
                if r % 2 == 0:
                    b = bank()
                mm(PS[b][:, (r % 2) * 256:(r % 2) * 256 + 256], pairs, wkeys(s) + A2k(0, 4), [("ps", b)])
                if r % 2 == 1:
                    act(kvb[:, (fb - 1) * 256:(fb + 1) * 256], PS[b][:], AF.Copy, [("ps", b)], ["kvb"])
        for half in range(2):
            s = wload(wkv[:, :, W + half * CW:W + (half + 1) * CW])
            for mb in range(2):
                b = bank()
                pairs = [(memnT[:, k, mb * 128:(mb + 1) * 128], wsl(s, k, 0, CW)) for k in range(16)]
                mm(PS[b][:], pairs, wkeys(s) + A2k(0, 4), [("ps", b)])
                o0 = 2048 + mb * 1024 + half * 512
                act(kvb[:, o0:o0 + 512], PS[b][:], AF.Copy, [("ps", b)], ["kvb"])
        S.add("sp", lambda e, l=l: e.dma_start(out=kvs_d[l], in_=kvb[:]), reads=["kvb"], writes=[("kvs", l)],
              dma="kvs")

    dve(lambda e: e.tensor_copy(out=identb[:], in_=ident), ["cst"], ["identb"])
    for l in range(NL):
        for c in range(8):
            sd = slot_ctr[0] % NS
            slot_ctr[0] += 1
            dgv = wb[sd][:].rearrange("p k c -> p (k c)")
            for kk in range(31):
                dve(lambda e, dgv=dgv, kk=kk, l=l, c=c: e.tensor_scalar(
                    out=dgv[:, kk * 128:(kk + 1) * 128], in0=identb[:],
                    scalar1=colv[:, l, CV_CW + c * 31 + kk:CV_CW + c * 31 + kk + 1], scalar2=None,
                    op0=ALU.mult), ["identb", "colv"], [("wb", sd)])
            S.add("sp", lambda e, dgv=dgv, l=l, c=c: e.dma_start(out=dgs_d[l, c], in_=dgv[:, 0:31 * 128]),
                  reads=[("wb", sd)], writes=[("dgs", l, c)], dma="dgs%d" % sd)

    KT = kvb[:, 0:2048].rearrange("p (c m) -> p c m", c=8)
    VV = kvb[:, 2048:4096].rearrange("p (b f) -> p b f", b=2)

    def rms_stats(dst):
        b = bank()
        for k in range(16):
            ti = tmp()
            act(TMP[ti][:, 0:512], xT[:, k, :], AF.Square, [("xT", k)], [("tmp", ti)])
            S.add("pe", lambda e, k=k, ti=ti, b=b: e.matmul(PS[b][:], onesf[:], TMP[ti][:, 0:512],
                                                            start=(k == 0), stop=(k == 15)),
                  reads=[("tmp", ti), "onesf"], writes=[("ps", b)])
        ti = tmp()
        act(TMP[ti][:, 0:512], PS[b][:], AF.Sqrt, [("ps", b)], [("tmp", ti)],
            scale=1.0 / D_MODEL, bias=cst[:, 319:320])
        dve(lambda e, ti=ti: e.reciprocal(out=dst, in_=TMP[ti][:, 0:512]), [("tmp", ti)], [("D", 0)])

    for t in range(NT):
        for tb in range(4):
            xi = tb % 2
            r0 = t * TT + tb * 128
            S.add("sp", lambda e, xi=xi, r0=r0: e.dma_start(out=xin[:, xi, :], in_=x_d[r0:r0 + 128, :]),
                  writes=A1k(4 * xi, 4 * xi + 4), dma="xin%d" % xi)
            for k4 in range(4):
                b = bank()

                def tr(e, xi=xi, k4=k4, b=b):
                    r = None
                    for j in range(4):
                        k = k4 * 4 + j
                        r = e.transpose(out=PS[b][:, j * 128:(j + 1) * 128],
                                        in_=xin[:, xi, k * 128:(k + 1) * 128], identity=ident)
                    return r
                S.add("pe", tr, reads=A1k(4 * xi, 4 * xi + 4) + ["cst"], writes=[("ps", b)])
                dve(lambda e, k4=k4, tb=tb, b=b: e.tensor_copy(
                    out=xT[:, k4 * 4:k4 * 4 + 4, tb * 128:(tb + 1) * 128],
                    in_=PS[b][:].rearrange("p (j t) -> p j t", j=4)),
                    [("ps", b)], [("xT", k4 * 4 + j) for j in range(4)])

        for l in range(NL):
            wv = w_in_v(l)

            def cv_col(c0, l=l):
                return colv[:, l, c0:c0 + 1]

            rms_stats(DD[0][:])
            for k in range(16):
                dve(lambda e, k=k, l=l: e.scalar_tensor_tensor(
                    out=hT[:, k, :], in0=xT[:, k, :], scalar=colv[:, l, CV_NG + k:CV_NG + k + 1],
                    in1=DD[0][:], op0=ALU.mult, op1=ALU.mult),
                    [("xT", k), ("D", 0), "colv"], [("hT", k)])

            def proj_block(s, r, b):
                pairs = [(wsl(s, k, r * 128, (r + 1) * 128), hT[:, k, :]) for k in range(16)]
                mm(PS[b][:], pairs, wkeys(s) + HT_ALL, [("ps", b)])

            S.add("sp", lambda e, l=l: e.dma_start(out=rows3[:], in_=rows3_d[l].rearrange("p (a b) -> p a b", a=3)),
                  writes=[("yT", c) for c in range(8, 20)], dma="rows3")
            S.add("sp", lambda e, l=l: e.dma_start(out=kvb[:], in_=kvs_d[l]), reads=[("kvs", l)],
                  writes=["kvb"], dma="kvl")

            sq = [wload(wv[:, :, M_Q + i * CW:M_Q + (i + 1) * CW]) for i in range(2)]
            for c in range(8):
                b = bank()
                proj_block(sq[c // R], c % R, b)
                act(qT[:, c, :], PS[b][:], AF.Copy, [("ps", b)], A2k(c // 2, c // 2 + 1))
            sg = [None, None]
            for h in range(4):
                eb = h % 2
                for mb in range(2):
                    b = bank()
                    pairs = [(KT[:, 2 * h + dc, mb * 128:(mb + 1) * 128], qT[:, 2 * h + dc, :]) for dc in range(2)]
                    mm(PS[b][:], pairs, ["kvb"] + A2k(h, h + 1), [("ps", b)])
                    act(expT[eb][:, mb, :], PS[b][:], AF.Exp, [("ps", b)], [("expT", eb)], scale=0.0625)
                b = bank()
                mm(PS[b][:], [(onesb[:], expT[eb][:, mb, :]) for mb in range(2)], [("expT", eb), "onesb"],
                   [("ps", b)])
                di = 1 + eb
                dve(lambda e, b=b, di=di: e.reciprocal(out=DD[di][:], in_=PS[b][:]), [("ps", b)], [("D", di)])
                for dblk in range(2):
                    c = 2 * h + dblk
                    if c % R == 0:
                        sg[0] = wload(wv[:, :, M_G + (c // R) * CW:M_G + (c // R + 1) * CW])
                    bo = bank()
                    pairs = [(VV[:, mb, c * 128:(c + 1) * 128], expT[eb][:, mb, :]) for mb in range(2)]
                    mm(PS[bo][:], pairs, ["kvb", ("expT", eb)], [("ps", bo)])
                    bg = bank()
                    proj_block(sg[0], c % R, bg)
                    t2 = tmp()
                    act(TMP[t2][:, 0:512], PS[bg][:], AF.Silu, [("ps", bg)], [("tmp", t2)])
                    dve(lambda e, t2=t2, di=di: e.tensor_tensor(out=TMP[t2][:, 0:512], in0=TMP[t2][:, 0:512],
                                                                in1=DD[di][:], op=ALU.mult),
                        [("tmp", t2), ("D", di)], [("tmp", t2)])
                    dve(lambda e, t2=t2, bo=bo, c=c: e.tensor_tensor(out=yT[:, 24 + c, :], in0=PS[bo][:],
                                                                     in1=TMP[t2][:, 0:512], op=ALU.mult),
                        [("tmp", t2), ("ps", bo)], [("yT", 24 + c)])

            for hh in range(2):
                ti = tmp()
                S.add("sp", lambda e, l=l, hh=hh, ti=ti: e.dma_start(out=TMP[ti][:, 0:512],
                                                                   in_=wsT_d[l, :, hh * 512:(hh + 1) * 512]),
                      writes=[("tmp", ti)], dma="wsT%d" % hh)
                for g4 in range(4):
                    g = hh * 4 + g4
                    dve(lambda e, ti=ti, g=g, g4=g4: e.tensor_tensor(out=wsTb[:, g, :],
                                                                      in0=TMP[ti][:, g4 * 128:(g4 + 1) * 128],
                                                                      in1=maskT, op=ALU.mult),
                        [("tmp", ti), "cst"], [("wsTb", g)])
            sv = [wload(wv[:, :, A_V + i * CW:A_V + (i + 1) * CW]) for i in range(2)]
            for half in range(2):
                for tb in range(4):
                    b = bank()
                    pairs = [(hT[:, k, tb * 128:(tb + 1) * 128], wsl(sv[half], k, 0, CW)) for k in range(16)]
                    mm(PS[b][:], pairs, wkeys(sv[half]) + HT_ALL, [("ps", b)])
                    act(gv[:, tb, half * 512:(half + 1) * 512], PS[b][:], AF.Gelu_apprx_tanh,
                        [("ps", b)], [("A1", tb * 2 + half)])
            RY = [("yT", c) for c in range(8, 20)]
            for tb in range(4):
                sc = tb * 16
                gk = A1k(tb * 2, tb * 2 + 2)
                dve(lambda e, tb=tb, sc=sc: e.bn_stats(out=stat[:, sc:sc + 6], in_=gv[:, tb, 0:512]),
                    gk, [("stat", tb)])
                dve(lambda e, tb=tb, sc=sc: e.bn_stats(out=stat[:, sc + 6:sc + 12], in_=gv[:, tb, 512:1024]),
                    gk, [("stat", tb, 1)])
                dve(lambda e, sc=sc: e.bn_aggr(out=stat[:, sc + 12:sc + 14], in_=stat[:, sc:sc + 12]),
                    [("stat", tb), ("stat", tb, 1)], [("stat", tb, 2)])
                act(stat[:, sc + 14:sc + 15], stat[:, sc + 13:sc + 14], AF.Sqrt, [("stat", tb, 2)],
                    [("stat", tb, 3)], scale=1.0, bias=cst[:, 319:320])
                dve(lambda e, sc=sc: e.reciprocal(out=stat[:, sc + 15:sc + 16], in_=stat[:, sc + 14:sc + 15]),
                    [("stat", tb, 3)], [("stat", tb, 4)])
                dve(lambda e, tb=tb, sc=sc: e.tensor_scalar(out=gv[:, tb, :], in0=gv[:, tb, :],
                                                            scalar1=stat[:, sc + 12:sc + 13],
                                                            scalar2=stat[:, sc + 15:sc + 16],
                                                            op0=ALU.subtract, op1=ALU.mult),
                    gk + [("stat", tb, 2), ("stat", tb, 4)], gk)
                dve(lambda e, tb=tb: e.tensor_tensor(out=gv[:, tb, :], in0=gv[:, tb, :], in1=rows3[:, 0, :],
                                                     op=ALU.mult), gk + RY, gk)
                dve(lambda e, tb=tb: e.tensor_tensor(out=vn[:, tb, :], in0=gv[:, tb, :], in1=rows3[:, 1, :],
                                                     op=ALU.add), gk + RY, A2k(tb, tb + 1))
            su = [None]
            sgA = [None]
            for c in range(8):
                if c % R == 0:
                    su[0] = wload(wv[:, :, A_U + (c // R) * CW:A_U + (c // R + 1) * CW])
                    sgA[0] = wload(wv[:, :, A_G + (c // R) * CW:A_G + (c // R + 1) * CW])
                bu = bank()
                proj_block(su[0], c % R, bu)
                t1 = tmp()
                act(TMP[t1][:, 0:512], PS[bu][:], AF.Gelu_apprx_tanh, [("ps", bu)], [("tmp", t1)])
                bg = bank()
                proj_block(sgA[0], c % R, bg)
                t2 = tmp()
                act(TMP[t2][:, 0:512], PS[bg][:], AF.Silu, [("ps", bg)], [("tmp", t2)])
                dve(lambda e, t1=t1, t2=t2: e.tensor_tensor(out=TMP[t1][:, 0:512], in0=TMP[t1][:, 0:512],
                                                            in1=TMP[t2][:, 0:512], op=ALU.mult),
                    [("tmp", t1), ("tmp", t2)], [("tmp", t1)])
                bs_ = bank()

                def sp_mm(e, c=c, bs_=bs_):
                    r = None
                    for tb in range(4):
                        r = e.matmul(PS[bs_][:, tb * 128:(tb + 1) * 128], vn[:, tb, c * 128:(c + 1) * 128],
                                     wsTb[:, c, :], start=True, stop=True)
                    return r
                S.add("pe", sp_mm, reads=A2k(0, 4) + [("wsTb", c)], writes=[("ps", bs_)])
                t3 = tmp()
                for tb in range(4):
                    dve(lambda e, t3=t3, bs_=bs_, c=c, tb=tb: e.tensor_tensor(
                        out=TMP[t3][:, tb * 128:(tb + 1) * 128], in0=PS[bs_][:, tb * 128:(tb + 1) * 128],
                        in1=rows3[:, 2, c * 128:(c + 1) * 128], op=ALU.add),
                        [("ps", bs_)] + RY, [("tmp", t3)])
                dve(lambda e, t1=t1, t3=t3, c=c: e.tensor_tensor(out=yT[:, c, :], in0=TMP[t3][:, 0:512],
                                                                 in1=TMP[t1][:, 0:512], op=ALU.mult),
                    [("tmp", t1), ("tmp", t3)], [("yT", c)])

            sa = [None]
            sb = [None]
            for c in range(8):
                if c % R == 0:
                    sa[0] = wload(wv[:, :, B_A + (c // R) * CW:B_A + (c // R + 1) * CW])
                    sb[0] = wload(wv[:, :, B_B + (c // R) * CW:B_B + (c // R + 1) * CW])
                    dsl = (slot_ctr[0] % NS, (slot_ctr[0] + 1) % NS)
                    slot_ctr[0] += 2
                sd = wload_half(dgs_d[l, c], eng="sp", extra_reads=[("dgs", l, c)], slot=dsl[c % 2])
                ba = bank()
                proj_block(sa[0], c % R, ba)
                bb = bank()
                proj_block(sb[0], c % R, bb)
                t1 = tmp()
                act(TMP[t1][:, 0:512], PS[bb][:], AF.Sigmoid, [("ps", bb)], [("tmp", t1)])
                hi = c % 3
                act(hbb[hi][:, 0:30], convh[:, l, c, :], AF.Copy, ["convh"], [("hbb", hi)])
                dve(lambda e, hi=hi, ba=ba, t1=t1: e.tensor_tensor(out=hbb[hi][:, 30:542], in0=PS[ba][:],
                                                                   in1=TMP[t1][:, 0:512], op=ALU.mult),
                    [("ps", ba), ("tmp", t1)], [("hbb", hi)])
                act(convh[:, l, c, :], hbb[hi][:, 512:542], AF.Copy, [("hbb", hi)], ["convh"])
                bc = bank()
                dgv = wb[sd][:].rearrange("p k c -> p (k c)")
                pairs = [(dgv[:, kk * 128:(kk + 1) * 128], hbb[hi][:, kk:kk + 512]) for kk in range(31)]
                mm(PS[bc][:], pairs, [("wb", sd), ("hbb", hi)], [("ps", bc)])
                dve(lambda e, c=c, bc=bc, l=l: e.tensor_scalar(out=cv[:, c, :], in0=PS[bc][:],
                                                               scalar1=colv[:, l, CV_CB + c:CV_CB + c + 1],
                                                               scalar2=None, op0=ALU.add),
                    [("ps", bc), "colv"], [("A1", c)])
            b1 = bank()
            for c in range(8):
                S.add("pe", lambda e, c=c, b1=b1: e.matmul(PS[b1][:], onesf[:], cv[:, c, :],
                                                           start=(c == 0), stop=(c == 7)),
                      reads=[("A1", c), "onesf"], writes=[("ps", b1)])
            b2 = bank()
            for c in range(8):
                ti = tmp()
                act(TMP[ti][:, 0:512], cv[:, c, :], AF.Square, [("A1", c)], [("tmp", ti)])
                S.add("pe", lambda e, c=c, b2=b2, ti=ti: e.matmul(PS[b2][:], onesf[:], TMP[ti][:, 0:512],
                                                                  start=(c == 0), stop=(c == 7)),
                      reads=[("tmp", ti), "onesf"], writes=[("ps", b2)])
            dve(lambda e, b1=b1: e.tensor_scalar(out=DD[0][:], in0=PS[b1][:], scalar1=1.0 / W, scalar2=None,
                                                 op0=ALU.mult), [("ps", b1)], [("D", 0)])
            tq = tmp()
            dve(lambda e, tq=tq: e.tensor_tensor(out=TMP[tq][:, 0:512], in0=DD[0][:], in1=DD[0][:], op=ALU.mult),
                [("D", 0)], [("tmp", tq)])
            dve(lambda e, tq=tq, b2=b2: e.scalar_tensor_tensor(out=TMP[tq][:, 0:512], in0=PS[b2][:],
                                                               scalar=1.0 / W, in1=TMP[tq][:, 0:512],
                                                               op0=ALU.mult, op1=ALU.subtract),
                [("ps", b2), ("tmp", tq)], [("tmp", tq)])
            act(TMP[tq][:, 0:512], TMP[tq][:, 0:512], AF.Sqrt, [("tmp", tq)], [("tmp", tq)],
                scale=1.0, bias=cst[:, 319:320])
            dve(lambda e, tq=tq: e.reciprocal(out=DD[1][:], in_=TMP[tq][:, 0:512]), [("tmp", tq)], [("D", 1)])
            sgB = [None]
            for c in range(8):
                if c % R == 0:
                    sgB[0] = wload(wv[:, :, B_G + (c // R) * CW:B_G + (c // R + 1) * CW])
                t1 = tmp()
                dve(lambda e, t1=t1, c=c: e.tensor_tensor(out=TMP[t1][:, 0:512], in0=cv[:, c, :], in1=DD[0][:],
                                                          op=ALU.subtract), [("A1", c), ("D", 0)], [("tmp", t1)])
                dve(lambda e, t1=t1: e.tensor_tensor(out=TMP[t1][:, 0:512], in0=TMP[t1][:, 0:512], in1=DD[1][:],
                                                     op=ALU.mult), [("tmp", t1), ("D", 1)], [("tmp", t1)])
                act(TMP[t1][:, 0:512], TMP[t1][:, 0:512], AF.Silu, [("tmp", t1), "colv"], [("tmp", t1)],
                    scale=cv_col(CV_CLG + c), bias=cv_col(CV_CLB + c))
                bg = bank()
                proj_block(sgB[0], c % R, bg)
                t2 = tmp()
                act(TMP[t2][:, 0:512], PS[bg][:], AF.Silu, [("ps", bg)], [("tmp", t2)])
                dve(lambda e, t1=t1, t2=t2, c=c: e.tensor_tensor(out=yT[:, 8 + c, :], in0=TMP[t1][:, 0:512],
                                                                 in1=TMP[t2][:, 0:512], op=ALU.mult),
                    [("tmp", t1), ("tmp", t2)], [("yT", 8 + c)])

            S.add("pool", lambda e, l=l: e.dma_start(
                out=poolw[:], in_=pool_w_d[l].rearrange("g (k p) d -> p g k d", p=128)),
                writes=["poolw"], dma="poolw")
            sc_ = [None]
            for c in range(8):
                if c % R == 0:
                    sc_[0] = wload(wv[:, :, C_IN + (c // R) * CW:C_IN + (c // R + 1) * CW])
                b = bank()
                proj_block(sc_[0], c % R, b)
                hb = tmp()
                act(TMP[hb][:, 0:15], poolh[:, l, c, :], AF.Copy, ["poolh"], [("tmp", hb)])
                act(TMP[hb][:, 15:527], PS[b][:], AF.Copy, [("ps", b)], [("tmp", hb)])
                act(poolh[:, l, c, :], TMP[hb][:, 512:527], AF.Copy, [("tmp", hb)], ["poolh"])
                wi = c // 2
                win = POOL_WINDOWS[wi]
                src = hb
                lo = 0
                sh = 1
                while sh < win:
                    dst = tmp()
                    lo2 = lo + sh
                    dve(lambda e, src=src, dst=dst, lo2=lo2, sh=sh: e.tensor_tensor(
                        out=TMP[dst][:, lo2:527], in0=TMP[src][:, lo2:527], in1=TMP[src][:, lo2 - sh:527 - sh],
                        op=ALU.add), [("tmp", src)], [("tmp", dst)])
                    src = dst
                    lo = lo2
                    sh *= 2
                dve(lambda e, src=src, hb=hb, c=c, win=win: e.scalar_tensor_tensor(
                    out=diff[:, c, :], in0=TMP[src][:, 15:527], scalar=1.0 / win, in1=TMP[hb][:, 15:527],
                    op0=ALU.mult, op1=ALU.subtract), [("tmp", src), ("tmp", hb)], A2k(c // 2, c // 2 + 1))
                if t == 0:
                    t5 = tmp()
                    dve(lambda e, src=src, t5=t5, wi=wi: e.tensor_tensor(
                        out=TMP[t5][:, 0:15], in0=TMP[src][:, 15:30], in1=cst[:, 256 + wi * 16:256 + wi * 16 + 15],
                        op=ALU.mult), [("tmp", src), "cst"], [("tmp", t5)])
                    dve(lambda e, t5=t5, hb=hb, c=c: e.tensor_tensor(
                        out=diff[:, c, 0:15], in0=TMP[t5][:, 0:15], in1=TMP[hb][:, 15:30], op=ALU.subtract),
                        [("tmp", t5), ("tmp", hb)] + A2k(c // 2, c // 2 + 1), A2k(c // 2, c // 2 + 1))
            sgC = [None]
            for c in range(8):
                if c % R == 0:
                    sgC[0] = wload(wv[:, :, C_G + (c // R) * CW:C_G + (c // R + 1) * CW])
                g, db = c // 2, c % 2
                bp = bank()
                pairs = [(poolw[:, g, k, db * 128:(db + 1) * 128], diff[:, 2 * g + k, :]) for k in range(2)]
                mm(PS[bp][:], pairs, ["poolw"] + A2k(g, g + 1), [("ps", bp)])
                bg = bank()
                proj_block(sgC[0], c % R, bg)
                t2 = tmp()
                act(TMP[t2][:, 0:512], PS[bg][:], AF.Silu, [("ps", bg)], [("tmp", t2)])
                dve(lambda e, t2=t2, bp=bp, c=c, l=l: e.scalar_tensor_tensor(
                    out=yT[:, 16 + c, :], in0=PS[bp][:], scalar=colv[:, l, CV_PS + c:CV_PS + c + 1],
                    in1=TMP[t2][:, 0:512], op0=ALU.mult, op1=ALU.mult),
                    [("ps", bp), ("tmp", t2), "colv"], [("yT", 16 + c)])

            for dg in range(D_MODEL // CW):
                for n in range(4):
                    sgt = wload(wv[:, :, GATES + n * D_MODEL + dg * CW:GATES + n * D_MODEL + (dg + 1) * CW])
                    sbr = wload(w_br_d[l, n].rearrange("(k p) c -> p k c", p=128)[:, :, dg * CW:(dg + 1) * CW],
                                kdim=8)
                    for r in range(R):
                        bp = bank()
                        pairs = [(wsl(sbr, k, r * 128, (r + 1) * 128), yT[:, n * 8 + k, :]) for k in range(8)]
                        mm(PS[bp][:], pairs, wkeys(sbr) + [("yT", n * 8 + k) for k in range(8)], [("ps", bp)])
                        bg = bank()
                        proj_block(sgt, r, bg)
                        t2 = tmp()
                        act(TMP[t2][:, 0:512], PS[bg][:], AF.Sigmoid, [("ps", bg)], [("tmp", t2)])
                        mi = dg * R + r
                        if n == 0:
                            dve(lambda e, r=r, bp=bp, t2=t2: e.tensor_tensor(out=DD[r][:], in0=PS[bp][:],
                                                                             in1=TMP[t2][:, 0:512], op=ALU.mult),
                                [("ps", bp), ("tmp", t2)], [("D", r)])
                        else:
                            dve(lambda e, bp=bp, t2=t2: e.tensor_tensor(out=TMP[t2][:, 0:512], in0=PS[bp][:],
                                                                        in1=TMP[t2][:, 0:512], op=ALU.mult),
                                [("ps", bp), ("tmp", t2)], [("tmp", t2)])
                            if n < 3:
                                dve(lambda e, r=r, t2=t2: e.tensor_tensor(out=DD[r][:], in0=DD[r][:],
                                                                          in1=TMP[t2][:, 0:512], op=ALU.add),
                                    [("D", r), ("tmp", t2)], [("D", r)])
                            else:
                                dve(lambda e, r=r, t2=t2, mi=mi: e.tensor_tensor(out=mT[:, mi, :], in0=DD[r][:],
                                                                                 in1=TMP[t2][:, 0:512],
                                                                                 op=ALU.add),
                                    [("D", r), ("tmp", t2)], [("A1", mi // 2)])

            wo = w_out_d[l].rearrange("(k p) c -> p k c", p=128)
            so = [None]
            for eb_ in range(16):
                if eb_ % R == 0:
                    so[0] = wload(wo[:, :, (eb_ // R) * CW:(eb_ // R + 1) * CW])
                b = bank()
                pairs = [(wsl(so[0], k, (eb_ % R) * 128, (eb_ % R + 1) * 128), mT[:, k, :]) for k in range(16)]
                mm(PS[b][:], pairs, wkeys(so[0]) + A1k(0, 8), [("ps", b)])
                dve(lambda e, eb_=eb_, b=b: e.tensor_tensor(out=xT[:, eb_, :], in0=PS[b][:], in1=xT[:, eb_, :],
                                                            op=ALU.add), [("ps", b), ("xT", eb_)], [("xT", eb_)])

        if final_norm:
            rms_stats(DD[0][:])
            for k in range(16):
                dve(lambda e, k=k: e.scalar_tensor_tensor(out=xT[:, k, :], in0=xT[:, k, :],
                                                          scalar=fing[:, k:k + 1], in1=DD[0][:],
                                                          op0=ALU.mult, op1=ALU.mult),
                    [("xT", k), ("D", 0), "fing"], [("xT", k)])
        for tb in range(4):
            xi = tb % 2
            for k4 in range(4):
                b = bank()

                def tr2(e, tb=tb, k4=k4, b=b):
                    r = None
                    for j in range(4):
                        k = k4 * 4 + j
                        r = e.transpose(out=PS[b][:, j * 128:(j + 1) * 128],
                                        in_=xT[:, k, tb * 128:(tb + 1) * 128], identity=ident)
                    return r
                S.add("pe", tr2, reads=[("xT", k4 * 4 + j) for j in range(4)] + ["cst"], writes=[("ps", b)])
                dve(lambda e, xi=xi, k4=k4, b=b: e.tensor_copy(out=xin[:, xi, k4 * 512:(k4 + 1) * 512],
                                                               in_=PS[b][:]),
                    [("ps", b)], A1k(4 * xi + k4, 4 * xi + k4 + 1))
            r0 = t * TT + tb * 128
            S.add("sp", lambda e, xi=xi, r0=r0: e.dma_start(out=out_d[r0:r0 + 128, :], in_=xin[:, xi, :]),
                  reads=A1k(4 * xi, 4 * xi + 4), writes=[("out", t, tb)], dma="out%d" % xi)
    S.add("sp", lambda e: e.nop(), reads=[("out", t, tb) for t in range(NT) for tb in range(4)])
    return nc, S


def host_layout(NL, norm_g, gmlp_ln_g, gmlp_ln_b, gmlp_ws, gmlp_bs, conv_w, conv_b, conv_ln_g, conv_ln_b,
                pool_scale, mem_norm_g, final_g):
    f = np.float32

    def cols(v, n):
        return np.asarray(v, f).reshape(n, 128).T

    colv = np.zeros((NL, 128, NCOLV), f)
    for l in range(NL):
        colv[l, :, CV_NG:CV_NG + 16] = cols(norm_g[l], 16)
        colv[l, :, CV_MG:CV_MG + 16] = cols(mem_norm_g[l], 16)
        colv[l, :, CV_CB:CV_CB + 8] = cols(conv_b[l], 8)
        colv[l, :, CV_CLG:CV_CLG + 8] = cols(conv_ln_g[l], 8)
        colv[l, :, CV_CLB:CV_CLB + 8] = cols(conv_ln_b[l], 8)
        colv[l, :, CV_PS:CV_PS + 8] = cols(pool_scale[l], 8)
        cw = np.asarray(conv_w[l], f)
        colv[l, :, CV_CW:CV_CW + 248] = cw.reshape(31, 8, 128).transpose(2, 1, 0).reshape(128, 248)
    fing = cols(final_g, 16).copy()
    wsT = np.ascontiguousarray(np.asarray(gmlp_ws[:NL], f).transpose(0, 3, 1, 2).reshape(NL, 128, 1024))
    rows3 = np.zeros((NL, 128, 3072), f)
    for l in range(NL):
        rows3[l, :, 0:1024] = np.asarray(gmlp_ln_g[l], f)[None, :]
        rows3[l, :, 1024:2048] = np.asarray(gmlp_ln_b[l], f)[None, :]
        rows3[l, :, 2048:3072] = np.asarray(gmlp_bs[l], f).reshape(1, 1024)
    cst = np.zeros((128, 320), f)
    cst[:, 0:128] = np.eye(128, dtype=f)
    s_idx = np.arange(128)[:, None]
    t_idx = np.arange(128)[None, :]
    cst[:, 128:256] = (s_idx <= t_idx).astype(f)
    for wi, win in enumerate(POOL_WINDOWS):
        cst[:, 256 + wi * 16:256 + wi * 16 + 15] = (1.0 / np.minimum(np.arange(1, 16), win)).astype(f)[None, :]
    cst[:, 319] = EPS
    return dict(colv=colv, fing=fing, wsT=wsT, rows3=rows3, cst=cst)


_CACHE = {}


def run_model(x, mem, params, NL, ncores, trace=False):
    T = x.shape[1]
    NT = T // TT
    key = (NL, NT)
    if key not in _CACHE:
        with contextlib.ExitStack() as st:
            nc, S = build_program(NL, NT)
            S.emit(nc, st)
        _CACHE[key] = nc
    nc = _CACHE[key]
    hl = host_layout(NL, params["norm_g"], params["gmlp_ln_g"], params["gmlp_ln_b"], params["gmlp_ws"],
                     params["gmlp_bs"], params["conv_w"], params["conv_b"], params["conv_ln_g"],
                     params["conv_ln_b"], params["pool_scale"], params["mem_norm_g"], params["final_g"])
    shared = dict(hl)
    f = np.float32
    shared["w_in"] = np.asarray(params["w_in"][:NL], f)
    shared["w_kv"] = np.asarray(params["w_kv"][:NL], f)
    shared["w_branch"] = np.asarray(params["w_branch"][:NL], f)
    shared["w_out"] = np.asarray(params["w_out"][:NL], f)
    shared["pool_w"] = np.asarray(params["pool_w"][:NL], f)
    in_maps = []
    for c in range(ncores):
        m = dict(shared)
        m["x"] = np.ascontiguousarray(x[c], dtype=f)
        m["mem"] = np.ascontiguousarray(mem[c], dtype=f)
        in_maps.append(m)
    res = run_bass_kernel_spmd(nc, in_maps, core_ids=list(range(ncores)), trace=trace)
    out = np.stack([np.asarray(r["out"]) for r in res.results], axis=0)
    return out, res


def kernel(x, mem, norm_g, w_in, gmlp_ln_g, gmlp_ln_b, gmlp_ws, gmlp_bs, conv_w, conv_b,
           conv_ln_g, conv_ln_b, pool_w, pool_scale, mem_norm_g, w_kv, w_branch, w_out, final_g):
    params = dict(norm_g=norm_g, w_in=w_in, gmlp_ln_g=gmlp_ln_g, gmlp_ln_b=gmlp_ln_b, gmlp_ws=gmlp_ws,
                  gmlp_bs=gmlp_bs, conv_w=conv_w, conv_b=conv_b, conv_ln_g=conv_ln_g, conv_ln_b=conv_ln_b,
                  pool_w=pool_w, pool_scale=pool_scale, mem_norm_g=mem_norm_g, w_kv=w_kv, w_branch=w_branch,
                  w_out=w_out, final_g=final_g)
    x = np.asarray(x)
    mem = np.asarray(mem)
    out, _ = run_model(x, mem, params, NL=4, ncores=8)
    return out.astype(np.float32)
```

```python
import contextlib
import numpy as np
import concourse.bass as bass
import concourse.mybir as mybir
from concourse.bass_utils import run_bass_kernel_spmd

F32 = mybir.dt.float32
BF16 = mybir.dt.bfloat16
AF = mybir.ActivationFunctionType
ALU = mybir.AluOpType

EPOCH_ENG = 8000
EPOCH_DMA = 500

D_MODEL = 2048
W = 1024
IN_COLS = 18432
TT = 512
EPS = 1e-6
A_U, A_V, A_G, B_A, B_B, B_G, C_IN, C_G, M_Q, M_G, GATES = (
    0, 1024, 2048, 3072, 4096, 5120, 6144, 7168, 8192, 9216, 10240)
POOL_WINDOWS = (2, 4, 8, 16)
NCOLV = 312
CV_NG, CV_MG, CV_CB, CV_CLG, CV_CLB, CV_PS, CV_CW = 0, 16, 32, 40, 48, 56, 64


class _Op:
    __slots__ = ("id", "eng", "fn", "deps", "dma", "need_ev", "ev")

    def __init__(self, id, eng, fn, dma):
        self.id = id
        self.eng = eng
        self.fn = fn
        self.dma = dma
        self.deps = set()
        self.need_ev = False
        self.ev = None


class Sched:
    def __init__(self):
        self.ops = []
        self.lastw = {}
        self.readers = {}

    def add(self, eng, fn, reads=(), writes=(), dma=None):
        op = _Op(len(self.ops), eng, fn, dma)
        lastw, readers = self.lastw, self.readers
        deps = set()
        raw = set()
        for k in reads:
            w = lastw.get(k)
            if w is not None:
                deps.add(w)
                raw.add(w)
        for k in writes:
            w = lastw.get(k)
            if w is not None:
                deps.add(w)
            for r in readers.get(k, ()):
                deps.add(r)
        ops = self.ops
        for d in deps:
            p = ops[d]
            if p.dma is None and dma is None and p.eng == eng:
                if eng == "pe" or d not in raw:
                    continue
            op.deps.add(d)
        for k in reads:
            readers.setdefault(k, []).append(op.id)
        for k in writes:
            lastw[k] = op.id
            readers[k] = []
        ops.append(op)
        return op.id

    def emit(self, nc, stack):
        ops = self.ops
        for op in ops:
            for d in op.deps:
                ops[d].need_ev = True
        counters = {}
        semkeys = set()
        for op in ops:
            if not op.need_ev:
                continue
            if op.dma is not None:
                key, inc, ep = ("dma", op.dma), 16, EPOCH_DMA
            else:
                key, inc, ep = ("eng", op.eng), 1, EPOCH_ENG
            c = counters.get(key, 0)
            counters[key] = c + 1
            epoch, idx = divmod(c, ep)
            op.ev = (key, epoch, (idx + 1) * inc)
            semkeys.add((key, epoch))
        sems = {}
        for i, sk in enumerate(sorted(semkeys, key=str)):
            sems[sk] = stack.enter_context(nc.semaphore("s%d" % i))
        self.nsems = len(sems)
        by_eng = {}
        for op in ops:
            by_eng.setdefault(op.eng, []).append(op)

        def run(eng_name, eng):
            known = {}
            for op in by_eng.get(eng_name, ()):
                need = {}
                for d in op.deps:
                    key, epoch, val = ops[d].ev
                    kn = known.get(key)
                    if kn is not None and kn >= (epoch, val):
                        continue
                    cur = need.get(key)
                    if cur is None or cur < (epoch, val):
                        need[key] = (epoch, val)
                for key, (epoch, val) in need.items():
                    eng.wait_ge(sems[(key, epoch)], val)
                    known[key] = (epoch, val)
                inst = op.fn(eng)
                if op.need_ev:
                    key, epoch, val = op.ev
                    inst.then_inc(sems[(key, epoch)], 16 if op.dma is not None else 1)

        with nc.Block() as block:
            @block.sync
            def _(e):
                run("sp", e)

            @block.tensor
            def _(e):
                run("pe", e)

            @block.scalar
            def _(e):
                run("act", e)

            @block.vector
            def _(e):
                run("dve", e)

            @block.gpsimd
            def _(e):
                run("pool", e)


def build_program(NL, NT, final_norm=True):
    CW = 512
    NS = 6
    NTMP = 6
    R = CW // 128
    nc = bass.Bass("TRN2", target_bir_lowering=False)

    def din(name, shape, dt=F32):
        return nc.dram_tensor(name, shape, dt, kind="ExternalInput").ap()

    x_d = din("x", [NT * TT, D_MODEL])
    mem_d = din("mem", [256, D_MODEL])
    w_in_d = din("w_in", [NL, D_MODEL, IN_COLS])
    w_kv_d = din("w_kv", [NL, D_MODEL, 2 * W])
    w_br_d = din("w_branch", [NL, 4, W, D_MODEL])
    w_out_d = din("w_out", [NL, D_MODEL, D_MODEL])
    pool_w_d = din("pool_w", [NL, 4, 256, 256])
    colv_d = din("colv", [NL, 128, NCOLV])
    fing_d = din("fing", [128, 16])
    wsT_d = din("wsT", [NL, 128, 1024])
    rows3_d = din("rows3", [NL, 128, 3072])
    cst_d = din("cst", [128, 320])
    out_d = nc.dram_tensor("out", [NT * TT, D_MODEL], F32, kind="ExternalOutput").ap()
    kvs_d = nc.dram_tensor("kvs", [NL, 128, 4096], BF16).ap()
    dgs_d = nc.dram_tensor("dgs", [NL, 8, 128, 31 * 128], BF16).ap()

    S = Sched()
    off = [16512]

    def alloc(name, shape, dt, at=None):
        esz = 4 if dt == F32 else 2
        n = esz
        for s in shape[1:]:
            n *= s
        if at is None:
            at = off[0]
            off[0] += (n + 63) // 64 * 64
        return nc.alloc_sbuf_tensor_at(name, list(shape), dt, offset=at)

    xT = alloc("xT", [128, 16, TT], F32)
    hT_off = off[0]
    hT = alloc("hT", [128, 16, TT], BF16)
    memhT = alloc("memhT", [128, 16, 256], F32, at=hT_off)
    yT_off = off[0]
    yT = alloc("yT", [128, 32, TT], BF16)
    rows3 = alloc("rows3", [128, 3, 1024], F32, at=yT_off + 8 * 1024)
    wb = [alloc("wb%d" % i, [128, 8, CW], BF16) for i in range(NS)]
    kvb = alloc("kvb", [128, 4096], BF16)
    A1_off = off[0]
    gv = alloc("gv", [128, 4, 1024], F32)
    cv = alloc("cv", [128, 8, TT], F32, at=A1_off)
    mT = alloc("mT", [128, 16, TT], BF16, at=A1_off)
    xin = alloc("xin", [128, 2, D_MODEL], F32, at=A1_off)
    A2_off = off[0]
    vn = alloc("vn", [128, 4, 1024], BF16)
    diff = alloc("diff", [128, 8, TT], BF16, at=A2_off)
    qT = alloc("qT", [128, 8, TT], BF16, at=A2_off)
    memnT = alloc("memnT", [128, 16, 256], BF16, at=A2_off)
    TMP = [alloc("tmp%d" % i, [128, 544], F32) for i in range(NTMP)]
    DD = [alloc("dd%d" % i, [128, TT], F32) for i in range(4)]
    expT = [alloc("expT%d" % i, [128, 2, TT], BF16) for i in range(2)]
    convh = alloc("convh", [128, NL, 8, 30], BF16)
    hbb = [alloc("hbb%d" % i, [128, 544], BF16) for i in range(4)]
    identb = alloc("identb", [128, 128], BF16)
    poolh = alloc("poolh", [128, NL, 8, 15], F32)
    colv = alloc("colv", [128, NL, NCOLV], F32)
    fing = alloc("fing", [128, 16], F32)
    wsTb = alloc("wsTb", [128, 8, 128], BF16)
    poolw = alloc("poolw", [128, 4, 2, 256], BF16)
    cst = alloc("cst", [128, 320], F32)
    onesb = alloc("onesb", [128, 128], BF16)
    onesf = alloc("onesf", [128, 128], F32)
    stat = alloc("stat", [128, 64], F32)
    assert off[0] <= 16512 + 212800, off[0]
    print("sbuf used", off[0] - 16512, "of 212863")
    PS = [nc.alloc_psum_tensor("ps%d" % i, [128, TT], F32) for i in range(8)]

    ident = cst[:, 0:128]
    maskT = cst[:, 128:256]

    bk = [0]
    held = set()

    def bank():
        while True:
            b = bk[0] % 8
            bk[0] += 1
            if b not in held:
                return b

    tm = [0]

    def tmp():
        i = tm[0] % NTMP
        tm[0] += 1
        return i

    slot_ctr = [0]

    def wload_half(src, eng="pool", extra_reads=(), slot=None):
        if slot is None:
            s = slot_ctr[0] % NS
            slot_ctr[0] += 1
        else:
            s = slot
        if len(src.shape) == 2:
            dst = wb[s][:].rearrange("p k c -> p (k c)")[:, 0:src.shape[1]]
        else:
            dst = wb[s][:]
        S.add(eng, lambda e: e.dma_start(out=dst, in_=src), reads=list(extra_reads), writes=[("wb", s)],
              dma="wb%d" % s)
        return s

    def wload(src, kdim=16, width=CW):
        if kdim == 8:
            return (wload_half(src),)
        return (wload_half(src[:, 0:8, :]), wload_half(src[:, 8:16, :]))

    def wsl(s, k, lo, hi):
        return wb[s[k // 8]][:, k % 8, lo:hi]

    def wkeys(s):
        return [("wb", i) for i in s]

    def mm(b_ap, pairs, reads, writes):
        def fn(e):
            n = len(pairs)
            r = None
            for i, (a, b) in enumerate(pairs):
                r = e.matmul(b_ap, a, b, start=(i == 0), stop=(i == n - 1))
            return r
        S.add("pe", fn, reads=reads, writes=writes)

    def act(out, in_, func, reads, writes, **kw):
        S.add("act", lambda e: e.activation(out=out, in_=in_, func=func, **kw),
              reads=list(reads) + ["cst"], writes=writes)

    def dve(fn, reads, writes):
        S.add("dve", fn, reads=reads, writes=writes)

    def A1k(lo, hi):
        return [("A1", i) for i in range(lo, hi)]

    def A2k(lo, hi):
        return [("A2", i) for i in range(lo, hi)]

    HT_ALL = [("hT", k) for k in range(16)]
    XT_ALL = [("xT", k) for k in range(16)]

    def w_in_v(l):
        return w_in_d[l].rearrange("(k p) c -> p k c", p=128)

    S.add("sp", lambda e: e.dma_start(out=cst[:], in_=cst_d), writes=["cst"], dma="c0")
    S.add("sp", lambda e: e.dma_start(out=colv[:], in_=colv_d.rearrange("l p c -> p l c")),
          writes=["colv"], dma="c1")
    S.add("sp", lambda e: e.dma_start(out=fing[:], in_=fing_d), writes=["fing"], dma="c2")
    dve(lambda e: e.memset(onesf[:], 1.0), [], ["onesf"])
    dve(lambda e: e.memset(onesb[:], 1.0), [], ["onesb"])
    dve(lambda e: e.memset(convh[:], 0.0), [], ["convh"])
    dve(lambda e: e.memset(poolh[:], 0.0), [], ["poolh"])
    dve(lambda e: e.memset(stat[:], 0.0), [], [("stat", i) for i in range(4)])

    for mb in range(2):
        S.add("sp", lambda e, mb=mb: e.dma_start(out=xin[:, mb, :], in_=mem_d[mb * 128:(mb + 1) * 128, :]),
              writes=A1k(4 * mb, 4 * mb + 4), dma="xin%d" % mb)
        ti = tmp()
        for hh in range(4):
            act(TMP[ti][:, 0:512], xin[:, mb, hh * 512:(hh + 1) * 512], AF.Square,
                A1k(4 * mb, 4 * mb + 4), [("tmp", ti), ("stat", hh)],
                accum_out=stat[:, hh * 16 + mb:hh * 16 + mb + 1])
        c0 = 8 + mb
        dve(lambda e, mb=mb, c0=c0: e.tensor_tensor(out=stat[:, c0:c0 + 1], in0=stat[:, mb:mb + 1],
                                                    in1=stat[:, 16 + mb:17 + mb], op=ALU.add),
            [("stat", 0), ("stat", 1)], [("stat", "m")])
        dve(lambda e, mb=mb, c0=c0: e.tensor_tensor(out=stat[:, c0 + 2:c0 + 3], in0=stat[:, 32 + mb:33 + mb],
                                                    in1=stat[:, 48 + mb:49 + mb], op=ALU.add),
            [("stat", 2), ("stat", 3)], [("stat", "m2")])
        dve(lambda e, c0=c0: e.tensor_tensor(out=stat[:, c0:c0 + 1], in0=stat[:, c0:c0 + 1],
                                             in1=stat[:, c0 + 2:c0 + 3], op=ALU.add),
            [("stat", "m"), ("stat", "m2")], [("stat", "m3")])
        act(stat[:, c0 + 4:c0 + 5], stat[:, c0:c0 + 1], AF.Sqrt, [("stat", "m3")], [("stat", "m4")],
            scale=1.0 / D_MODEL, bias=cst[:, 319:320])
        dve(lambda e, c0=c0: e.reciprocal(out=stat[:, c0 + 6:c0 + 7], in_=stat[:, c0 + 4:c0 + 5]),
            [("stat", "m4")], [("stat", "m5")])
        dve(lambda e, mb=mb, c0=c0: e.tensor_scalar(out=xin[:, mb, :], in0=xin[:, mb, :],
                                                    scalar1=stat[:, c0 + 6:c0 + 7], scalar2=None, op0=ALU.mult),
            [("stat", "m5")] + A1k(4 * mb, 4 * mb + 4), A1k(4 * mb, 4 * mb + 4))
        for k4 in range(4):
            b = bank()

            def tr(e, mb=mb, k4=k4, b=b):
                r = None
                for j in range(4):
                    k = k4 * 4 + j
                    r = e.transpose(out=PS[b][:, j * 128:(j + 1) * 128], in_=xin[:, mb, k * 128:(k + 1) * 128],
                                    identity=ident)
                return r
            S.add("pe", tr, reads=A1k(4 * mb, 4 * mb + 4) + ["cst"], writes=[("ps", b)])
            act(memhT[:, k4 * 4:k4 * 4 + 4, mb * 128:(mb + 1) * 128],
                PS[b][:].rearrange("p (j t) -> p j t", j=4), AF.Copy, [("ps", b)], HT_ALL)
    for l in range(NL):
        for k in range(16):
            dve(lambda e, l=l, k=k: e.tensor_scalar(out=memnT[:, k, :], in0=memhT[:, k, :],
                                                    scalar1=colv[:, l, CV_MG + k:CV_MG + k + 1], scalar2=None,
                                                    op0=ALU.mult),
                HT_ALL + ["colv"], A2k(0, 4))
        wkv = w_kv_d[l].rearrange("(k p) c -> p k c", p=128)
        for half in range(2):
            s = wload(wkv[:, :, half * CW:(half + 1) * CW])
            b = bank()
            for r in range(R):
                fb = half * R + r
                pairs = [(wsl(s, k, r * 128, (r + 1) * 128), memnT[:, k, :]) for k in range(16)]
                if r % 2 == 0:
                    b = bank()
                mm(PS[b][:, (r % 2) * 256:(r % 2) * 256 + 256], pairs, wkeys(s) + A2k(0, 4), [("ps", b)])
                if r % 2 == 1:
                    act(kvb[:, (fb - 1) * 256:(fb + 1) * 256], PS[b][:], AF.Copy, [("ps", b)], ["kvb"])
        for half in range(2):
            s = wload(wkv[:, :, W + half * CW:W + (half + 1) * CW])
            for mb in range(2):
                b = bank()
                pairs = [(memnT[:, k, mb * 128:(mb + 1) * 128], wsl(s, k, 0, CW)) for k in range(16)]
                mm(PS[b][:], pairs, wkeys(s) + A2k(0, 4), [("ps", b)])
                o0 = 2048 + mb * 1024 + half * 512
                act(kvb[:, o0:o0 + 512], PS[b][:], AF.Copy, [("ps", b)], ["kvb"])
        S.add("sp", lambda e, l=l: e.dma_start(out=kvs_d[l], in_=kvb[:]), reads=["kvb"], writes=[("kvs", l)],
              dma="kvs")

    dve(lambda e: e.tensor_copy(out=identb[:], in_=ident), ["cst"], ["identb"])
    for l in range(NL):
        for c in range(8):
            sd = slot_ctr[0] % NS
            slot_ctr[0] += 1
            dgv = wb[sd][:].rearrange("p k c -> p (k c)")
            for kk in range(31):
                dve(lambda e, dgv=dgv, kk=kk, l=l, c=c: e.tensor_scalar(
                    out=dgv[:, kk * 128:(kk + 1) * 128], in0=identb[:],
                    scalar1=colv[:, l, CV_CW + c * 31 + kk:CV_CW + c * 31 + kk + 1], scalar2=None,
                    op0=ALU.mult), ["identb", "colv"], [("wb", sd)])
            S.add("sp", lambda e, dgv=dgv, l=l, c=c: e.dma_start(out=dgs_d[l, c], in_=dgv[:, 0:31 * 128]),
                  reads=[("wb", sd)], writes=[("dgs", l, c)], dma="dgs%d" % sd)

    KT = kvb[:, 0:2048].rearrange("p (c m) -> p c m", c=8)
    VV = kvb[:, 2048:4096].rearrange("p (b f) -> p b f", b=2)

    def rms_tail(b, dst):
        ti = tmp()
        act(TMP[ti][:, 0:512], PS[b][:], AF.Sqrt, [("ps", b)], [("tmp", ti)],
            scale=1.0 / D_MODEL, bias=cst[:, 319:320])
        dve(lambda e, ti=ti: e.reciprocal(out=dst, in_=TMP[ti][:, 0:512]), [("tmp", ti)], [("D", 0)])

    def hilo(src_ap, src_keys):
        th = tmp()
        hv = TMP[th][:].bitcast(BF16)
        dve(lambda e, hv=hv: e.tensor_copy(out=hv[:, 0:512], in_=src_ap), list(src_keys), [("tmp", th)])
        dve(lambda e, hv=hv: e.tensor_tensor(out=hv[:, 512:1024], in0=src_ap, in1=hv[:, 0:512],
                                             op=ALU.subtract), list(src_keys) + [("tmp", th)], [("tmp", th)])
        return th

    def stat_mm(b, th, first, last):
        hv = TMP[th][:].bitcast(BF16)

        def fn(e):
            e.matmul(PS[b][:], onesb[:], hv[:, 0:512], start=first, stop=False)
            return e.matmul(PS[b][:], onesb[:], hv[:, 512:1024], start=False, stop=last)
        S.add("pe", fn, reads=[("tmp", th), "onesb"], writes=[("ps", b)])

    def rms_sq(k):
        ti = tmp()
        act(TMP[ti][:, 0:512], xT[:, k, :], AF.Square, [("xT", k)], [("tmp", ti)])
        return hilo(TMP[ti][:, 0:512], [("tmp", ti)])

    def rms_acc(b, k, th):
        stat_mm(b, th, k == 0, k == 15)

    def rms_stats(dst):
        b = bank()
        for k in range(16):
            rms_acc(b, k, rms_sq(k))
        rms_tail(b, dst)

    pend_rms = [None]

    for t in range(NT):
        for tb in range(4):
            xi = tb % 2
            r0 = t * TT + tb * 128
            S.add("sp", lambda e, xi=xi, r0=r0: e.dma_start(out=xin[:, xi, :], in_=x_d[r0:r0 + 128, :]),
                  writes=A1k(4 * xi, 4 * xi + 4), dma="xin%d" % xi)
            for k4 in range(4):
                b = bank()

                def tr(e, xi=xi, k4=k4, b=b):
                    r = None
                    for j in range(4):
                        k = k4 * 4 + j
                        r = e.transpose(out=PS[b][:, j * 128:(j + 1) * 128],
                                        in_=xin[:, xi, k * 128:(k + 1) * 128], identity=ident)
                    return r
                S.add("pe", tr, reads=A1k(4 * xi, 4 * xi + 4) + ["cst"], writes=[("ps", b)])
                dve(lambda e, k4=k4, tb=tb, b=b: e.tensor_copy(
                    out=xT[:, k4 * 4:k4 * 4 + 4, tb * 128:(tb + 1) * 128],
                    in_=PS[b][:].rearrange("p (j t) -> p j t", j=4)),
                    [("ps", b)], [("xT", k4 * 4 + j) for j in range(4)])

        for l in range(NL):
            wv = w_in_v(l)

            def cv_col(c0, l=l):
                return colv[:, l, c0:c0 + 1]

            if pend_rms[0] is None:
                rms_stats(DD[0][:])
            else:
                rms_tail(pend_rms[0], DD[0][:])
                held.discard(pend_rms[0])
                pend_rms[0] = None
            for k in range(16):
                dve(lambda e, k=k, l=l: e.scalar_tensor_tensor(
                    out=hT[:, k, :], in0=xT[:, k, :], scalar=colv[:, l, CV_NG + k:CV_NG + k + 1],
                    in1=DD[0][:], op0=ALU.mult, op1=ALU.mult),
                    [("xT", k), ("D", 0), "colv"], [("hT", k)])

            def proj_block(s, r, b):
                pairs = [(wsl(s, k, r * 128, (r + 1) * 128), hT[:, k, :]) for k in range(16)]
                mm(PS[b][:], pairs, wkeys(s) + HT_ALL, [("ps", b)])

            S.add("sp", lambda e, l=l: e.dma_start(out=rows3[:], in_=rows3_d[l].rearrange("p (a b) -> p a b", a=3)),
                  writes=[("yT", c) for c in range(8, 20)], dma="rows3")
            S.add("sp", lambda e, l=l: e.dma_start(out=kvb[:], in_=kvs_d[l]), reads=[("kvs", l)],
                  writes=["kvb"], dma="kvl")

            sq = [wload(wv[:, :, M_Q + i * CW:M_Q + (i + 1) * CW]) for i in range(2)]
            for c in range(8):
                b = bank()
                proj_block(sq[c // R], c % R, b)
                act(qT[:, c, :], PS[b][:], AF.Copy, [("ps", b)], A2k(c // 2, c // 2 + 1))
            sg = [None]

            def m_scores(h):
                eb = h % 2
                for mb in range(2):
                    b = bank()
                    pairs = [(KT[:, 2 * h + dc, mb * 128:(mb + 1) * 128], qT[:, 2 * h + dc, :]) for dc in range(2)]
                    mm(PS[b][:], pairs, ["kvb"] + A2k(h, h + 1), [("ps", b)])
                    act(expT[eb][:, mb, :], PS[b][:], AF.Exp, [("ps", b)], [("expT", eb)], scale=0.0625)

            def m_tail(h, l=l, wv=wv):
                eb = h % 2
                b = bank()
                mm(PS[b][:], [(onesb[:], expT[eb][:, mb, :]) for mb in range(2)], [("expT", eb), "onesb"],
                   [("ps", b)])
                di = 1 + eb
                dve(lambda e, b=b, di=di: e.reciprocal(out=DD[di][:], in_=PS[b][:]), [("ps", b)], [("D", di)])
                for dblk in range(2):
                    c = 2 * h + dblk
                    if c % R == 0:
                        sg[0] = wload(wv[:, :, M_G + (c // R) * CW:M_G + (c // R + 1) * CW])
                    bo = bank()
                    pairs = [(VV[:, mb, c * 128:(c + 1) * 128], expT[eb][:, mb, :]) for mb in range(2)]
                    mm(PS[bo][:], pairs, ["kvb", ("expT", eb)], [("ps", bo)])
                    bg = bank()
                    proj_block(sg[0], c % R, bg)
                    t2 = tmp()
                    act(TMP[t2][:, 0:512], PS[bg][:], AF.Silu, [("ps", bg)], [("tmp", t2)])
                    dve(lambda e, t2=t2, di=di: e.tensor_tensor(out=TMP[t2][:, 0:512], in0=TMP[t2][:, 0:512],
                                                                in1=DD[di][:], op=ALU.mult),
                        [("tmp", t2), ("D", di)], [("tmp", t2)])
                    dve(lambda e, t2=t2, bo=bo, c=c: e.tensor_tensor(out=yT[:, 24 + c, :], in0=PS[bo][:],
                                                                     in1=TMP[t2][:, 0:512], op=ALU.mult),
                        [("tmp", t2), ("ps", bo)], [("yT", 24 + c)])

            for h in range(4):
                m_scores(h)
                if h > 0:
                    m_tail(h - 1)
            m_tail(3)

            for hh in range(2):
                ti = tmp()
                S.add("sp", lambda e, l=l, hh=hh, ti=ti: e.dma_start(out=TMP[ti][:, 0:512],
                                                                   in_=wsT_d[l, :, hh * 512:(hh + 1) * 512]),
                      writes=[("tmp", ti)], dma="wsT%d" % hh)
                for g4 in range(4):
                    g = hh * 4 + g4
                    dve(lambda e, ti=ti, g=g, g4=g4: e.tensor_tensor(out=wsTb[:, g, :],
                                                                      in0=TMP[ti][:, g4 * 128:(g4 + 1) * 128],
                                                                      in1=maskT, op=ALU.mult),
                        [("tmp", ti), "cst"], [("wsTb", g)])
            sv = [wload(wv[:, :, A_V + i * CW:A_V + (i + 1) * CW]) for i in range(2)]
            for tb in range(4):
                for half in range(2):
                    b = bank()
                    pairs = [(hT[:, k, tb * 128:(tb + 1) * 128], wsl(sv[half], k, 0, CW)) for k in range(16)]
                    mm(PS[b][:], pairs, wkeys(sv[half]) + HT_ALL, [("ps", b)])
                    act(gv[:, tb, half * 512:(half + 1) * 512], PS[b][:], AF.Gelu_apprx_tanh,
                        [("ps", b)], [("A1", tb * 2 + half)])
            RY = [("yT", c) for c in range(8, 20)]
            for tb in range(4):
                sc = tb * 16
                gk = A1k(tb * 2, tb * 2 + 2)
                dve(lambda e, tb=tb, sc=sc: e.bn_stats(out=stat[:, sc:sc + 6], in_=gv[:, tb, 0:512]),
                    gk, [("stat", tb)])
                dve(lambda e, tb=tb, sc=sc: e.bn_stats(out=stat[:, sc + 6:sc + 12], in_=gv[:, tb, 512:1024]),
                    gk, [("stat", tb, 1)])
                dve(lambda e, sc=sc: e.bn_aggr(out=stat[:, sc + 12:sc + 14], in_=stat[:, sc:sc + 12]),
                    [("stat", tb), ("stat", tb, 1)], [("stat", tb, 2)])
                act(stat[:, sc + 14:sc + 15], stat[:, sc + 13:sc + 14], AF.Sqrt, [("stat", tb, 2)],
                    [("stat", tb, 3)], scale=1.0, bias=cst[:, 319:320])
                dve(lambda e, sc=sc: e.reciprocal(out=stat[:, sc + 15:sc + 16], in_=stat[:, sc + 14:sc + 15]),
                    [("stat", tb, 3)], [("stat", tb, 4)])
                dve(lambda e, tb=tb, sc=sc: e.tensor_scalar(out=gv[:, tb, :], in0=gv[:, tb, :],
                                                            scalar1=stat[:, sc + 12:sc + 13],
                                                            scalar2=stat[:, sc + 15:sc + 16],
                                                            op0=ALU.subtract, op1=ALU.mult),
                    gk + [("stat", tb, 2), ("stat", tb, 4)], gk)
                dve(lambda e, tb=tb: e.tensor_tensor(out=gv[:, tb, :], in0=gv[:, tb, :], in1=rows3[:, 0, :],
                                                     op=ALU.mult), gk + RY, gk)
                dve(lambda e, tb=tb: e.tensor_tensor(out=vn[:, tb, :], in0=gv[:, tb, :], in1=rows3[:, 1, :],
                                                     op=ALU.add), gk + RY, A2k(tb, tb + 1))
            su = [None]
            sgA = [None]
            for c in range(8):
                if c % R == 0:
                    su[0] = wload(wv[:, :, A_U + (c // R) * CW:A_U + (c // R + 1) * CW])
                    sgA[0] = wload(wv[:, :, A_G + (c // R) * CW:A_G + (c // R + 1) * CW])
                bu = bank()
                proj_block(su[0], c % R, bu)
                t1 = tmp()
                act(TMP[t1][:, 0:512], PS[bu][:], AF.Gelu_apprx_tanh, [("ps", bu)], [("tmp", t1)])
                bg = bank()
                proj_block(sgA[0], c % R, bg)
                t2 = tmp()
                act(TMP[t2][:, 0:512], PS[bg][:], AF.Silu, [("ps", bg)], [("tmp", t2)])
                dve(lambda e, t1=t1, t2=t2: e.tensor_tensor(out=TMP[t1][:, 0:512], in0=TMP[t1][:, 0:512],
                                                            in1=TMP[t2][:, 0:512], op=ALU.mult),
                    [("tmp", t1), ("tmp", t2)], [("tmp", t1)])
                bs_ = bank()

                def sp_mm(e, c=c, bs_=bs_):
                    r = None
                    for tb in range(4):
                        r = e.matmul(PS[bs_][:, tb * 128:(tb + 1) * 128], vn[:, tb, c * 128:(c + 1) * 128],
                                     wsTb[:, c, :], start=True, stop=True)
                    return r
                S.add("pe", sp_mm, reads=A2k(0, 4) + [("wsTb", c)], writes=[("ps", bs_)])
                t3 = tmp()
                for tb in range(4):
                    dve(lambda e, t3=t3, bs_=bs_, c=c, tb=tb: e.tensor_tensor(
                        out=TMP[t3][:, tb * 128:(tb + 1) * 128], in0=PS[bs_][:, tb * 128:(tb + 1) * 128],
                        in1=rows3[:, 2, c * 128:(c + 1) * 128], op=ALU.add),
                        [("ps", bs_)] + RY, [("tmp", t3)])
                dve(lambda e, t1=t1, t3=t3, c=c: e.tensor_tensor(out=yT[:, c, :], in0=TMP[t3][:, 0:512],
                                                                 in1=TMP[t1][:, 0:512], op=ALU.mult),
                    [("tmp", t1), ("tmp", t3)], [("yT", c)])

            for g0 in (0, 4):
                sa = wload(wv[:, :, B_A + (g0 // R) * CW:B_A + (g0 // R + 1) * CW])
                sb = wload(wv[:, :, B_B + (g0 // R) * CW:B_B + (g0 // R + 1) * CW])
                dsl = (slot_ctr[0] % NS, (slot_ctr[0] + 1) % NS)
                slot_ctr[0] += 2
                for c in range(g0, g0 + 4):
                    ba = bank()
                    proj_block(sa, c % R, ba)
                    bb = bank()
                    proj_block(sb, c % R, bb)
                    t1 = tmp()
                    act(TMP[t1][:, 0:512], PS[bb][:], AF.Sigmoid, [("ps", bb)], [("tmp", t1)])
                    hi = c % 4
                    act(hbb[hi][:, 0:30], convh[:, l, c, :], AF.Copy, ["convh"], [("hbb", hi)])
                    dve(lambda e, hi=hi, ba=ba, t1=t1: e.tensor_tensor(out=hbb[hi][:, 30:542], in0=PS[ba][:],
                                                                       in1=TMP[t1][:, 0:512], op=ALU.mult),
                        [("ps", ba), ("tmp", t1)], [("hbb", hi)])
                    act(convh[:, l, c, :], hbb[hi][:, 512:542], AF.Copy, [("hbb", hi)], ["convh"])
                for c in range(g0, g0 + 4):
                    hi = c % 4
                    sd = wload_half(dgs_d[l, c], eng="sp", extra_reads=[("dgs", l, c)], slot=dsl[c % 2])
                    bc = bank()
                    dgv = wb[sd][:].rearrange("p k c -> p (k c)")
                    pairs = [(dgv[:, kk * 128:(kk + 1) * 128], hbb[hi][:, kk:kk + 512]) for kk in range(31)]
                    mm(PS[bc][:], pairs, [("wb", sd), ("hbb", hi)], [("ps", bc)])
                    dve(lambda e, c=c, bc=bc, l=l: e.tensor_scalar(out=cv[:, c, :], in0=PS[bc][:],
                                                                   scalar1=colv[:, l, CV_CB + c:CV_CB + c + 1],
                                                                   scalar2=None, op0=ALU.add),
                        [("ps", bc), "colv"], [("A1", c)])
            b1 = bank()
            held.add(b1)
            b2 = bank()
            held.add(b2)
            for c in range(8):
                th = hilo(cv[:, c, :], [("A1", c)])
                stat_mm(b1, th, c == 0, c == 7)
                ti = tmp()
                act(TMP[ti][:, 0:512], cv[:, c, :], AF.Square, [("A1", c)], [("tmp", ti)])
                th2 = hilo(TMP[ti][:, 0:512], [("tmp", ti)])
                stat_mm(b2, th2, c == 0, c == 7)
            held.discard(b1)
            held.discard(b2)
            dve(lambda e, b1=b1: e.tensor_scalar(out=DD[0][:], in0=PS[b1][:], scalar1=1.0 / W, scalar2=None,
                                                 op0=ALU.mult), [("ps", b1)], [("D", 0)])
            tq = tmp()
            dve(lambda e, tq=tq: e.tensor_tensor(out=TMP[tq][:, 0:512], in0=DD[0][:], in1=DD[0][:], op=ALU.mult),
                [("D", 0)], [("tmp", tq)])
            dve(lambda e, tq=tq, b2=b2: e.scalar_tensor_tensor(out=TMP[tq][:, 0:512], in0=PS[b2][:],
                                                               scalar=1.0 / W, in1=TMP[tq][:, 0:512],
                                                               op0=ALU.mult, op1=ALU.subtract),
                [("ps", b2), ("tmp", tq)], [("tmp", tq)])
            act(TMP[tq][:, 0:512], TMP[tq][:, 0:512], AF.Sqrt, [("tmp", tq)], [("tmp", tq)],
                scale=1.0, bias=cst[:, 319:320])
            dve(lambda e, tq=tq: e.reciprocal(out=DD[1][:], in_=TMP[tq][:, 0:512]), [("tmp", tq)], [("D", 1)])
            sgB = [None]
            for c in range(8):
                if c % R == 0:
                    sgB[0] = wload(wv[:, :, B_G + (c // R) * CW:B_G + (c // R + 1) * CW])
                t1 = tmp()
                dve(lambda e, t1=t1, c=c: e.tensor_tensor(out=TMP[t1][:, 0:512], in0=cv[:, c, :], in1=DD[0][:],
                                                          op=ALU.subtract), [("A1", c), ("D", 0)], [("tmp", t1)])
                dve(lambda e, t1=t1: e.tensor_tensor(out=TMP[t1][:, 0:512], in0=TMP[t1][:, 0:512], in1=DD[1][:],
                                                     op=ALU.mult), [("tmp", t1), ("D", 1)], [("tmp", t1)])
                act(TMP[t1][:, 0:512], TMP[t1][:, 0:512], AF.Silu, [("tmp", t1), "colv"], [("tmp", t1)],
                    scale=cv_col(CV_CLG + c), bias=cv_col(CV_CLB + c))
                bg = bank()
                proj_block(sgB[0], c % R, bg)
                t2 = tmp()
                act(TMP[t2][:, 0:512], PS[bg][:], AF.Silu, [("ps", bg)], [("tmp", t2)])
                dve(lambda e, t1=t1, t2=t2, c=c: e.tensor_tensor(out=yT[:, 8 + c, :], in0=TMP[t1][:, 0:512],
                                                                 in1=TMP[t2][:, 0:512], op=ALU.mult),
                    [("tmp", t1), ("tmp", t2)], [("yT", 8 + c)])

            S.add("pool", lambda e, l=l: e.dma_start(
                out=poolw[:], in_=pool_w_d[l].rearrange("g (k p) d -> p g k d", p=128)),
                writes=["poolw"], dma="poolw")
            sc_ = [None]
            for c in range(8):
                if c % R == 0:
                    sc_[0] = wload(wv[:, :, C_IN + (c // R) * CW:C_IN + (c // R + 1) * CW])
                b = bank()
                proj_block(sc_[0], c % R, b)
                hb = tmp()
                act(TMP[hb][:, 0:15], poolh[:, l, c, :], AF.Copy, ["poolh"], [("tmp", hb)])
                act(TMP[hb][:, 15:527], PS[b][:], AF.Copy, [("ps", b)], [("tmp", hb)])
                act(poolh[:, l, c, :], TMP[hb][:, 512:527], AF.Copy, [("tmp", hb)], ["poolh"])
                wi = c // 2
                win = POOL_WINDOWS[wi]
                src = hb
                lo = 0
                sh = 1
                while sh < win:
                    dst = tmp()
                    lo2 = lo + sh
                    dve(lambda e, src=src, dst=dst, lo2=lo2, sh=sh: e.tensor_tensor(
                        out=TMP[dst][:, lo2:527], in0=TMP[src][:, lo2:527], in1=TMP[src][:, lo2 - sh:527 - sh],
                        op=ALU.add), [("tmp", src)], [("tmp", dst)])
                    src = dst
                    lo = lo2
                    sh *= 2
                dve(lambda e, src=src, hb=hb, c=c, win=win: e.scalar_tensor_tensor(
                    out=diff[:, c, :], in0=TMP[src][:, 15:527], scalar=1.0 / win, in1=TMP[hb][:, 15:527],
                    op0=ALU.mult, op1=ALU.subtract), [("tmp", src), ("tmp", hb)], A2k(c // 2, c // 2 + 1))
                if t == 0:
                    t5 = tmp()
                    dve(lambda e, src=src, t5=t5, wi=wi: e.tensor_tensor(
                        out=TMP[t5][:, 0:15], in0=TMP[src][:, 15:30], in1=cst[:, 256 + wi * 16:256 + wi * 16 + 15],
                        op=ALU.mult), [("tmp", src), "cst"], [("tmp", t5)])
                    dve(lambda e, t5=t5, hb=hb, c=c: e.tensor_tensor(
                        out=diff[:, c, 0:15], in0=TMP[t5][:, 0:15], in1=TMP[hb][:, 15:30], op=ALU.subtract),
                        [("tmp", t5), ("tmp", hb)] + A2k(c // 2, c // 2 + 1), A2k(c // 2, c // 2 + 1))
            sgC = [None]
            for c in range(8):
                if c % R == 0:
                    sgC[0] = wload(wv[:, :, C_G + (c // R) * CW:C_G + (c // R + 1) * CW])
                g, db = c // 2, c % 2
                bp = bank()
                pairs = [(poolw[:, g, k, db * 128:(db + 1) * 128], diff[:, 2 * g + k, :]) for k in range(2)]
                mm(PS[bp][:], pairs, ["poolw"] + A2k(g, g + 1), [("ps", bp)])
                bg = bank()
                proj_block(sgC[0], c % R, bg)
                t2 = tmp()
                act(TMP[t2][:, 0:512], PS[bg][:], AF.Silu, [("ps", bg)], [("tmp", t2)])
                dve(lambda e, t2=t2, bp=bp, c=c, l=l: e.scalar_tensor_tensor(
                    out=yT[:, 16 + c, :], in0=PS[bp][:], scalar=colv[:, l, CV_PS + c:CV_PS + c + 1],
                    in1=TMP[t2][:, 0:512], op0=ALU.mult, op1=ALU.mult),
                    [("ps", bp), ("tmp", t2), "colv"], [("yT", 16 + c)])

            for dg in range(D_MODEL // CW):
                for n in range(4):
                    sgt = wload(wv[:, :, GATES + n * D_MODEL + dg * CW:GATES + n * D_MODEL + (dg + 1) * CW])
                    sbr = wload(w_br_d[l, n].rearrange("(k p) c -> p k c", p=128)[:, :, dg * CW:(dg + 1) * CW],
                                kdim=8)
                    for r in range(R):
                        bp = bank()
                        pairs = [(wsl(sbr, k, r * 128, (r + 1) * 128), yT[:, n * 8 + k, :]) for k in range(8)]
                        mm(PS[bp][:], pairs, wkeys(sbr) + [("yT", n * 8 + k) for k in range(8)], [("ps", bp)])
                        bg = bank()
                        proj_block(sgt, r, bg)
                        t2 = tmp()
                        act(TMP[t2][:, 0:512], PS[bg][:], AF.Sigmoid, [("ps", bg)], [("tmp", t2)])
                        mi = dg * R + r
                        if n == 0:
                            dve(lambda e, r=r, bp=bp, t2=t2: e.tensor_tensor(out=DD[r][:], in0=PS[bp][:],
                                                                             in1=TMP[t2][:, 0:512], op=ALU.mult),
                                [("ps", bp), ("tmp", t2)], [("D", r)])
                        else:
                            dve(lambda e, bp=bp, t2=t2: e.tensor_tensor(out=TMP[t2][:, 0:512], in0=PS[bp][:],
                                                                        in1=TMP[t2][:, 0:512], op=ALU.mult),
                                [("ps", bp), ("tmp", t2)], [("tmp", t2)])
                            if n < 3:
                                dve(lambda e, r=r, t2=t2: e.tensor_tensor(out=DD[r][:], in0=DD[r][:],
                                                                          in1=TMP[t2][:, 0:512], op=ALU.add),
                                    [("D", r), ("tmp", t2)], [("D", r)])
                            else:
                                dve(lambda e, r=r, t2=t2, mi=mi: e.tensor_tensor(out=mT[:, mi, :], in0=DD[r][:],
                                                                                 in1=TMP[t2][:, 0:512],
                                                                                 op=ALU.add),
                                    [("D", r), ("tmp", t2)], [("A1", mi // 2)])

            wo = w_out_d[l].rearrange("(k p) c -> p k c", p=128)
            so = [None]
            rb = bank()
            held.add(rb)
            pend_rms[0] = rb
            sqq = []
            for eb_ in range(16):
                if eb_ % R == 0:
                    so[0] = wload(wo[:, :, (eb_ // R) * CW:(eb_ // R + 1) * CW])
                b = bank()
                pairs = [(wsl(so[0], k, (eb_ % R) * 128, (eb_ % R + 1) * 128), mT[:, k, :]) for k in range(16)]
                mm(PS[b][:], pairs, wkeys(so[0]) + A1k(0, 8), [("ps", b)])
                dve(lambda e, eb_=eb_, b=b: e.tensor_tensor(out=xT[:, eb_, :], in0=PS[b][:], in1=xT[:, eb_, :],
                                                            op=ALU.add), [("ps", b), ("xT", eb_)], [("xT", eb_)])
                sqq.append((eb_, rms_sq(eb_)))
                if len(sqq) > 2:
                    k_, ti_ = sqq.pop(0)
                    rms_acc(rb, k_, ti_)
            for k_, ti_ in sqq:
                rms_acc(rb, k_, ti_)
            if not final_norm and l == NL - 1:
                held.discard(rb)
                pend_rms[0] = None

        if final_norm:
            rms_tail(pend_rms[0], DD[0][:])
            held.discard(pend_rms[0])
            pend_rms[0] = None
            for k in range(16):
                dve(lambda e, k=k: e.scalar_tensor_tensor(out=xT[:, k, :], in0=xT[:, k, :],
                                                          scalar=fing[:, k:k + 1], in1=DD[0][:],
                                                          op0=ALU.mult, op1=ALU.mult),
                    [("xT", k), ("D", 0), "fing"], [("xT", k)])
        for tb in range(4):
            xi = tb % 2
            for k4 in range(4):
                b = bank()

                def tr2(e, tb=tb, k4=k4, b=b):
                    r = None
                    for j in range(4):
                        k = k4 * 4 + j
                        r = e.transpose(out=PS[b][:, j * 128:(j + 1) * 128],
                                        in_=xT[:, k, tb * 128:(tb + 1) * 128], identity=ident)
                    return r
                S.add("pe", tr2, reads=[("xT", k4 * 4 + j) for j in range(4)] + ["cst"], writes=[("ps", b)])
                dve(lambda e, xi=xi, k4=k4, b=b: e.tensor_copy(out=xin[:, xi, k4 * 512:(k4 + 1) * 512],
                                                               in_=PS[b][:]),
                    [("ps", b)], A1k(4 * xi + k4, 4 * xi + k4 + 1))
            r0 = t * TT + tb * 128
            S.add("sp", lambda e, xi=xi, r0=r0: e.dma_start(out=out_d[r0:r0 + 128, :], in_=xin[:, xi, :]),
                  reads=A1k(4 * xi, 4 * xi + 4), writes=[("out", t, tb)], dma="out%d" % xi)
    S.add("sp", lambda e: e.nop(), reads=[("out", t, tb) for t in range(NT) for tb in range(4)])
    return nc, S


def host_layout(NL, norm_g, gmlp_ln_g, gmlp_ln_b, gmlp_ws, gmlp_bs, conv_w, conv_b, conv_ln_g, conv_ln_b,
                pool_scale, mem_norm_g, final_g):
    f = np.float32

    def cols(v, n):
        return np.asarray(v, f).reshape(n, 128).T

    colv = np.zeros((NL, 128, NCOLV), f)
    for l in range(NL):
        colv[l, :, CV_NG:CV_NG + 16] = cols(norm_g[l], 16)
        colv[l, :, CV_MG:CV_MG + 16] = cols(mem_norm_g[l], 16)
        colv[l, :, CV_CB:CV_CB + 8] = cols(conv_b[l], 8)
        colv[l, :, CV_CLG:CV_CLG + 8] = cols(conv_ln_g[l], 8)
        colv[l, :, CV_CLB:CV_CLB + 8] = cols(conv_ln_b[l], 8)
        colv[l, :, CV_PS:CV_PS + 8] = cols(pool_scale[l], 8)
        cw = np.asarray(conv_w[l], f)
        colv[l, :, CV_CW:CV_CW + 248] = cw.reshape(31, 8, 128).transpose(2, 1, 0).reshape(128, 248)
    fing = cols(final_g, 16).copy()
    wsT = np.ascontiguousarray(np.asarray(gmlp_ws[:NL], f).transpose(0, 3, 1, 2).reshape(NL, 128, 1024))
    rows3 = np.zeros((NL, 128, 3072), f)
    for l in range(NL):
        rows3[l, :, 0:1024] = np.asarray(gmlp_ln_g[l], f)[None, :]
        rows3[l, :, 1024:2048] = np.asarray(gmlp_ln_b[l], f)[None, :]
        rows3[l, :, 2048:3072] = np.asarray(gmlp_bs[l], f).reshape(1, 1024)
    cst = np.zeros((128, 320), f)
    cst[:, 0:128] = np.eye(128, dtype=f)
    s_idx = np.arange(128)[:, None]
    t_idx = np.arange(128)[None, :]
    cst[:, 128:256] = (s_idx <= t_idx).astype(f)
    for wi, win in enumerate(POOL_WINDOWS):
        cst[:, 256 + wi * 16:256 + wi * 16 + 15] = (1.0 / np.minimum(np.arange(1, 16), win)).astype(f)[None, :]
    cst[:, 319] = EPS
    return dict(colv=colv, fing=fing, wsT=wsT, rows3=rows3, cst=cst)


_CACHE = {}


def run_model(x, mem, params, NL, ncores, trace=False):
    T = x.shape[1]
    NT = T // TT
    key = (NL, NT)
    if key not in _CACHE:
        with contextlib.ExitStack() as st:
            nc, S = build_program(NL, NT)
            S.emit(nc, st)
        _CACHE[key] = nc
    nc = _CACHE[key]
    hl = host_layout(NL, params["norm_g"], params["gmlp_ln_g"], params["gmlp_ln_b"], params["gmlp_ws"],
                     params["gmlp_bs"], params["conv_w"], params["conv_b"], params["conv_ln_g"],
                     params["conv_ln_b"], params["pool_scale"], params["mem_norm_g"], params["final_g"])
    shared = dict(hl)
    f = np.float32
    shared["w_in"] = np.asarray(params["w_in"][:NL], f)
    shared["w_kv"] = np.asarray(params["w_kv"][:NL], f)
    shared["w_branch"] = np.asarray(params["w_branch"][:NL], f)
    shared["w_out"] = np.asarray(params["w_out"][:NL], f)
    shared["pool_w"] = np.asarray(params["pool_w"][:NL], f)
    in_maps = []
    for c in range(ncores):
        m = dict(shared)
        m["x"] = np.ascontiguousarray(x[c], dtype=f)
        m["mem"] = np.ascontiguousarray(mem[c], dtype=f)
        in_maps.append(m)
    res = run_bass_kernel_spmd(nc, in_maps, core_ids=list(range(ncores)), trace=trace)
    out = np.stack([np.asarray(r["out"]) for r in res.results], axis=0)
    return out, res


def kernel(x, mem, norm_g, w_in, gmlp_ln_g, gmlp_ln_b, gmlp_ws, gmlp_bs, conv_w, conv_b,
           conv_ln_g, conv_ln_b, pool_w, pool_scale, mem_norm_g, w_kv, w_branch, w_out, final_g):
    params = dict(norm_g=norm_g, w_in=w_in, gmlp_ln_g=gmlp_ln_g, gmlp_ln_b=gmlp_ln_b, gmlp_ws=gmlp_ws,
                  gmlp_bs=gmlp_bs, conv_w=conv_w, conv_b=conv_b, conv_ln_g=conv_ln_g, conv_ln_b=conv_ln_b,
                  pool_w=pool_w, pool_scale=pool_scale, mem_norm_g=mem_norm_g, w_kv=w_kv, w_branch=w_branch,
                  w_out=w_out, final_g=final_g)
    x = np.asarray(x)
    mem = np.asarray(mem)
    out, _ = run_model(x, mem, params, NL=4, ncores=8)
    return out.astype(np.float32)
```

```python
import contextlib
import numpy as np
import concourse.bass as bass
import concourse.mybir as mybir
from concourse.bass_utils import run_bass_kernel_spmd

F32 = mybir.dt.float32
BF16 = mybir.dt.bfloat16
AF = mybir.ActivationFunctionType
ALU = mybir.AluOpType

EPOCH_ENG = 8000
EPOCH_DMA = 500

D_MODEL = 2048
W = 1024
IN_COLS = 18432
TT = 512
EPS = 1e-6
A_U, A_V, A_G, B_A, B_B, B_G, C_IN, C_G, M_Q, M_G, GATES = (
    0, 1024, 2048, 3072, 4096, 5120, 6144, 7168, 8192, 9216, 10240)
POOL_WINDOWS = (2, 4, 8, 16)
NCOLV = 312
CV_NG, CV_MG, CV_CB, CV_CLG, CV_CLB, CV_PS, CV_CW = 0, 16, 32, 40, 48, 56, 64


class _Op:
    __slots__ = ("id", "eng", "fn", "deps", "dma", "need_ev", "ev")

    def __init__(self, id, eng, fn, dma):
        self.id = id
        self.eng = eng
        self.fn = fn
        self.dma = dma
        self.deps = set()
        self.need_ev = False
        self.ev = None


class Sched:
    def __init__(self):
        self.ops = []
        self.lastw = {}
        self.readers = {}

    def add(self, eng, fn, reads=(), writes=(), dma=None):
        op = _Op(len(self.ops), eng, fn, dma)
        lastw, readers = self.lastw, self.readers
        deps = set()
        raw = set()
        for k in reads:
            w = lastw.get(k)
            if w is not None:
                deps.add(w)
                raw.add(w)
        for k in writes:
            w = lastw.get(k)
            if w is not None:
                deps.add(w)
            for r in readers.get(k, ()):
                deps.add(r)
        ops = self.ops
        for d in deps:
            p = ops[d]
            if p.dma is None and dma is None and p.eng == eng:
                if eng == "pe" or d not in raw:
                    continue
            op.deps.add(d)
        for k in reads:
            readers.setdefault(k, []).append(op.id)
        for k in writes:
            lastw[k] = op.id
            readers[k] = []
        ops.append(op)
        return op.id

    def emit(self, nc, stack):
        ops = self.ops
        for op in ops:
            for d in op.deps:
                ops[d].need_ev = True
        counters = {}
        semkeys = set()
        for op in ops:
            if not op.need_ev:
                continue
            if op.dma is not None:
                key, inc, ep = ("dma", op.dma), 16, EPOCH_DMA
            else:
                key, inc, ep = ("eng", op.eng), 1, EPOCH_ENG
            c = counters.get(key, 0)
            counters[key] = c + 1
            epoch, idx = divmod(c, ep)
            op.ev = (key, epoch, (idx + 1) * inc)
            semkeys.add((key, epoch))
        sems = {}
        for i, sk in enumerate(sorted(semkeys, key=str)):
            sems[sk] = stack.enter_context(nc.semaphore("s%d" % i))
        self.nsems = len(sems)
        by_eng = {}
        for op in ops:
            by_eng.setdefault(op.eng, []).append(op)

        def run(eng_name, eng):
            known = {}
            for op in by_eng.get(eng_name, ()):
                need = {}
                for d in op.deps:
                    key, epoch, val = ops[d].ev
                    kn = known.get(key)
                    if kn is not None and kn >= (epoch, val):
                        continue
                    cur = need.get(key)
                    if cur is None or cur < (epoch, val):
                        need[key] = (epoch, val)
                for key, (epoch, val) in need.items():
                    eng.wait_ge(sems[(key, epoch)], val)
                    known[key] = (epoch, val)
                inst = op.fn(eng)
                if op.need_ev:
                    key, epoch, val = op.ev
                    inst.then_inc(sems[(key, epoch)], 16 if op.dma is not None else 1)

        with nc.Block() as block:
            @block.sync
            def _(e):
                run("sp", e)

            @block.tensor
            def _(e):
                run("pe", e)

            @block.scalar
            def _(e):
                run("act", e)

            @block.vector
            def _(e):
                run("dve", e)

            @block.gpsimd
            def _(e):
                run("pool", e)


def build_program(NL, NT, final_norm=True):
    CW = 512
    NS = 7
    NTMP = 6
    R = CW // 128
    nc = bass.Bass("TRN2", target_bir_lowering=False)

    def din(name, shape, dt=F32):
        return nc.dram_tensor(name, shape, dt, kind="ExternalInput").ap()

    x_d = din("x", [NT * TT, D_MODEL])
    mem_d = din("mem", [256, D_MODEL])
    w_in_d = din("w_in", [NL, D_MODEL, IN_COLS])
    w_kv_d = din("w_kv", [NL, D_MODEL, 2 * W])
    w_br_d = din("w_branch", [NL, 4, W, D_MODEL])
    w_out_d = din("w_out", [NL, D_MODEL, D_MODEL])
    pool_w_d = din("pool_w", [NL, 4, 256, 256])
    colv_d = din("colv", [NL, 128, NCOLV])
    fing_d = din("fing", [128, 16])
    wsT_d = din("wsT", [NL, 128, 1024])
    rows3_d = din("rows3", [NL, 128, 3072])
    cst_d = din("cst", [128, 320])
    out_d = nc.dram_tensor("out", [NT * TT, D_MODEL], F32, kind="ExternalOutput").ap()
    kvs_d = nc.dram_tensor("kvs", [NL, 128, 4096], BF16).ap()
    dgs_d = nc.dram_tensor("dgs", [NL, 8, 128, 31 * 128], BF16).ap()

    S = Sched()
    off = [16512]

    def alloc(name, shape, dt, at=None):
        esz = 4 if dt == F32 else 2
        n = esz
        for s in shape[1:]:
            n *= s
        if at is None:
            at = off[0]
            off[0] += (n + 63) // 64 * 64
        return nc.alloc_sbuf_tensor_at(name, list(shape), dt, offset=at)

    xT = alloc("xT", [128, 16, TT], F32)
    kvstage = alloc("kvstage", [128, 4096], BF16, at=16512)
    hT_off = off[0]
    hT = alloc("hT", [128, 16, TT], BF16)
    memhT = alloc("memhT", [128, 16, 256], F32, at=hT_off)
    yT_off = off[0]
    yT = alloc("yT", [128, 32, TT], BF16)
    rows3 = alloc("rows3", [128, 3, 1024], F32, at=yT_off + 8 * 1024)
    wb = [alloc("wb%d" % i, [128, 8, CW], BF16) for i in range(NS)]
    A1_off = off[0]
    gv = alloc("gv", [128, 4, 1024], F32)
    cv = alloc("cv", [128, 8, TT], F32, at=A1_off)
    mT = alloc("mT", [128, 16, TT], BF16, at=A1_off)
    xin = alloc("xin", [128, 2, D_MODEL], F32, at=A1_off)
    A2_off = off[0]
    vn = alloc("vn", [128, 4, 1024], BF16)
    diff = alloc("diff", [128, 8, TT], BF16, at=A2_off)
    qT = alloc("qT", [128, 8, TT], BF16, at=A2_off)
    memnT = alloc("memnT", [128, 16, 256], BF16, at=A2_off)
    TMP = [alloc("tmp%d" % i, [128, 544], F32) for i in range(NTMP)]
    DD = [alloc("dd%d" % i, [128, TT], F32) for i in range(4)]
    expT = [alloc("expT%d" % i, [128, 2, TT], BF16) for i in range(2)]
    convh = alloc("convh", [128, NL, 8, 30], BF16)
    hbb = [alloc("hbb%d" % i, [128, 544], BF16) for i in range(4)]
    identb = alloc("identb", [128, 128], BF16)
    poolh = alloc("poolh", [128, NL, 8, 15], F32)
    colv = alloc("colv", [128, NL, NCOLV], F32)
    fing = alloc("fing", [128, 16], F32)
    wsTb = alloc("wsTb", [128, 8, 128], BF16)
    poolw = alloc("poolw", [128, 4, 2, 256], BF16)
    cst = alloc("cst", [128, 320], F32)
    onesb = alloc("onesb", [128, 128], BF16)
    onesf = alloc("onesf", [128, 128], F32)
    stat = alloc("stat", [128, 64], F32)
    assert off[0] <= 16512 + 212800, off[0]
    print("sbuf used", off[0] - 16512, "of 212863")
    PS = [nc.alloc_psum_tensor("ps%d" % i, [128, TT], F32) for i in range(8)]

    ident = cst[:, 0:128]
    maskT = cst[:, 128:256]

    bk = [0]
    held = set()

    def bank():
        while True:
            b = bk[0] % 8
            bk[0] += 1
            if b not in held:
                return b

    tm = [0]

    def tmp():
        i = tm[0] % NTMP
        tm[0] += 1
        return i

    slot_ctr = [0]

    pinned = set()

    def next_slot():
        while True:
            s = slot_ctr[0] % NS
            slot_ctr[0] += 1
            if s not in pinned:
                return s

    def wload_half(src, eng="pool", extra_reads=(), slot=None):
        if slot is None:
            s = next_slot()
        else:
            s = slot
        if len(src.shape) == 2:
            dst = wb[s][:].rearrange("p k c -> p (k c)")[:, 0:src.shape[1]]
        else:
            dst = wb[s][:]
        S.add(eng, lambda e: e.dma_start(out=dst, in_=src), reads=list(extra_reads), writes=[("wb", s)],
              dma="wb%d" % s)
        return s

    def wload(src, kdim=16, width=CW):
        if kdim == 8:
            return (wload_half(src),)
        return (wload_half(src[:, 0:8, :]), wload_half(src[:, 8:16, :]))

    def wsl(s, k, lo, hi):
        return wb[s[k // 8]][:, k % 8, lo:hi]

    def wkeys(s):
        return [("wb", i) for i in s]

    def mm(b_ap, pairs, reads, writes):
        def fn(e):
            n = len(pairs)
            r = None
            for i, (a, b) in enumerate(pairs):
                r = e.matmul(b_ap, a, b, start=(i == 0), stop=(i == n - 1))
            return r
        S.add("pe", fn, reads=reads, writes=writes)

    def act(out, in_, func, reads, writes, **kw):
        S.add("act", lambda e: e.activation(out=out, in_=in_, func=func, **kw),
              reads=list(reads) + ["cst"], writes=writes)

    def dve(fn, reads, writes):
        S.add("dve", fn, reads=reads, writes=writes)

    def A1k(lo, hi):
        return [("A1", i) for i in range(lo, hi)]

    def A2k(lo, hi):
        return [("A2", i) for i in range(lo, hi)]

    HT_ALL = [("hT", k) for k in range(16)]
    XT_ALL = [("xT", k) for k in range(16)]

    def w_in_v(l):
        return w_in_d[l].rearrange("(k p) c -> p k c", p=128)

    S.add("sp", lambda e: e.dma_start(out=cst[:], in_=cst_d), writes=["cst"], dma="c0")
    S.add("sp", lambda e: e.dma_start(out=colv[:], in_=colv_d.rearrange("l p c -> p l c")),
          writes=["colv"], dma="c1")
    S.add("sp", lambda e: e.dma_start(out=fing[:], in_=fing_d), writes=["fing"], dma="c2")
    dve(lambda e: e.memset(onesf[:], 1.0), [], ["onesf"])
    dve(lambda e: e.memset(onesb[:], 1.0), [], ["onesb"])
    dve(lambda e: e.memset(convh[:], 0.0), [], ["convh"])
    dve(lambda e: e.memset(poolh[:], 0.0), [], ["poolh"])
    dve(lambda e: e.memset(stat[:], 0.0), [], [("stat", i) for i in range(4)])

    dve(lambda e: e.tensor_copy(out=identb[:], in_=ident), ["cst"], ["identb"])
    stg_ctr = [0]

    def build_taps(l):
        for c in range(8):
            j = stg_ctr[0] % 4
            stg_ctr[0] += 1
            dgv = yT[:, 8 * j:8 * j + 8, :].rearrange("p k c -> p (k c)")
            keys = [("yT", 8 * j + i) for i in range(8)]
            for kk in range(31):
                dve(lambda e, dgv=dgv, kk=kk, l=l, c=c: e.tensor_scalar(
                    out=dgv[:, kk * 128:(kk + 1) * 128], in0=identb[:],
                    scalar1=colv[:, l, CV_CW + c * 31 + kk:CV_CW + c * 31 + kk + 1], scalar2=None,
                    op0=ALU.mult), ["identb", "colv"], keys)
            S.add("sp", lambda e, dgv=dgv, l=l, c=c: e.dma_start(out=dgs_d[l, c], in_=dgv[:, 0:31 * 128]),
                  reads=keys, writes=[("dgs", l, c)], dma="dgs%d" % j)

    for mb in range(2):
        S.add("sp", lambda e, mb=mb: e.dma_start(out=xin[:, mb, :], in_=mem_d[mb * 128:(mb + 1) * 128, :]),
              writes=A1k(4 * mb, 4 * mb + 4), dma="xin%d" % mb)
        ti = tmp()
        for hh in range(4):
            act(TMP[ti][:, 0:512], xin[:, mb, hh * 512:(hh + 1) * 512], AF.Square,
                A1k(4 * mb, 4 * mb + 4), [("tmp", ti), ("stat", hh)],
                accum_out=stat[:, hh * 16 + mb:hh * 16 + mb + 1])
        c0 = 8 + mb
        dve(lambda e, mb=mb, c0=c0: e.tensor_tensor(out=stat[:, c0:c0 + 1], in0=stat[:, mb:mb + 1],
                                                    in1=stat[:, 16 + mb:17 + mb], op=ALU.add),
            [("stat", 0), ("stat", 1)], [("stat", "m")])
        dve(lambda e, mb=mb, c0=c0: e.tensor_tensor(out=stat[:, c0 + 2:c0 + 3], in0=stat[:, 32 + mb:33 + mb],
                                                    in1=stat[:, 48 + mb:49 + mb], op=ALU.add),
            [("stat", 2), ("stat", 3)], [("stat", "m2")])
        dve(lambda e, c0=c0: e.tensor_tensor(out=stat[:, c0:c0 + 1], in0=stat[:, c0:c0 + 1],
                                             in1=stat[:, c0 + 2:c0 + 3], op=ALU.add),
            [("stat", "m"), ("stat", "m2")], [("stat", "m3")])
        act(stat[:, c0 + 4:c0 + 5], stat[:, c0:c0 + 1], AF.Sqrt, [("stat", "m3")], [("stat", "m4")],
            scale=1.0 / D_MODEL, bias=cst[:, 319:320])
        dve(lambda e, c0=c0: e.reciprocal(out=stat[:, c0 + 6:c0 + 7], in_=stat[:, c0 + 4:c0 + 5]),
            [("stat", "m4")], [("stat", "m5")])
        dve(lambda e, mb=mb, c0=c0: e.tensor_scalar(out=xin[:, mb, :], in0=xin[:, mb, :],
                                                    scalar1=stat[:, c0 + 6:c0 + 7], scalar2=None, op0=ALU.mult),
            [("stat", "m5")] + A1k(4 * mb, 4 * mb + 4), A1k(4 * mb, 4 * mb + 4))
        for k4 in range(4):
            b = bank()

            def tr(e, mb=mb, k4=k4, b=b):
                r = None
                for j in range(4):
                    k = k4 * 4 + j
                    r = e.transpose(out=PS[b][:, j * 128:(j + 1) * 128], in_=xin[:, mb, k * 128:(k + 1) * 128],
                                    identity=ident)
                return r
            S.add("pe", tr, reads=A1k(4 * mb, 4 * mb + 4) + ["cst"], writes=[("ps", b)])
            act(memhT[:, k4 * 4:k4 * 4 + 4, mb * 128:(mb + 1) * 128],
                PS[b][:].rearrange("p (j t) -> p j t", j=4), AF.Copy, [("ps", b)], HT_ALL)
    for l in range(NL):
        for k in range(16):
            dve(lambda e, l=l, k=k: e.tensor_scalar(out=memnT[:, k, :], in0=memhT[:, k, :],
                                                    scalar1=colv[:, l, CV_MG + k:CV_MG + k + 1], scalar2=None,
                                                    op0=ALU.mult),
                HT_ALL + ["colv"], A2k(0, 4))
        wkv = w_kv_d[l].rearrange("(k p) c -> p k c", p=128)
        kvb = kvstage[:]
        KVK = XT_ALL
        for half in range(2):
            s = wload(wkv[:, :, half * CW:(half + 1) * CW])
            b = bank()
            for r in range(R):
                fb = half * R + r
                pairs = [(wsl(s, k, r * 128, (r + 1) * 128), memnT[:, k, :]) for k in range(16)]
                if r % 2 == 0:
                    b = bank()
                mm(PS[b][:, (r % 2) * 256:(r % 2) * 256 + 256], pairs, wkeys(s) + A2k(0, 4), [("ps", b)])
                if r % 2 == 1:
                    act(kvb[:, (fb - 1) * 256:(fb + 1) * 256], PS[b][:], AF.Copy, [("ps", b)], KVK)
        for half in range(2):
            s = wload(wkv[:, :, W + half * CW:W + (half + 1) * CW])
            for mb in range(2):
                b = bank()
                pairs = [(memnT[:, k, mb * 128:(mb + 1) * 128], wsl(s, k, 0, CW)) for k in range(16)]
                mm(PS[b][:], pairs, wkeys(s) + A2k(0, 4), [("ps", b)])
                o0 = 2048 + mb * 1024 + half * 512
                act(kvb[:, o0:o0 + 512], PS[b][:], AF.Copy, [("ps", b)], KVK)
        S.add("sp", lambda e, l=l, kvb=kvb: e.dma_start(out=kvs_d[l], in_=kvb), reads=KVK, writes=[("kvs", l)],
              dma="kvs")
        build_taps(l)

    def rms_tail(b, dst):
        ti = tmp()
        act(TMP[ti][:, 0:512], PS[b][:], AF.Sqrt, [("ps", b)], [("tmp", ti)],
            scale=1.0 / D_MODEL, bias=cst[:, 319:320])
        dve(lambda e, ti=ti: e.reciprocal(out=dst, in_=TMP[ti][:, 0:512]), [("tmp", ti)], [("D", 0)])

    def hilo(src_ap, src_keys):
        th = tmp()
        hv = TMP[th][:].bitcast(BF16)
        dve(lambda e, hv=hv: e.tensor_copy(out=hv[:, 0:512], in_=src_ap), list(src_keys), [("tmp", th)])
        dve(lambda e, hv=hv: e.tensor_tensor(out=hv[:, 512:1024], in0=src_ap, in1=hv[:, 0:512],
                                             op=ALU.subtract), list(src_keys) + [("tmp", th)], [("tmp", th)])
        return th

    def stat_mm(b, th, first, last):
        hv = TMP[th][:].bitcast(BF16)

        def fn(e):
            e.matmul(PS[b][:], onesb[:], hv[:, 0:512], start=first, stop=False)
            return e.matmul(PS[b][:], onesb[:], hv[:, 512:1024], start=False, stop=last)
        S.add("pe", fn, reads=[("tmp", th), "onesb"], writes=[("ps", b)])

    def rms_sq(k):
        ti = tmp()
        act(TMP[ti][:, 0:512], xT[:, k, :], AF.Square, [("xT", k)], [("tmp", ti)])
        return hilo(TMP[ti][:, 0:512], [("tmp", ti)])

    def rms_acc(b, k, th):
        stat_mm(b, th, k == 0, k == 15)

    def rms_stats(dst):
        b = bank()
        for k in range(16):
            rms_acc(b, k, rms_sq(k))
        rms_tail(b, dst)

    pend_rms = [None]

    for t in range(NT):
        for tb in range(4):
            xi = tb % 2
            r0 = t * TT + tb * 128
            S.add("sp", lambda e, xi=xi, r0=r0: e.dma_start(out=xin[:, xi, :], in_=x_d[r0:r0 + 128, :]),
                  writes=A1k(4 * xi, 4 * xi + 4), dma="xin%d" % xi)
            for k4 in range(4):
                b = bank()

                def tr(e, xi=xi, k4=k4, b=b):
                    r = None
                    for j in range(4):
                        k = k4 * 4 + j
                        r = e.transpose(out=PS[b][:, j * 128:(j + 1) * 128],
                                        in_=xin[:, xi, k * 128:(k + 1) * 128], identity=ident)
                    return r
                S.add("pe", tr, reads=A1k(4 * xi, 4 * xi + 4) + ["cst"], writes=[("ps", b)])
                dve(lambda e, k4=k4, tb=tb, b=b: e.tensor_copy(
                    out=xT[:, k4 * 4:k4 * 4 + 4, tb * 128:(tb + 1) * 128],
                    in_=PS[b][:].rearrange("p (j t) -> p j t", j=4)),
                    [("ps", b)], [("xT", k4 * 4 + j) for j in range(4)])

        for l in range(NL):
            wv = w_in_v(l)

            def cv_col(c0, l=l):
                return colv[:, l, c0:c0 + 1]

            if pend_rms[0] is None:
                rms_stats(DD[0][:])
            else:
                rms_tail(pend_rms[0], DD[0][:])
                held.discard(pend_rms[0])
                pend_rms[0] = None
            for k in range(16):
                dve(lambda e, k=k, l=l: e.scalar_tensor_tensor(
                    out=hT[:, k, :], in0=xT[:, k, :], scalar=colv[:, l, CV_NG + k:CV_NG + k + 1],
                    in1=DD[0][:], op0=ALU.mult, op1=ALU.mult),
                    [("xT", k), ("D", 0), "colv"], [("hT", k)])

            def proj_block(s, r, b):
                pairs = [(wsl(s, k, r * 128, (r + 1) * 128), hT[:, k, :]) for k in range(16)]
                mm(PS[b][:], pairs, wkeys(s) + HT_ALL, [("ps", b)])

            S.add("sp", lambda e, l=l: e.dma_start(out=rows3[:], in_=rows3_d[l].rearrange("p (a b) -> p a b", a=3)),
                  writes=[("yT", c) for c in range(8, 20)], dma="rows3")
            kvslot = wload_half(kvs_d[l], eng="sp", extra_reads=[("kvs", l)])
            kvb = wb[kvslot][:].rearrange("p k c -> p (k c)")
            KT = kvb[:, 0:2048].rearrange("p (c m) -> p c m", c=8)
            VV = kvb[:, 2048:4096].rearrange("p (b f) -> p b f", b=2)
            KVK = [("wb", kvslot)]
            pinned.add(kvslot)

            sq = [wload(wv[:, :, M_Q + i * CW:M_Q + (i + 1) * CW]) for i in range(2)]
            for c in range(8):
                b = bank()
                proj_block(sq[c // R], c % R, b)
                act(qT[:, c, :], PS[b][:], AF.Copy, [("ps", b)], A2k(c // 2, c // 2 + 1))
            sg = [None]

            def m_scores(h):
                eb = h % 2
                for mb in range(2):
                    b = bank()
                    pairs = [(KT[:, 2 * h + dc, mb * 128:(mb + 1) * 128], qT[:, 2 * h + dc, :]) for dc in range(2)]
                    mm(PS[b][:], pairs, KVK + A2k(h, h + 1), [("ps", b)])
                    act(expT[eb][:, mb, :], PS[b][:], AF.Exp, [("ps", b)], [("expT", eb)], scale=0.0625)

            def m_tail(h, l=l, wv=wv):
                eb = h % 2
                b = bank()
                mm(PS[b][:], [(onesb[:], expT[eb][:, mb, :]) for mb in range(2)], [("expT", eb), "onesb"],
                   [("ps", b)])
                di = 1 + eb
                dve(lambda e, b=b, di=di: e.reciprocal(out=DD[di][:], in_=PS[b][:]), [("ps", b)], [("D", di)])
                for dblk in range(2):
                    c = 2 * h + dblk
                    if c % R == 0:
                        sg[0] = wload(wv[:, :, M_G + (c // R) * CW:M_G + (c // R + 1) * CW])
                    bo = bank()
                    pairs = [(VV[:, mb, c * 128:(c + 1) * 128], expT[eb][:, mb, :]) for mb in range(2)]
                    mm(PS[bo][:], pairs, KVK + [("expT", eb)], [("ps", bo)])
                    bg = bank()
                    proj_block(sg[0], c % R, bg)
                    t2 = tmp()
                    act(TMP[t2][:, 0:512], PS[bg][:], AF.Silu, [("ps", bg)], [("tmp", t2)])
                    dve(lambda e, t2=t2, di=di: e.tensor_tensor(out=TMP[t2][:, 0:512], in0=TMP[t2][:, 0:512],
                                                                in1=DD[di][:], op=ALU.mult),
                        [("tmp", t2), ("D", di)], [("tmp", t2)])
                    dve(lambda e, t2=t2, bo=bo, c=c: e.tensor_tensor(out=yT[:, 24 + c, :], in0=PS[bo][:],
                                                                     in1=TMP[t2][:, 0:512], op=ALU.mult),
                        [("tmp", t2), ("ps", bo)], [("yT", 24 + c)])

            for h in range(4):
                m_scores(h)
                if h > 0:
                    m_tail(h - 1)
            m_tail(3)
            pinned.discard(kvslot)

            for hh in range(2):
                ti = tmp()
                S.add("sp", lambda e, l=l, hh=hh, ti=ti: e.dma_start(out=TMP[ti][:, 0:512],
                                                                   in_=wsT_d[l, :, hh * 512:(hh + 1) * 512]),
                      writes=[("tmp", ti)], dma="wsT%d" % hh)
                for g4 in range(4):
                    g = hh * 4 + g4
                    dve(lambda e, ti=ti, g=g, g4=g4: e.tensor_tensor(out=wsTb[:, g, :],
                                                                      in0=TMP[ti][:, g4 * 128:(g4 + 1) * 128],
                                                                      in1=maskT, op=ALU.mult),
                        [("tmp", ti), "cst"], [("wsTb", g)])
            sv = [wload(wv[:, :, A_V + i * CW:A_V + (i + 1) * CW]) for i in range(2)]
            for tb in range(4):
                for half in range(2):
                    b = bank()
                    pairs = [(hT[:, k, tb * 128:(tb + 1) * 128], wsl(sv[half], k, 0, CW)) for k in range(16)]
                    mm(PS[b][:], pairs, wkeys(sv[half]) + HT_ALL, [("ps", b)])
                    act(gv[:, tb, half * 512:(half + 1) * 512], PS[b][:], AF.Gelu_apprx_tanh,
                        [("ps", b)], [("A1", tb * 2 + half)])
            RY = [("yT", c) for c in range(8, 20)]
            for tb in range(4):
                sc = tb * 16
                gk = A1k(tb * 2, tb * 2 + 2)
                dve(lambda e, tb=tb, sc=sc: e.bn_stats(out=stat[:, sc:sc + 6], in_=gv[:, tb, 0:512]),
                    gk, [("stat", tb)])
                dve(lambda e, tb=tb, sc=sc: e.bn_stats(out=stat[:, sc + 6:sc + 12], in_=gv[:, tb, 512:1024]),
                    gk, [("stat", tb, 1)])
                dve(lambda e, sc=sc: e.bn_aggr(out=stat[:, sc + 12:sc + 14], in_=stat[:, sc:sc + 12]),
                    [("stat", tb), ("stat", tb, 1)], [("stat", tb, 2)])
                act(stat[:, sc + 14:sc + 15], stat[:, sc + 13:sc + 14], AF.Sqrt, [("stat", tb, 2)],
                    [("stat", tb, 3)], scale=1.0, bias=cst[:, 319:320])
                dve(lambda e, sc=sc: e.reciprocal(out=stat[:, sc + 15:sc + 16], in_=stat[:, sc + 14:sc + 15]),
                    [("stat", tb, 3)], [("stat", tb, 4)])
                dve(lambda e, tb=tb, sc=sc: e.tensor_scalar(out=gv[:, tb, :], in0=gv[:, tb, :],
                                                            scalar1=stat[:, sc + 12:sc + 13],
                                                            scalar2=stat[:, sc + 15:sc + 16],
                                                            op0=ALU.subtract, op1=ALU.mult),
                    gk + [("stat", tb, 2), ("stat", tb, 4)], gk)
                dve(lambda e, tb=tb: e.tensor_tensor(out=gv[:, tb, :], in0=gv[:, tb, :], in1=rows3[:, 0, :],
                                                     op=ALU.mult), gk + RY, gk)
                dve(lambda e, tb=tb: e.tensor_tensor(out=vn[:, tb, :], in0=gv[:, tb, :], in1=rows3[:, 1, :],
                                                     op=ALU.add), gk + RY, A2k(tb, tb + 1))
            su = [None]
            sgA = [None]
            t1s = {}

            def a_stage1(c, l=l, wv=wv):
                if c % R == 0:
                    su[0] = wload(wv[:, :, A_U + (c // R) * CW:A_U + (c // R + 1) * CW])
                    sgA[0] = wload(wv[:, :, A_G + (c // R) * CW:A_G + (c // R + 1) * CW])
                bu = bank()
                proj_block(su[0], c % R, bu)
                t1 = tmp()
                act(TMP[t1][:, 0:512], PS[bu][:], AF.Gelu_apprx_tanh, [("ps", bu)], [("tmp", t1)])
                bg = bank()
                proj_block(sgA[0], c % R, bg)
                t2 = tmp()
                act(TMP[t2][:, 0:512], PS[bg][:], AF.Silu, [("ps", bg)], [("tmp", t2)])
                dve(lambda e, t1=t1, t2=t2: e.tensor_tensor(out=TMP[t1][:, 0:512], in0=TMP[t1][:, 0:512],
                                                            in1=TMP[t2][:, 0:512], op=ALU.mult),
                    [("tmp", t1), ("tmp", t2)], [("tmp", t1)])
                t1s[c] = t1

            def a_stage2(c):
                t1 = t1s[c]
                bs_ = bank()

                def sp_mm(e, c=c, bs_=bs_):
                    r = None
                    for tb in range(4):
                        r = e.matmul(PS[bs_][:, tb * 128:(tb + 1) * 128], vn[:, tb, c * 128:(c + 1) * 128],
                                     wsTb[:, c, :], start=True, stop=True)
                    return r
                S.add("pe", sp_mm, reads=A2k(0, 4) + [("wsTb", c)], writes=[("ps", bs_)])
                t3 = tmp()
                for tb in range(4):
                    dve(lambda e, t3=t3, bs_=bs_, c=c, tb=tb: e.tensor_tensor(
                        out=TMP[t3][:, tb * 128:(tb + 1) * 128], in0=PS[bs_][:, tb * 128:(tb + 1) * 128],
                        in1=rows3[:, 2, c * 128:(c + 1) * 128], op=ALU.add),
                        [("ps", bs_)] + RY, [("tmp", t3)])
                dve(lambda e, t1=t1, t3=t3, c=c: e.tensor_tensor(out=yT[:, c, :], in0=TMP[t3][:, 0:512],
                                                                 in1=TMP[t1][:, 0:512], op=ALU.mult),
                    [("tmp", t1), ("tmp", t3)], [("yT", c)])

            for c in range(8):
                a_stage1(c)
                if c > 0:
                    a_stage2(c - 1)
            a_stage2(7)

            for g0 in (0, 4):
                sa = wload(wv[:, :, B_A + (g0 // R) * CW:B_A + (g0 // R + 1) * CW])
                sb = wload(wv[:, :, B_B + (g0 // R) * CW:B_B + (g0 // R + 1) * CW])
                dsl = (next_slot(), next_slot())
                for c in range(g0, g0 + 4):
                    ba = bank()
                    proj_block(sa, c % R, ba)
                    bb = bank()
                    proj_block(sb, c % R, bb)
                    t1 = tmp()
                    act(TMP[t1][:, 0:512], PS[bb][:], AF.Sigmoid, [("ps", bb)], [("tmp", t1)])
                    hi = c % 4
                    act(hbb[hi][:, 0:30], convh[:, l, c, :], AF.Copy, ["convh"], [("hbb", hi)])
                    dve(lambda e, hi=hi, ba=ba, t1=t1: e.tensor_tensor(out=hbb[hi][:, 30:542], in0=PS[ba][:],
                                                                       in1=TMP[t1][:, 0:512], op=ALU.mult),
                        [("ps", ba), ("tmp", t1)], [("hbb", hi)])
                    act(convh[:, l, c, :], hbb[hi][:, 512:542], AF.Copy, [("hbb", hi)], ["convh"])
                for c in range(g0, g0 + 4):
                    hi = c % 4
                    sd = wload_half(dgs_d[l, c], eng="sp", extra_reads=[("dgs", l, c)], slot=dsl[c % 2])
                    bc = bank()
                    dgv = wb[sd][:].rearrange("p k c -> p (k c)")
                    pairs = [(dgv[:, kk * 128:(kk + 1) * 128], hbb[hi][:, kk:kk + 512]) for kk in range(31)]
                    mm(PS[bc][:], pairs, [("wb", sd), ("hbb", hi)], [("ps", bc)])
                    dve(lambda e, c=c, bc=bc, l=l: e.tensor_scalar(out=cv[:, c, :], in0=PS[bc][:],
                                                                   scalar1=colv[:, l, CV_CB + c:CV_CB + c + 1],
                                                                   scalar2=None, op0=ALU.add),
                        [("ps", bc), "colv"], [("A1", c)])
            b1 = bank()
            held.add(b1)
            b2 = bank()
            held.add(b2)
            for c in range(8):
                th = hilo(cv[:, c, :], [("A1", c)])
                stat_mm(b1, th, c == 0, c == 7)
                ti = tmp()
                act(TMP[ti][:, 0:512], cv[:, c, :], AF.Square, [("A1", c)], [("tmp", ti)])
                th2 = hilo(TMP[ti][:, 0:512], [("tmp", ti)])
                stat_mm(b2, th2, c == 0, c == 7)
            held.discard(b1)
            held.discard(b2)
            dve(lambda e, b1=b1: e.tensor_scalar(out=DD[0][:], in0=PS[b1][:], scalar1=1.0 / W, scalar2=None,
                                                 op0=ALU.mult), [("ps", b1)], [("D", 0)])
            tq = tmp()
            dve(lambda e, tq=tq: e.tensor_tensor(out=TMP[tq][:, 0:512], in0=DD[0][:], in1=DD[0][:], op=ALU.mult),
                [("D", 0)], [("tmp", tq)])
            dve(lambda e, tq=tq, b2=b2: e.scalar_tensor_tensor(out=TMP[tq][:, 0:512], in0=PS[b2][:],
                                                               scalar=1.0 / W, in1=TMP[tq][:, 0:512],
                                                               op0=ALU.mult, op1=ALU.subtract),
                [("ps", b2), ("tmp", tq)], [("tmp", tq)])
            act(TMP[tq][:, 0:512], TMP[tq][:, 0:512], AF.Sqrt, [("tmp", tq)], [("tmp", tq)],
                scale=1.0, bias=cst[:, 319:320])
            dve(lambda e, tq=tq: e.reciprocal(out=DD[1][:], in_=TMP[tq][:, 0:512]), [("tmp", tq)], [("D", 1)])
            sgB = [None]
            for c in range(8):
                if c % R == 0:
                    sgB[0] = wload(wv[:, :, B_G + (c // R) * CW:B_G + (c // R + 1) * CW])
                t1 = tmp()
                dve(lambda e, t1=t1, c=c: e.tensor_tensor(out=TMP[t1][:, 0:512], in0=cv[:, c, :], in1=DD[0][:],
                                                          op=ALU.subtract), [("A1", c), ("D", 0)], [("tmp", t1)])
                dve(lambda e, t1=t1: e.tensor_tensor(out=TMP[t1][:, 0:512], in0=TMP[t1][:, 0:512], in1=DD[1][:],
                                                     op=ALU.mult), [("tmp", t1), ("D", 1)], [("tmp", t1)])
                act(TMP[t1][:, 0:512], TMP[t1][:, 0:512], AF.Silu, [("tmp", t1), "colv"], [("tmp", t1)],
                    scale=cv_col(CV_CLG + c), bias=cv_col(CV_CLB + c))
                bg = bank()
                proj_block(sgB[0], c % R, bg)
                t2 = tmp()
                act(TMP[t2][:, 0:512], PS[bg][:], AF.Silu, [("ps", bg)], [("tmp", t2)])
                dve(lambda e, t1=t1, t2=t2, c=c: e.tensor_tensor(out=yT[:, 8 + c, :], in0=TMP[t1][:, 0:512],
                                                                 in1=TMP[t2][:, 0:512], op=ALU.mult),
                    [("tmp", t1), ("tmp", t2)], [("yT", 8 + c)])

            S.add("pool", lambda e, l=l: e.dma_start(
                out=poolw[:], in_=pool_w_d[l].rearrange("g (k p) d -> p g k d", p=128)),
                writes=["poolw"], dma="poolw")
            sc_ = [None]
            for c in range(8):
                if c % R == 0:
                    sc_[0] = wload(wv[:, :, C_IN + (c // R) * CW:C_IN + (c // R + 1) * CW])
                b = bank()
                proj_block(sc_[0], c % R, b)
                hb = tmp()
                act(TMP[hb][:, 0:15], poolh[:, l, c, :], AF.Copy, ["poolh"], [("tmp", hb)])
                act(TMP[hb][:, 15:527], PS[b][:], AF.Copy, [("ps", b)], [("tmp", hb)])
                act(poolh[:, l, c, :], TMP[hb][:, 512:527], AF.Copy, [("tmp", hb)], ["poolh"])
                wi = c // 2
                win = POOL_WINDOWS[wi]
                src = hb
                lo = 0
                sh = 1
                while sh < win:
                    dst = tmp()
                    lo2 = lo + sh
                    dve(lambda e, src=src, dst=dst, lo2=lo2, sh=sh: e.tensor_tensor(
                        out=TMP[dst][:, lo2:527], in0=TMP[src][:, lo2:527], in1=TMP[src][:, lo2 - sh:527 - sh],
                        op=ALU.add), [("tmp", src)], [("tmp", dst)])
                    src = dst
                    lo = lo2
                    sh *= 2
                dve(lambda e, src=src, hb=hb, c=c, win=win: e.scalar_tensor_tensor(
                    out=diff[:, c, :], in0=TMP[src][:, 15:527], scalar=1.0 / win, in1=TMP[hb][:, 15:527],
                    op0=ALU.mult, op1=ALU.subtract), [("tmp", src), ("tmp", hb)], A2k(c // 2, c // 2 + 1))
                if t == 0:
                    t5 = tmp()
                    dve(lambda e, src=src, t5=t5, wi=wi: e.tensor_tensor(
                        out=TMP[t5][:, 0:15], in0=TMP[src][:, 15:30], in1=cst[:, 256 + wi * 16:256 + wi * 16 + 15],
                        op=ALU.mult), [("tmp", src), "cst"], [("tmp", t5)])
                    dve(lambda e, t5=t5, hb=hb, c=c: e.tensor_tensor(
                        out=diff[:, c, 0:15], in0=TMP[t5][:, 0:15], in1=TMP[hb][:, 15:30], op=ALU.subtract),
                        [("tmp", t5), ("tmp", hb)] + A2k(c // 2, c // 2 + 1), A2k(c // 2, c // 2 + 1))
            sgC = [None]
            for c in range(8):
                if c % R == 0:
                    sgC[0] = wload(wv[:, :, C_G + (c // R) * CW:C_G + (c // R + 1) * CW])
                g, db = c // 2, c % 2
                bp = bank()
                pairs = [(poolw[:, g, k, db * 128:(db + 1) * 128], diff[:, 2 * g + k, :]) for k in range(2)]
                mm(PS[bp][:], pairs, ["poolw"] + A2k(g, g + 1), [("ps", bp)])
                bg = bank()
                proj_block(sgC[0], c % R, bg)
                t2 = tmp()
                act(TMP[t2][:, 0:512], PS[bg][:], AF.Silu, [("ps", bg)], [("tmp", t2)])
                dve(lambda e, t2=t2, bp=bp, c=c, l=l: e.scalar_tensor_tensor(
                    out=yT[:, 16 + c, :], in0=PS[bp][:], scalar=colv[:, l, CV_PS + c:CV_PS + c + 1],
                    in1=TMP[t2][:, 0:512], op0=ALU.mult, op1=ALU.mult),
                    [("ps", bp), ("tmp", t2), "colv"], [("yT", 16 + c)])

            for dg in range(D_MODEL // CW):
                for n in range(4):
                    sgt = wload(wv[:, :, GATES + n * D_MODEL + dg * CW:GATES + n * D_MODEL + (dg + 1) * CW])
                    sbr = wload(w_br_d[l, n].rearrange("(k p) c -> p k c", p=128)[:, :, dg * CW:(dg + 1) * CW],
                                kdim=8)
                    for r in range(R):
                        bp = bank()
                        pairs = [(wsl(sbr, k, r * 128, (r + 1) * 128), yT[:, n * 8 + k, :]) for k in range(8)]
                        mm(PS[bp][:], pairs, wkeys(sbr) + [("yT", n * 8 + k) for k in range(8)], [("ps", bp)])
                        bg = bank()
                        proj_block(sgt, r, bg)
                        t2 = tmp()
                        act(TMP[t2][:, 0:512], PS[bg][:], AF.Sigmoid, [("ps", bg)], [("tmp", t2)])
                        mi = dg * R + r
                        if n == 0:
                            dve(lambda e, r=r, bp=bp, t2=t2: e.tensor_tensor(out=DD[r][:], in0=PS[bp][:],
                                                                             in1=TMP[t2][:, 0:512], op=ALU.mult),
                                [("ps", bp), ("tmp", t2)], [("D", r)])
                        else:
                            dve(lambda e, bp=bp, t2=t2: e.tensor_tensor(out=TMP[t2][:, 0:512], in0=PS[bp][:],
                                                                        in1=TMP[t2][:, 0:512], op=ALU.mult),
                                [("ps", bp), ("tmp", t2)], [("tmp", t2)])
                            if n < 3:
                                dve(lambda e, r=r, t2=t2: e.tensor_tensor(out=DD[r][:], in0=DD[r][:],
                                                                          in1=TMP[t2][:, 0:512], op=ALU.add),
                                    [("D", r), ("tmp", t2)], [("D", r)])
                            else:
                                dve(lambda e, r=r, t2=t2, mi=mi: e.tensor_tensor(out=mT[:, mi, :], in0=DD[r][:],
                                                                                 in1=TMP[t2][:, 0:512],
                                                                                 op=ALU.add),
                                    [("D", r), ("tmp", t2)], [("A1", mi // 2)])

            wo = w_out_d[l].rearrange("(k p) c -> p k c", p=128)
            so = [None]
            rb = bank()
            held.add(rb)
            pend_rms[0] = rb
            sqq = []
            for eb_ in range(16):
                if eb_ % R == 0:
                    so[0] = wload(wo[:, :, (eb_ // R) * CW:(eb_ // R + 1) * CW])
                b = bank()
                pairs = [(wsl(so[0], k, (eb_ % R) * 128, (eb_ % R + 1) * 128), mT[:, k, :]) for k in range(16)]
                mm(PS[b][:], pairs, wkeys(so[0]) + A1k(0, 8), [("ps", b)])
                dve(lambda e, eb_=eb_, b=b: e.tensor_tensor(out=xT[:, eb_, :], in0=PS[b][:], in1=xT[:, eb_, :],
                                                            op=ALU.add), [("ps", b), ("xT", eb_)], [("xT", eb_)])
                sqq.append((eb_, rms_sq(eb_)))
                if len(sqq) > 2:
                    k_, ti_ = sqq.pop(0)
                    rms_acc(rb, k_, ti_)
            for k_, ti_ in sqq:
                rms_acc(rb, k_, ti_)
            if not final_norm and l == NL - 1:
                held.discard(rb)
                pend_rms[0] = None

        if final_norm:
            rms_tail(pend_rms[0], DD[0][:])
            held.discard(pend_rms[0])
            pend_rms[0] = None
            for k in range(16):
                dve(lambda e, k=k: e.scalar_tensor_tensor(out=xT[:, k, :], in0=xT[:, k, :],
                                                          scalar=fing[:, k:k + 1], in1=DD[0][:],
                                                          op0=ALU.mult, op1=ALU.mult),
                    [("xT", k), ("D", 0), "fing"], [("xT", k)])
        for tb in range(4):
            xi = tb % 2
            for k4 in range(4):
                b = bank()

                def tr2(e, tb=tb, k4=k4, b=b):
                    r = None
                    for j in range(4):
                        k = k4 * 4 + j
                        r = e.transpose(out=PS[b][:, j * 128:(j + 1) * 128],
                                        in_=xT[:, k, tb * 128:(tb + 1) * 128], identity=ident)
                    return r
                S.add("pe", tr2, reads=[("xT", k4 * 4 + j) for j in range(4)] + ["cst"], writes=[("ps", b)])
                dve(lambda e, xi=xi, k4=k4, b=b: e.tensor_copy(out=xin[:, xi, k4 * 512:(k4 + 1) * 512],
                                                               in_=PS[b][:]),
                    [("ps", b)], A1k(4 * xi + k4, 4 * xi + k4 + 1))
            r0 = t * TT + tb * 128
            S.add("sp", lambda e, xi=xi, r0=r0: e.dma_start(out=out_d[r0:r0 + 128, :], in_=xin[:, xi, :]),
                  reads=A1k(4 * xi, 4 * xi + 4), writes=[("out", t, tb)], dma="out%d" % xi)
    S.add("sp", lambda e: e.nop(), reads=[("out", t, tb) for t in range(NT) for tb in range(4)])
    return nc, S


def host_layout(NL, norm_g, gmlp_ln_g, gmlp_ln_b, gmlp_ws, gmlp_bs, conv_w, conv_b, conv_ln_g, conv_ln_b,
                pool_scale, mem_norm_g, final_g):
    f = np.float32

    def cols(v, n):
        return np.asarray(v, f).reshape(n, 128).T

    colv = np.zeros((NL, 128, NCOLV), f)
    for l in range(NL):
        colv[l, :, CV_NG:CV_NG + 16] = cols(norm_g[l], 16)
        colv[l, :, CV_MG:CV_MG + 16] = cols(mem_norm_g[l], 16)
        colv[l, :, CV_CB:CV_CB + 8] = cols(conv_b[l], 8)
        colv[l, :, CV_CLG:CV_CLG + 8] = cols(conv_ln_g[l], 8)
        colv[l, :, CV_CLB:CV_CLB + 8] = cols(conv_ln_b[l], 8)
        colv[l, :, CV_PS:CV_PS + 8] = cols(pool_scale[l], 8)
        cw = np.asarray(conv_w[l], f)
        colv[l, :, CV_CW:CV_CW + 248] = cw.reshape(31, 8, 128).transpose(2, 1, 0).reshape(128, 248)
    fing = cols(final_g, 16).copy()
    wsT = np.ascontiguousarray(np.asarray(gmlp_ws[:NL], f).transpose(0, 3, 1, 2).reshape(NL, 128, 1024))
    rows3 = np.zeros((NL, 128, 3072), f)
    for l in range(NL):
        rows3[l, :, 0:1024] = np.asarray(gmlp_ln_g[l], f)[None, :]
        rows3[l, :, 1024:2048] = np.asarray(gmlp_ln_b[l], f)[None, :]
        rows3[l, :, 2048:3072] = np.asarray(gmlp_bs[l], f).reshape(1, 1024)
    cst = np.zeros((128, 320), f)
    cst[:, 0:128] = np.eye(128, dtype=f)
    s_idx = np.arange(128)[:, None]
    t_idx = np.arange(128)[None, :]
    cst[:, 128:256] = (s_idx <= t_idx).astype(f)
    for wi, win in enumerate(POOL_WINDOWS):
        cst[:, 256 + wi * 16:256 + wi * 16 + 15] = (1.0 / np.minimum(np.arange(1, 16), win)).astype(f)[None, :]
    cst[:, 319] = EPS
    return dict(colv=colv, fing=fing, wsT=wsT, rows3=rows3, cst=cst)


_CACHE = {}


def run_model(x, mem, params, NL, ncores, trace=False):
    T = x.shape[1]
    NT = T // TT
    key = (NL, NT)
    if key not in _CACHE:
        with contextlib.ExitStack() as st:
            nc, S = build_program(NL, NT)
            S.emit(nc, st)
        _CACHE[key] = nc
    nc = _CACHE[key]
    hl = host_layout(NL, params["norm_g"], params["gmlp_ln_g"], params["gmlp_ln_b"], params["gmlp_ws"],
                     params["gmlp_bs"], params["conv_w"], params["conv_b"], params["conv_ln_g"],
                     params["conv_ln_b"], params["pool_scale"], params["mem_norm_g"], params["final_g"])
    shared = dict(hl)
    f = np.float32
    shared["w_in"] = np.asarray(params["w_in"][:NL], f)
    shared["w_kv"] = np.asarray(params["w_kv"][:NL], f)
    shared["w_branch"] = np.asarray(params["w_branch"][:NL], f)
    shared["w_out"] = np.asarray(params["w_out"][:NL], f)
    shared["pool_w"] = np.asarray(params["pool_w"][:NL], f)
    in_maps = []
    for c in range(ncores):
        m = dict(shared)
        m["x"] = np.ascontiguousarray(x[c], dtype=f)
        m["mem"] = np.ascontiguousarray(mem[c], dtype=f)
        in_maps.append(m)
    res = run_bass_kernel_spmd(nc, in_maps, core_ids=list(range(ncores)), trace=trace)
    out = np.stack([np.asarray(r["out"]) for r in res.results], axis=0)
    return out, res


def kernel(x, mem, norm_g, w_in, gmlp_ln_g, gmlp_ln_b, gmlp_ws, gmlp_bs, conv_w, conv_b,
           conv_ln_g, conv_ln_b, pool_w, pool_scale, mem_norm_g, w_kv, w_branch, w_out, final_g):
    params = dict(norm_g=norm_g, w_in=w_in, gmlp_ln_g=gmlp_ln_g, gmlp_ln_b=gmlp_ln_b, gmlp_ws=gmlp_ws,
                  gmlp_bs=gmlp_bs, conv_w=conv_w, conv_b=conv_b, conv_ln_g=conv_ln_g, conv_ln_b=conv_ln_b,
                  pool_w=pool_w, pool_scale=pool_scale, mem_norm_g=mem_norm_g, w_kv=w_kv, w_branch=w_branch,
                  w_out=w_out, final_g=final_g)
    x = np.asarray(x)
    mem = np.asarray(mem)
    out, _ = run_model(x, mem, params, NL=4, ncores=8)
    return out.astype(np.float32)
```

```python
import contextlib
import numpy as np
import concourse.bass as bass
import concourse.mybir as mybir
from concourse.bass_utils import run_bass_kernel_spmd

F32 = mybir.dt.float32
BF16 = mybir.dt.bfloat16
AF = mybir.ActivationFunctionType
ALU = mybir.AluOpType

EPOCH_ENG = 8000
EPOCH_DMA = 500

D_MODEL = 2048
W = 1024
IN_COLS = 18432
TT = 512
EPS = 1e-6
A_U, A_V, A_G, B_A, B_B, B_G, C_IN, C_G, M_Q, M_G, GATES = (
    0, 1024, 2048, 3072, 4096, 5120, 6144, 7168, 8192, 9216, 10240)
POOL_WINDOWS = (2, 4, 8, 16)
NCOLV = 312
CV_NG, CV_MG, CV_CB, CV_CLG, CV_CLB, CV_PS, CV_CW = 0, 16, 32, 40, 48, 56, 64


class _Op:
    __slots__ = ("id", "eng", "fn", "deps", "dma", "need_ev", "ev")

    def __init__(self, id, eng, fn, dma):
        self.id = id
        self.eng = eng
        self.fn = fn
        self.dma = dma
        self.deps = set()
        self.need_ev = False
        self.ev = None


class Sched:
    def __init__(self):
        self.ops = []
        self.lastw = {}
        self.readers = {}

    def add(self, eng, fn, reads=(), writes=(), dma=None):
        op = _Op(len(self.ops), eng, fn, dma)
        lastw, readers = self.lastw, self.readers
        deps = set()
        raw = set()
        for k in reads:
            w = lastw.get(k)
            if w is not None:
                deps.add(w)
                raw.add(w)
        for k in writes:
            w = lastw.get(k)
            if w is not None:
                deps.add(w)
            for r in readers.get(k, ()):
                deps.add(r)
        ops = self.ops
        for d in deps:
            p = ops[d]
            if p.dma is None and dma is None and p.eng == eng:
                if eng == "pe" or d not in raw:
                    continue
            op.deps.add(d)
        for k in reads:
            readers.setdefault(k, []).append(op.id)
        for k in writes:
            lastw[k] = op.id
            readers[k] = []
        ops.append(op)
        return op.id

    def emit(self, nc, stack):
        ops = self.ops
        for op in ops:
            for d in op.deps:
                ops[d].need_ev = True
        counters = {}
        semkeys = set()
        for op in ops:
            if not op.need_ev:
                continue
            if op.dma is not None:
                key, inc, ep = ("dma", op.dma), 16, EPOCH_DMA
            else:
                key, inc, ep = ("eng", op.eng), 1, EPOCH_ENG
            c = counters.get(key, 0)
            counters[key] = c + 1
            epoch, idx = divmod(c, ep)
            op.ev = (key, epoch, (idx + 1) * inc)
            semkeys.add((key, epoch))
        sems = {}
        for i, sk in enumerate(sorted(semkeys, key=str)):
            sems[sk] = stack.enter_context(nc.semaphore("s%d" % i))
        self.nsems = len(sems)
        by_eng = {}
        for op in ops:
            by_eng.setdefault(op.eng, []).append(op)

        def run(eng_name, eng):
            known = {}
            for op in by_eng.get(eng_name, ()):
                need = {}
                for d in op.deps:
                    key, epoch, val = ops[d].ev
                    kn = known.get(key)
                    if kn is not None and kn >= (epoch, val):
                        continue
                    cur = need.get(key)
                    if cur is None or cur < (epoch, val):
                        need[key] = (epoch, val)
                for key, (epoch, val) in need.items():
                    eng.wait_ge(sems[(key, epoch)], val)
                    known[key] = (epoch, val)
                inst = op.fn(eng)
                if op.need_ev:
                    key, epoch, val = op.ev
                    inst.then_inc(sems[(key, epoch)], 16 if op.dma is not None else 1)

        with nc.Block() as block:
            @block.sync
            def _(e):
                run("sp", e)

            @block.tensor
            def _(e):
                run("pe", e)

            @block.scalar
            def _(e):
                run("act", e)

            @block.vector
            def _(e):
                run("dve", e)

            @block.gpsimd
            def _(e):
                run("pool", e)


def build_program(NL, NT, final_norm=True):
    CW = 512
    NS = 7
    NTMP = 6
    R = CW // 128
    nc = bass.Bass("TRN2", target_bir_lowering=False)

    def din(name, shape, dt=F32):
        return nc.dram_tensor(name, shape, dt, kind="ExternalInput").ap()

    x_d = din("x", [NT * TT, D_MODEL])
    mem_d = din("mem", [256, D_MODEL])
    w_in_d = din("w_in", [NL, D_MODEL, IN_COLS])
    w_kv_d = din("w_kv", [NL, D_MODEL, 2 * W])
    w_br_d = din("w_branch", [NL, 4, W, D_MODEL])
    w_out_d = din("w_out", [NL, D_MODEL, D_MODEL])
    pool_w_d = din("pool_w", [NL, 4, 256, 256])
    colv_d = din("colv", [NL, 128, NCOLV])
    fing_d = din("fing", [128, 16])
    wsT_d = din("wsT", [NL, 128, 1024])
    rows3_d = din("rows3", [NL, 128, 3072])
    cst_d = din("cst", [128, 320])
    out_d = nc.dram_tensor("out", [NT * TT, D_MODEL], F32, kind="ExternalOutput").ap()
    kvs_d = nc.dram_tensor("kvs", [NL, 128, 4096], BF16).ap()
    dgs_d = nc.dram_tensor("dgs", [NL, 8, 128, 31 * 128], BF16).ap()

    S = Sched()
    off = [16512]

    def alloc(name, shape, dt, at=None):
        esz = 4 if dt == F32 else 2
        n = esz
        for s in shape[1:]:
            n *= s
        if at is None:
            at = off[0]
            off[0] += (n + 63) // 64 * 64
        return nc.alloc_sbuf_tensor_at(name, list(shape), dt, offset=at)

    xT = alloc("xT", [128, 16, TT], F32)
    kvstage = alloc("kvstage", [128, 4096], BF16, at=16512)
    hT_off = off[0]
    hT = alloc("hT", [128, 16, TT], BF16)
    memhT = alloc("memhT", [128, 16, 256], F32, at=hT_off)
    yT_off = off[0]
    yT = alloc("yT", [128, 32, TT], BF16)
    rows3 = alloc("rows3", [128, 3, 1024], F32, at=yT_off + 8 * 1024)
    wb = [alloc("wb%d" % i, [128, 8, CW], BF16) for i in range(NS)]
    A1_off = off[0]
    gv = alloc("gv", [128, 4, 1024], F32)
    cv = alloc("cv", [128, 8, TT], F32, at=A1_off)
    mT = alloc("mT", [128, 16, TT], BF16, at=A1_off)
    xin = alloc("xin", [128, 2, D_MODEL], F32, at=A1_off)
    A2_off = off[0]
    vn = alloc("vn", [128, 4, 1024], BF16)
    diff = alloc("diff", [128, 8, TT], BF16, at=A2_off)
    qT = alloc("qT", [128, 8, TT], BF16, at=A2_off)
    memnTs = [alloc("memnT%d" % i, [128, 16, 256], BF16, at=A1_off + 8192 * i) for i in range(2)]
    TMP = [alloc("tmp%d" % i, [128, 544], F32) for i in range(NTMP)]
    DD = [alloc("dd%d" % i, [128, TT], F32) for i in range(4)]
    expT = [alloc("expT%d" % i, [128, 2, TT], BF16) for i in range(2)]
    convh = alloc("convh", [128, NL, 8, 30], BF16)
    hbb = [alloc("hbb%d" % i, [128, 544], BF16) for i in range(4)]
    identb = alloc("identb", [128, 128], BF16)
    poolh = alloc("poolh", [128, NL, 8, 15], F32)
    colv = alloc("colv", [128, NL, NCOLV], F32)
    fing = alloc("fing", [128, 16], F32)
    wsTb = alloc("wsTb", [128, 8, 128], BF16)
    poolw = alloc("poolw", [128, 4, 2, 256], BF16)
    cst = alloc("cst", [128, 320], F32)
    onesb = alloc("onesb", [128, 128], BF16)
    onesf = alloc("onesf", [128, 128], F32)
    stat = alloc("stat", [128, 64], F32)
    assert off[0] <= 16512 + 212800, off[0]
    print("sbuf used", off[0] - 16512, "of 212863")
    PS = [nc.alloc_psum_tensor("ps%d" % i, [128, TT], F32) for i in range(8)]

    ident = cst[:, 0:128]
    maskT = cst[:, 128:256]

    bk = [0]
    held = set()

    def bank():
        while True:
            b = bk[0] % 8
            bk[0] += 1
            if b not in held:
                return b

    tm = [0]

    def tmp():
        i = tm[0] % NTMP
        tm[0] += 1
        return i

    slot_ctr = [0]

    pinned = set()

    def next_slot():
        while True:
            s = slot_ctr[0] % NS
            slot_ctr[0] += 1
            if s not in pinned:
                return s

    def wload_half(src, eng="pool", extra_reads=(), slot=None):
        if slot is None:
            s = next_slot()
        else:
            s = slot
        if len(src.shape) == 2:
            dst = wb[s][:].rearrange("p k c -> p (k c)")[:, 0:src.shape[1]]
        else:
            dst = wb[s][:]
        S.add(eng, lambda e: e.dma_start(out=dst, in_=src), reads=list(extra_reads), writes=[("wb", s)],
              dma="wb%d" % s)
        return s

    def wload(src, kdim=16, width=CW):
        if kdim == 8:
            return (wload_half(src),)
        return (wload_half(src[:, 0:8, :]), wload_half(src[:, 8:16, :]))

    def wsl(s, k, lo, hi):
        return wb[s[k // 8]][:, k % 8, lo:hi]

    def wkeys(s):
        return [("wb", i) for i in s]

    def mm(b_ap, pairs, reads, writes):
        def fn(e):
            n = len(pairs)
            r = None
            for i, (a, b) in enumerate(pairs):
                r = e.matmul(b_ap, a, b, start=(i == 0), stop=(i == n - 1))
            return r
        S.add("pe", fn, reads=reads, writes=writes)

    def act(out, in_, func, reads, writes, **kw):
        S.add("act", lambda e: e.activation(out=out, in_=in_, func=func, **kw),
              reads=list(reads) + ["cst"], writes=writes)

    def dve(fn, reads, writes):
        S.add("dve", fn, reads=reads, writes=writes)

    def A1k(lo, hi):
        return [("A1", i) for i in range(lo, hi)]

    def A2k(lo, hi):
        return [("A2", i) for i in range(lo, hi)]

    HT_ALL = [("hT", k) for k in range(16)]
    XT_ALL = [("xT", k) for k in range(16)]

    def w_in_v(l):
        return w_in_d[l].rearrange("(k p) c -> p k c", p=128)

    S.add("sp", lambda e: e.dma_start(out=cst[:], in_=cst_d), writes=["cst"], dma="c0")
    S.add("sp", lambda e: e.dma_start(out=colv[:], in_=colv_d.rearrange("l p c -> p l c")),
          writes=["colv"], dma="c1")
    S.add("sp", lambda e: e.dma_start(out=fing[:], in_=fing_d), writes=["fing"], dma="c2")
    dve(lambda e: e.memset(onesf[:], 1.0), [], ["onesf"])
    dve(lambda e: e.memset(onesb[:], 1.0), [], ["onesb"])
    dve(lambda e: e.memset(convh[:], 0.0), [], ["convh"])
    dve(lambda e: e.memset(poolh[:], 0.0), [], ["poolh"])
    dve(lambda e: e.memset(stat[:], 0.0), [], [("stat", i) for i in range(4)])

    dve(lambda e: e.tensor_copy(out=identb[:], in_=ident), ["cst"], ["identb"])
    stg_ctr = [0]

    def build_taps(l):
        for c in range(8):
            j = stg_ctr[0] % 4
            stg_ctr[0] += 1
            dgv = yT[:, 8 * j:8 * j + 8, :].rearrange("p k c -> p (k c)")
            keys = [("yT", 8 * j + i) for i in range(8)]
            for kk in range(31):
                dve(lambda e, dgv=dgv, kk=kk, l=l, c=c: e.tensor_scalar(
                    out=dgv[:, kk * 128:(kk + 1) * 128], in0=identb[:],
                    scalar1=colv[:, l, CV_CW + c * 31 + kk:CV_CW + c * 31 + kk + 1], scalar2=None,
                    op0=ALU.mult), ["identb", "colv"], keys)
            S.add("sp", lambda e, dgv=dgv, l=l, c=c: e.dma_start(out=dgs_d[l, c], in_=dgv[:, 0:31 * 128]),
                  reads=keys, writes=[("dgs", l, c)], dma="dgs%d" % j)

    for mb in range(2):
        S.add("sp", lambda e, mb=mb: e.dma_start(out=xin[:, mb, :], in_=mem_d[mb * 128:(mb + 1) * 128, :]),
              writes=A1k(4 * mb, 4 * mb + 4), dma="xin%d" % mb)
        ti = tmp()
        for hh in range(4):
            act(TMP[ti][:, 0:512], xin[:, mb, hh * 512:(hh + 1) * 512], AF.Square,
                A1k(4 * mb, 4 * mb + 4), [("tmp", ti), ("stat", hh)],
                accum_out=stat[:, hh * 16 + mb:hh * 16 + mb + 1])
        c0 = 8 + mb
        dve(lambda e, mb=mb, c0=c0: e.tensor_tensor(out=stat[:, c0:c0 + 1], in0=stat[:, mb:mb + 1],
                                                    in1=stat[:, 16 + mb:17 + mb], op=ALU.add),
            [("stat", 0), ("stat", 1)], [("stat", "m")])
        dve(lambda e, mb=mb, c0=c0: e.tensor_tensor(out=stat[:, c0 + 2:c0 + 3], in0=stat[:, 32 + mb:33 + mb],
                                                    in1=stat[:, 48 + mb:49 + mb], op=ALU.add),
            [("stat", 2), ("stat", 3)], [("stat", "m2")])
        dve(lambda e, c0=c0: e.tensor_tensor(out=stat[:, c0:c0 + 1], in0=stat[:, c0:c0 + 1],
                                             in1=stat[:, c0 + 2:c0 + 3], op=ALU.add),
            [("stat", "m"), ("stat", "m2")], [("stat", "m3")])
        act(stat[:, c0 + 4:c0 + 5], stat[:, c0:c0 + 1], AF.Sqrt, [("stat", "m3")], [("stat", "m4")],
            scale=1.0 / D_MODEL, bias=cst[:, 319:320])
        dve(lambda e, c0=c0: e.reciprocal(out=stat[:, c0 + 6:c0 + 7], in_=stat[:, c0 + 4:c0 + 5]),
            [("stat", "m4")], [("stat", "m5")])
        dve(lambda e, mb=mb, c0=c0: e.tensor_scalar(out=xin[:, mb, :], in0=xin[:, mb, :],
                                                    scalar1=stat[:, c0 + 6:c0 + 7], scalar2=None, op0=ALU.mult),
            [("stat", "m5")] + A1k(4 * mb, 4 * mb + 4), A1k(4 * mb, 4 * mb + 4))
        for k4 in range(4):
            b = bank()

            def tr(e, mb=mb, k4=k4, b=b):
                r = None
                for j in range(4):
                    k = k4 * 4 + j
                    r = e.transpose(out=PS[b][:, j * 128:(j + 1) * 128], in_=xin[:, mb, k * 128:(k + 1) * 128],
                                    identity=ident)
                return r
            S.add("pe", tr, reads=A1k(4 * mb, 4 * mb + 4) + ["cst"], writes=[("ps", b)])
            act(memhT[:, k4 * 4:k4 * 4 + 4, mb * 128:(mb + 1) * 128],
                PS[b][:].rearrange("p (j t) -> p j t", j=4), AF.Copy, [("ps", b)], HT_ALL)
    def scale_mem(l):
        mt = memnTs[l % 2]
        for k in range(16):
            dve(lambda e, l=l, k=k, mt=mt: e.tensor_scalar(out=mt[:, k, :], in0=memhT[:, k, :],
                                                           scalar1=colv[:, l, CV_MG + k:CV_MG + k + 1],
                                                           scalar2=None, op0=ALU.mult),
                HT_ALL + ["colv"], A1k(4 * (l % 2), 4 * (l % 2) + 4))

    scale_mem(0)
    for l in range(NL):
        if l + 1 < NL:
            scale_mem(l + 1)
        memnT = memnTs[l % 2]
        MK = A1k(4 * (l % 2), 4 * (l % 2) + 4)
        wkv = w_kv_d[l].rearrange("(k p) c -> p k c", p=128)
        kvb = kvstage[:]
        KVK = XT_ALL
        for half in range(2):
            s = wload(wkv[:, :, half * CW:(half + 1) * CW])
            b = bank()
            for r in range(R):
                fb = half * R + r
                pairs = [(wsl(s, k, r * 128, (r + 1) * 128), memnT[:, k, :]) for k in range(16)]
                if r % 2 == 0:
                    b = bank()
                mm(PS[b][:, (r % 2) * 256:(r % 2) * 256 + 256], pairs, wkeys(s) + MK, [("ps", b)])
                if r % 2 == 1:
                    act(kvb[:, (fb - 1) * 256:(fb + 1) * 256], PS[b][:], AF.Copy, [("ps", b)], KVK)
        for half in range(2):
            s = wload(wkv[:, :, W + half * CW:W + (half + 1) * CW])
            for mb in range(2):
                b = bank()
                pairs = [(memnT[:, k, mb * 128:(mb + 1) * 128], wsl(s, k, 0, CW)) for k in range(16)]
                mm(PS[b][:], pairs, wkeys(s) + MK, [("ps", b)])
                o0 = 2048 + mb * 1024 + half * 512
                act(kvb[:, o0:o0 + 512], PS[b][:], AF.Copy, [("ps", b)], KVK)
        S.add("sp", lambda e, l=l, kvb=kvb: e.dma_start(out=kvs_d[l], in_=kvb), reads=KVK, writes=[("kvs", l)],
              dma="kvs")
        build_taps(l)

    def rms_tail(b, dst):
        ti = tmp()
        act(TMP[ti][:, 0:512], PS[b][:], AF.Sqrt, [("ps", b)], [("tmp", ti)],
            scale=1.0 / D_MODEL, bias=cst[:, 319:320])
        dve(lambda e, ti=ti: e.reciprocal(out=dst, in_=TMP[ti][:, 0:512]), [("tmp", ti)], [("D", 0)])

    def hilo(src_ap, src_keys):
        th = tmp()
        hv = TMP[th][:].bitcast(BF16)
        dve(lambda e, hv=hv: e.tensor_copy(out=hv[:, 0:512], in_=src_ap), list(src_keys), [("tmp", th)])
        dve(lambda e, hv=hv: e.tensor_tensor(out=hv[:, 512:1024], in0=src_ap, in1=hv[:, 0:512],
                                             op=ALU.subtract), list(src_keys) + [("tmp", th)], [("tmp", th)])
        return th

    def stat_mm(b, th, first, last):
        hv = TMP[th][:].bitcast(BF16)

        def fn(e):
            e.matmul(PS[b][:], onesb[:], hv[:, 0:512], start=first, stop=False)
            return e.matmul(PS[b][:], onesb[:], hv[:, 512:1024], start=False, stop=last)
        S.add("pe", fn, reads=[("tmp", th), "onesb"], writes=[("ps", b)])

    def rms_sq_acc(k):
        if k == 0:
            act(DD[1][:], xT[:, k, :], AF.Square, [("xT", k)], [("D", 1)])
        else:
            ti = tmp()
            act(TMP[ti][:, 0:512], xT[:, k, :], AF.Square, [("xT", k)], [("tmp", ti)])
            dve(lambda e, ti=ti: e.tensor_tensor(out=DD[1][:], in0=DD[1][:], in1=TMP[ti][:, 0:512], op=ALU.add),
                [("D", 1), ("tmp", ti)], [("D", 1)])

    def rms_finish(dst):
        th = hilo(DD[1][:], [("D", 1)])
        b = bank()
        stat_mm(b, th, True, True)
        rms_tail(b, dst)

    def rms_stats(dst):
        for k in range(16):
            rms_sq_acc(k)
        rms_finish(dst)

    pend_rms = [None]

    for t in range(NT):
        for tb in range(4):
            xi = tb % 2
            r0 = t * TT + tb * 128
            S.add("sp", lambda e, xi=xi, r0=r0: e.dma_start(out=xin[:, xi, :], in_=x_d[r0:r0 + 128, :]),
                  writes=A1k(4 * xi, 4 * xi + 4), dma="xin%d" % xi)
            for k4 in range(4):
                b = bank()

                def tr(e, xi=xi, k4=k4, b=b):
                    r = None
                    for j in range(4):
                        k = k4 * 4 + j
                        r = e.transpose(out=PS[b][:, j * 128:(j + 1) * 128],
                                        in_=xin[:, xi, k * 128:(k + 1) * 128], identity=ident)
                    return r
                S.add("pe", tr, reads=A1k(4 * xi, 4 * xi + 4) + ["cst"], writes=[("ps", b)])
                dve(lambda e, k4=k4, tb=tb, b=b: e.tensor_copy(
                    out=xT[:, k4 * 4:k4 * 4 + 4, tb * 128:(tb + 1) * 128],
                    in_=PS[b][:].rearrange("p (j t) -> p j t", j=4)),
                    [("ps", b)], [("xT", k4 * 4 + j) for j in range(4)])

        for l in range(NL):
            wv = w_in_v(l)

            def cv_col(c0, l=l):
                return colv[:, l, c0:c0 + 1]

            if pend_rms[0] is None:
                rms_stats(DD[0][:])
            else:
                rms_finish(DD[0][:])
                pend_rms[0] = None
            for k in range(16):
                dve(lambda e, k=k, l=l: e.scalar_tensor_tensor(
                    out=hT[:, k, :], in0=xT[:, k, :], scalar=colv[:, l, CV_NG + k:CV_NG + k + 1],
                    in1=DD[0][:], op0=ALU.mult, op1=ALU.mult),
                    [("xT", k), ("D", 0), "colv"], [("hT", k)])

            def proj_block(s, r, b):
                pairs = [(wsl(s, k, r * 128, (r + 1) * 128), hT[:, k, :]) for k in range(16)]
                mm(PS[b][:], pairs, wkeys(s) + HT_ALL, [("ps", b)])

            S.add("sp", lambda e, l=l: e.dma_start(out=rows3[:], in_=rows3_d[l].rearrange("p (a b) -> p a b", a=3)),
                  writes=[("yT", c) for c in range(8, 20)], dma="rows3")
            kvslot = wload_half(kvs_d[l], eng="sp", extra_reads=[("kvs", l)])
            kvb = wb[kvslot][:].rearrange("p k c -> p (k c)")
            KT = kvb[:, 0:2048].rearrange("p (c m) -> p c m", c=8)
            VV = kvb[:, 2048:4096].rearrange("p (b f) -> p b f", b=2)
            KVK = [("wb", kvslot)]
            pinned.add(kvslot)

            sq = [wload(wv[:, :, M_Q + i * CW:M_Q + (i + 1) * CW]) for i in range(2)]
            for c in range(8):
                b = bank()
                proj_block(sq[c // R], c % R, b)
                act(qT[:, c, :], PS[b][:], AF.Copy, [("ps", b)], A2k(c // 2, c // 2 + 1))
            sg = [None]

            def m_scores(h):
                eb = h % 2
                for mb in range(2):
                    b = bank()
                    pairs = [(KT[:, 2 * h + dc, mb * 128:(mb + 1) * 128], qT[:, 2 * h + dc, :]) for dc in range(2)]
                    mm(PS[b][:], pairs, KVK + A2k(h, h + 1), [("ps", b)])
                    act(expT[eb][:, mb, :], PS[b][:], AF.Exp, [("ps", b)], [("expT", eb)], scale=0.0625)

            def m_tail(h, l=l, wv=wv):
                eb = h % 2
                b = bank()
                mm(PS[b][:], [(onesb[:], expT[eb][:, mb, :]) for mb in range(2)], [("expT", eb), "onesb"],
                   [("ps", b)])
                di = 1 + eb
                dve(lambda e, b=b, di=di: e.reciprocal(out=DD[di][:], in_=PS[b][:]), [("ps", b)], [("D", di)])
                for dblk in range(2):
                    c = 2 * h + dblk
                    if c % R == 0:
                        sg[0] = wload(wv[:, :, M_G + (c // R) * CW:M_G + (c // R + 1) * CW])
                    bo = bank()
                    pairs = [(VV[:, mb, c * 128:(c + 1) * 128], expT[eb][:, mb, :]) for mb in range(2)]
                    mm(PS[bo][:], pairs, KVK + [("expT", eb)], [("ps", bo)])
                    bg = bank()
                    proj_block(sg[0], c % R, bg)
                    t2 = tmp()
                    act(TMP[t2][:, 0:512], PS[bg][:], AF.Silu, [("ps", bg)], [("tmp", t2)])
                    dve(lambda e, t2=t2, di=di: e.tensor_tensor(out=TMP[t2][:, 0:512], in0=TMP[t2][:, 0:512],
                                                                in1=DD[di][:], op=ALU.mult),
                        [("tmp", t2), ("D", di)], [("tmp", t2)])
                    dve(lambda e, t2=t2, bo=bo, c=c: e.tensor_tensor(out=yT[:, 24 + c, :], in0=PS[bo][:],
                                                                     in1=TMP[t2][:, 0:512], op=ALU.mult),
                        [("tmp", t2), ("ps", bo)], [("yT", 24 + c)])

            for h in range(4):
                m_scores(h)
                if h > 0:
                    m_tail(h - 1)
            m_tail(3)
            pinned.discard(kvslot)

            for hh in range(2):
                ti = tmp()
                S.add("sp", lambda e, l=l, hh=hh, ti=ti: e.dma_start(out=TMP[ti][:, 0:512],
                                                                   in_=wsT_d[l, :, hh * 512:(hh + 1) * 512]),
                      writes=[("tmp", ti)], dma="wsT%d" % hh)
                for g4 in range(4):
                    g = hh * 4 + g4
                    dve(lambda e, ti=ti, g=g, g4=g4: e.tensor_tensor(out=wsTb[:, g, :],
                                                                      in0=TMP[ti][:, g4 * 128:(g4 + 1) * 128],
                                                                      in1=maskT, op=ALU.mult),
                        [("tmp", ti), "cst"], [("wsTb", g)])
            sv = [wload(wv[:, :, A_V + i * CW:A_V + (i + 1) * CW]) for i in range(2)]
            for tb in range(4):
                for half in range(2):
                    b = bank()
                    pairs = [(hT[:, k, tb * 128:(tb + 1) * 128], wsl(sv[half], k, 0, CW)) for k in range(16)]
                    mm(PS[b][:], pairs, wkeys(sv[half]) + HT_ALL, [("ps", b)])
                    act(gv[:, tb, half * 512:(half + 1) * 512], PS[b][:], AF.Gelu_apprx_tanh,
                        [("ps", b)], [("A1", tb * 2 + half)])
            RY = [("yT", c) for c in range(8, 20)]
            for tb in range(4):
                sc = tb * 16
                gk = A1k(tb * 2, tb * 2 + 2)
                dve(lambda e, tb=tb, sc=sc: e.bn_stats(out=stat[:, sc:sc + 6], in_=gv[:, tb, 0:512]),
                    gk, [("stat", tb)])
                dve(lambda e, tb=tb, sc=sc: e.bn_stats(out=stat[:, sc + 6:sc + 12], in_=gv[:, tb, 512:1024]),
                    gk, [("stat", tb, 1)])
                dve(lambda e, sc=sc: e.bn_aggr(out=stat[:, sc + 12:sc + 14], in_=stat[:, sc:sc + 12]),
                    [("stat", tb), ("stat", tb, 1)], [("stat", tb, 2)])
                act(stat[:, sc + 14:sc + 15], stat[:, sc + 13:sc + 14], AF.Sqrt, [("stat", tb, 2)],
                    [("stat", tb, 3)], scale=1.0, bias=cst[:, 319:320])
                dve(lambda e, sc=sc: e.reciprocal(out=stat[:, sc + 15:sc + 16], in_=stat[:, sc + 14:sc + 15]),
                    [("stat", tb, 3)], [("stat", tb, 4)])
                dve(lambda e, tb=tb, sc=sc: e.tensor_scalar(out=gv[:, tb, :], in0=gv[:, tb, :],
                                                            scalar1=stat[:, sc + 12:sc + 13],
                                                            scalar2=stat[:, sc + 15:sc + 16],
                                                            op0=ALU.subtract, op1=ALU.mult),
                    gk + [("stat", tb, 2), ("stat", tb, 4)], gk)
                dve(lambda e, tb=tb: e.tensor_tensor(out=gv[:, tb, :], in0=gv[:, tb, :], in1=rows3[:, 0, :],
                                                     op=ALU.mult), gk + RY, gk)
                dve(lambda e, tb=tb: e.tensor_tensor(out=vn[:, tb, :], in0=gv[:, tb, :], in1=rows3[:, 1, :],
                                                     op=ALU.add), gk + RY, A2k(tb, tb + 1))
            su = [None]
            sgA = [None]
            t1s = {}

            def a_stage1(c, l=l, wv=wv):
                if c % R == 0:
                    su[0] = wload(wv[:, :, A_U + (c // R) * CW:A_U + (c // R + 1) * CW])
                    sgA[0] = wload(wv[:, :, A_G + (c // R) * CW:A_G + (c // R + 1) * CW])
                bu = bank()
                proj_block(su[0], c % R, bu)
                t1 = tmp()
                act(TMP[t1][:, 0:512], PS[bu][:], AF.Gelu_apprx_tanh, [("ps", bu)], [("tmp", t1)])
                bg = bank()
                proj_block(sgA[0], c % R, bg)
                t2 = tmp()
                act(TMP[t2][:, 0:512], PS[bg][:], AF.Silu, [("ps", bg)], [("tmp", t2)])
                dve(lambda e, t1=t1, t2=t2: e.tensor_tensor(out=TMP[t1][:, 0:512], in0=TMP[t1][:, 0:512],
                                                            in1=TMP[t2][:, 0:512], op=ALU.mult),
                    [("tmp", t1), ("tmp", t2)], [("tmp", t1)])
                t1s[c] = t1

            def a_stage2(c):
                t1 = t1s[c]
                bs_ = bank()

                def sp_mm(e, c=c, bs_=bs_):
                    r = None
                    for tb in range(4):
                        r = e.matmul(PS[bs_][:, tb * 128:(tb + 1) * 128], vn[:, tb, c * 128:(c + 1) * 128],
                                     wsTb[:, c, :], start=True, stop=True)
                    return r
                S.add("pe", sp_mm, reads=A2k(0, 4) + [("wsTb", c)], writes=[("ps", bs_)])
                t3 = tmp()
                for tb in range(4):
                    dve(lambda e, t3=t3, bs_=bs_, c=c, tb=tb: e.tensor_tensor(
                        out=TMP[t3][:, tb * 128:(tb + 1) * 128], in0=PS[bs_][:, tb * 128:(tb + 1) * 128],
                        in1=rows3[:, 2, c * 128:(c + 1) * 128], op=ALU.add),
                        [("ps", bs_)] + RY, [("tmp", t3)])
                dve(lambda e, t1=t1, t3=t3, c=c: e.tensor_tensor(out=yT[:, c, :], in0=TMP[t3][:, 0:512],
                                                                 in1=TMP[t1][:, 0:512], op=ALU.mult),
                    [("tmp", t1), ("tmp", t3)], [("yT", c)])

            for c in range(8):
                a_stage1(c)
                if c > 0:
                    a_stage2(c - 1)
            a_stage2(7)

            for g0 in (0, 4):
                sa = wload(wv[:, :, B_A + (g0 // R) * CW:B_A + (g0 // R + 1) * CW])
                sb = wload(wv[:, :, B_B + (g0 // R) * CW:B_B + (g0 // R + 1) * CW])
                dsl = (next_slot(), next_slot())
                for c in range(g0, g0 + 4):
                    ba = bank()
                    proj_block(sa, c % R, ba)
                    bb = bank()
                    proj_block(sb, c % R, bb)
                    t1 = tmp()
                    act(TMP[t1][:, 0:512], PS[bb][:], AF.Sigmoid, [("ps", bb)], [("tmp", t1)])
                    hi = c % 4
                    act(hbb[hi][:, 0:30], convh[:, l, c, :], AF.Copy, ["convh"], [("hbb", hi)])
                    dve(lambda e, hi=hi, ba=ba, t1=t1: e.tensor_tensor(out=hbb[hi][:, 30:542], in0=PS[ba][:],
                                                                       in1=TMP[t1][:, 0:512], op=ALU.mult),
                        [("ps", ba), ("tmp", t1)], [("hbb", hi)])
                    act(convh[:, l, c, :], hbb[hi][:, 512:542], AF.Copy, [("hbb", hi)], ["convh"])
                for c in range(g0, g0 + 4):
                    hi = c % 4
                    sd = wload_half(dgs_d[l, c], eng="sp", extra_reads=[("dgs", l, c)], slot=dsl[c % 2])
                    bc = bank()
                    dgv = wb[sd][:].rearrange("p k c -> p (k c)")
                    pairs = [(dgv[:, kk * 128:(kk + 1) * 128], hbb[hi][:, kk:kk + 512]) for kk in range(31)]
                    mm(PS[bc][:], pairs, [("wb", sd), ("hbb", hi)], [("ps", bc)])
                    dve(lambda e, c=c, bc=bc, l=l: e.tensor_scalar(out=cv[:, c, :], in0=PS[bc][:],
                                                                   scalar1=colv[:, l, CV_CB + c:CV_CB + c + 1],
                                                                   scalar2=None, op0=ALU.add),
                        [("ps", bc), "colv"], [("A1", c)])
            b1 = bank()
            held.add(b1)
            b2 = bank()
            held.add(b2)
            for c in range(8):
                th = hilo(cv[:, c, :], [("A1", c)])
                stat_mm(b1, th, c == 0, c == 7)
                ti = tmp()
                act(TMP[ti][:, 0:512], cv[:, c, :], AF.Square, [("A1", c)], [("tmp", ti)])
                th2 = hilo(TMP[ti][:, 0:512], [("tmp", ti)])
                stat_mm(b2, th2, c == 0, c == 7)
            held.discard(b1)
            held.discard(b2)
            dve(lambda e, b1=b1: e.tensor_scalar(out=DD[0][:], in0=PS[b1][:], scalar1=1.0 / W, scalar2=None,
                                                 op0=ALU.mult), [("ps", b1)], [("D", 0)])
            tq = tmp()
            dve(lambda e, tq=tq: e.tensor_tensor(out=TMP[tq][:, 0:512], in0=DD[0][:], in1=DD[0][:], op=ALU.mult),
                [("D", 0)], [("tmp", tq)])
            dve(lambda e, tq=tq, b2=b2: e.scalar_tensor_tensor(out=TMP[tq][:, 0:512], in0=PS[b2][:],
                                                               scalar=1.0 / W, in1=TMP[tq][:, 0:512],
                                                               op0=ALU.mult, op1=ALU.subtract),
                [("ps", b2), ("tmp", tq)], [("tmp", tq)])
            act(TMP[tq][:, 0:512], TMP[tq][:, 0:512], AF.Sqrt, [("tmp", tq)], [("tmp", tq)],
                scale=1.0, bias=cst[:, 319:320])
            dve(lambda e, tq=tq: e.reciprocal(out=DD[1][:], in_=TMP[tq][:, 0:512]), [("tmp", tq)], [("D", 1)])
            sgB = [None]
            for c in range(8):
                if c % R == 0:
                    sgB[0] = wload(wv[:, :, B_G + (c // R) * CW:B_G + (c // R + 1) * CW])
                t1 = tmp()
                dve(lambda e, t1=t1, c=c: e.tensor_tensor(out=TMP[t1][:, 0:512], in0=cv[:, c, :], in1=DD[0][:],
                                                          op=ALU.subtract), [("A1", c), ("D", 0)], [("tmp", t1)])
                dve(lambda e, t1=t1: e.tensor_tensor(out=TMP[t1][:, 0:512], in0=TMP[t1][:, 0:512], in1=DD[1][:],
                                                     op=ALU.mult), [("tmp", t1), ("D", 1)], [("tmp", t1)])
                act(TMP[t1][:, 0:512], TMP[t1][:, 0:512], AF.Silu, [("tmp", t1), "colv"], [("tmp", t1)],
                    scale=cv_col(CV_CLG + c), bias=cv_col(CV_CLB + c))
                bg = bank()
                proj_block(sgB[0], c % R, bg)
                t2 = tmp()
                act(TMP[t2][:, 0:512], PS[bg][:], AF.Silu, [("ps", bg)], [("tmp", t2)])
                dve(lambda e, t1=t1, t2=t2, c=c: e.tensor_tensor(out=yT[:, 8 + c, :], in0=TMP[t1][:, 0:512],
                                                                 in1=TMP[t2][:, 0:512], op=ALU.mult),
                    [("tmp", t1), ("tmp", t2)], [("yT", 8 + c)])

            S.add("pool", lambda e, l=l: e.dma_start(
                out=poolw[:], in_=pool_w_d[l].rearrange("g (k p) d -> p g k d", p=128)),
                writes=["poolw"], dma="poolw")
            sc_ = [None]
            for c in range(8):
                if c % R == 0:
                    sc_[0] = wload(wv[:, :, C_IN + (c // R) * CW:C_IN + (c // R + 1) * CW])
                b = bank()
                proj_block(sc_[0], c % R, b)
                hb = tmp()
                act(TMP[hb][:, 0:15], poolh[:, l, c, :], AF.Copy, ["poolh"], [("tmp", hb)])
                act(TMP[hb][:, 15:527], PS[b][:], AF.Copy, [("ps", b)], [("tmp", hb)])
                act(poolh[:, l, c, :], TMP[hb][:, 512:527], AF.Copy, [("tmp", hb)], ["poolh"])
                wi = c // 2
                win = POOL_WINDOWS[wi]
                src = hb
                lo = 0
                sh = 1
                while sh < win:
                    dst = tmp()
                    lo2 = lo + sh
                    dve(lambda e, src=src, dst=dst, lo2=lo2, sh=sh: e.tensor_tensor(
                        out=TMP[dst][:, lo2:527], in0=TMP[src][:, lo2:527], in1=TMP[src][:, lo2 - sh:527 - sh],
                        op=ALU.add), [("tmp", src)], [("tmp", dst)])
                    src = dst
                    lo = lo2
                    sh *= 2
                dve(lambda e, src=src, hb=hb, c=c, win=win: e.scalar_tensor_tensor(
                    out=diff[:, c, :], in0=TMP[src][:, 15:527], scalar=1.0 / win, in1=TMP[hb][:, 15:527],
                    op0=ALU.mult, op1=ALU.subtract), [("tmp", src), ("tmp", hb)], A2k(c // 2, c // 2 + 1))
                if t == 0:
                    t5 = tmp()
                    dve(lambda e, src=src, t5=t5, wi=wi: e.tensor_tensor(
                        out=TMP[t5][:, 0:15], in0=TMP[src][:, 15:30], in1=cst[:, 256 + wi * 16:256 + wi * 16 + 15],
                        op=ALU.mult), [("tmp", src), "cst"], [("tmp", t5)])
                    dve(lambda e, t5=t5, hb=hb, c=c: e.tensor_tensor(
                        out=diff[:, c, 0:15], in0=TMP[t5][:, 0:15], in1=TMP[hb][:, 15:30], op=ALU.subtract),
                        [("tmp", t5), ("tmp", hb)] + A2k(c // 2, c // 2 + 1), A2k(c // 2, c // 2 + 1))
            sgC = [None]
            for c in range(8):
                if c % R == 0:
                    sgC[0] = wload(wv[:, :, C_G + (c // R) * CW:C_G + (c // R + 1) * CW])
                g, db = c // 2, c % 2
                bp = bank()
                pairs = [(poolw[:, g, k, db * 128:(db + 1) * 128], diff[:, 2 * g + k, :]) for k in range(2)]
                mm(PS[bp][:], pairs, ["poolw"] + A2k(g, g + 1), [("ps", bp)])
                bg = bank()
                proj_block(sgC[0], c % R, bg)
                t2 = tmp()
                act(TMP[t2][:, 0:512], PS[bg][:], AF.Silu, [("ps", bg)], [("tmp", t2)])
                dve(lambda e, t2=t2, bp=bp, c=c, l=l: e.scalar_tensor_tensor(
                    out=yT[:, 16 + c, :], in0=PS[bp][:], scalar=colv[:, l, CV_PS + c:CV_PS + c + 1],
                    in1=TMP[t2][:, 0:512], op0=ALU.mult, op1=ALU.mult),
                    [("ps", bp), ("tmp", t2), "colv"], [("yT", 16 + c)])

            for dg in range(D_MODEL // CW):
                for n in range(4):
                    sgt = wload(wv[:, :, GATES + n * D_MODEL + dg * CW:GATES + n * D_MODEL + (dg + 1) * CW])
                    sbr = wload(w_br_d[l, n].rearrange("(k p) c -> p k c", p=128)[:, :, dg * CW:(dg + 1) * CW],
                                kdim=8)
                    for r in range(R):
                        bp = bank()
                        pairs = [(wsl(sbr, k, r * 128, (r + 1) * 128), yT[:, n * 8 + k, :]) for k in range(8)]
                        mm(PS[bp][:], pairs, wkeys(sbr) + [("yT", n * 8 + k) for k in range(8)], [("ps", bp)])
                        bg = bank()
                        proj_block(sgt, r, bg)
                        t2 = tmp()
                        act(TMP[t2][:, 0:512], PS[bg][:], AF.Sigmoid, [("ps", bg)], [("tmp", t2)])
                        mi = dg * R + r
                        if n == 0:
                            dve(lambda e, r=r, bp=bp, t2=t2: e.tensor_tensor(out=DD[r][:], in0=PS[bp][:],
                                                                             in1=TMP[t2][:, 0:512], op=ALU.mult),
                                [("ps", bp), ("tmp", t2)], [("D", r)])
                        else:
                            dve(lambda e, bp=bp, t2=t2: e.tensor_tensor(out=TMP[t2][:, 0:512], in0=PS[bp][:],
                                                                        in1=TMP[t2][:, 0:512], op=ALU.mult),
                                [("ps", bp), ("tmp", t2)], [("tmp", t2)])
                            if n < 3:
                                dve(lambda e, r=r, t2=t2: e.tensor_tensor(out=DD[r][:], in0=DD[r][:],
                                                                          in1=TMP[t2][:, 0:512], op=ALU.add),
                                    [("D", r), ("tmp", t2)], [("D", r)])
                            else:
                                dve(lambda e, r=r, t2=t2, mi=mi: e.tensor_tensor(out=mT[:, mi, :], in0=DD[r][:],
                                                                                 in1=TMP[t2][:, 0:512],
                                                                                 op=ALU.add),
                                    [("D", r), ("tmp", t2)], [("A1", mi // 2)])

            wo = w_out_d[l].rearrange("(k p) c -> p k c", p=128)
            so = [None]
            pend_rms[0] = True
            for eb_ in range(16):
                if eb_ % R == 0:
                    so[0] = wload(wo[:, :, (eb_ // R) * CW:(eb_ // R + 1) * CW])
                b = bank()
                pairs = [(wsl(so[0], k, (eb_ % R) * 128, (eb_ % R + 1) * 128), mT[:, k, :]) for k in range(16)]
                mm(PS[b][:], pairs, wkeys(so[0]) + A1k(0, 8), [("ps", b)])
                dve(lambda e, eb_=eb_, b=b: e.tensor_tensor(out=xT[:, eb_, :], in0=PS[b][:], in1=xT[:, eb_, :],
                                                            op=ALU.add), [("ps", b), ("xT", eb_)], [("xT", eb_)])
                if final_norm or l < NL - 1:
                    rms_sq_acc(eb_)
            if not final_norm and l == NL - 1:
                pend_rms[0] = None

        if final_norm:
            rms_finish(DD[0][:])
            pend_rms[0] = None
            for k in range(16):
                dve(lambda e, k=k: e.scalar_tensor_tensor(out=xT[:, k, :], in0=xT[:, k, :],
                                                          scalar=fing[:, k:k + 1], in1=DD[0][:],
                                                          op0=ALU.mult, op1=ALU.mult),
                    [("xT", k), ("D", 0), "fing"], [("xT", k)])
        for tb in range(4):
            xi = tb % 2
            for k4 in range(4):
                b = bank()

                def tr2(e, tb=tb, k4=k4, b=b):
                    r = None
                    for j in range(4):
                        k = k4 * 4 + j
                        r = e.transpose(out=PS[b][:, j * 128:(j + 1) * 128],
                                        in_=xT[:, k, tb * 128:(tb + 1) * 128], identity=ident)
                    return r
                S.add("pe", tr2, reads=[("xT", k4 * 4 + j) for j in range(4)] + ["cst"], writes=[("ps", b)])
                dve(lambda e, xi=xi, k4=k4, b=b: e.tensor_copy(out=xin[:, xi, k4 * 512:(k4 + 1) * 512],
                                                               in_=PS[b][:]),
                    [("ps", b)], A1k(4 * xi + k4, 4 * xi + k4 + 1))
            r0 = t * TT + tb * 128
            S.add("sp", lambda e, xi=xi, r0=r0: e.dma_start(out=out_d[r0:r0 + 128, :], in_=xin[:, xi, :]),
                  reads=A1k(4 * xi, 4 * xi + 4), writes=[("out", t, tb)], dma="out%d" % xi)
    S.add("sp", lambda e: e.nop(), reads=[("out", t, tb) for t in range(NT) for tb in range(4)])
    return nc, S


def host_layout(NL, norm_g, gmlp_ln_g, gmlp_ln_b, gmlp_ws, gmlp_bs, conv_w, conv_b, conv_ln_g, conv_ln_b,
                pool_scale, mem_norm_g, final_g):
    f = np.float32

    def cols(v, n):
        return np.asarray(v, f).reshape(n, 128).T

    colv = np.zeros((NL, 128, NCOLV), f)
    for l in range(NL):
        colv[l, :, CV_NG:CV_NG + 16] = cols(norm_g[l], 16)
        colv[l, :, CV_MG:CV_MG + 16] = cols(mem_norm_g[l], 16)
        colv[l, :, CV_CB:CV_CB + 8] = cols(conv_b[l], 8)
        colv[l, :, CV_CLG:CV_CLG + 8] = cols(conv_ln_g[l], 8)
        colv[l, :, CV_CLB:CV_CLB + 8] = cols(conv_ln_b[l], 8)
        colv[l, :, CV_PS:CV_PS + 8] = cols(pool_scale[l], 8)
        cw = np.asarray(conv_w[l], f)
        colv[l, :, CV_CW:CV_CW + 248] = cw.reshape(31, 8, 128).transpose(2, 1, 0).reshape(128, 248)
    fing = cols(final_g, 16).copy()
    wsT = np.ascontiguousarray(np.asarray(gmlp_ws[:NL], f).transpose(0, 3, 1, 2).reshape(NL, 128, 1024))
    rows3 = np.zeros((NL, 128, 3072), f)
    for l in range(NL):
        rows3[l, :, 0:1024] = np.asarray(gmlp_ln_g[l], f)[None, :]
        rows3[l, :, 1024:2048] = np.asarray(gmlp_ln_b[l], f)[None, :]
        rows3[l, :, 2048:3072] = np.asarray(gmlp_bs[l], f).reshape(1, 1024)
    cst = np.zeros((128, 320), f)
    cst[:, 0:128] = np.eye(128, dtype=f)
    s_idx = np.arange(128)[:, None]
    t_idx = np.arange(128)[None, :]
    cst[:, 128:256] = (s_idx <= t_idx).astype(f)
    for wi, win in enumerate(POOL_WINDOWS):
        cst[:, 256 + wi * 16:256 + wi * 16 + 15] = (1.0 / np.minimum(np.arange(1, 16), win)).astype(f)[None, :]
    cst[:, 319] = EPS
    return dict(colv=colv, fing=fing, wsT=wsT, rows3=rows3, cst=cst)


_CACHE = {}


def run_model(x, mem, params, NL, ncores, trace=False):
    T = x.shape[1]
    NT = T // TT
    key = (NL, NT)
    if key not in _CACHE:
        with contextlib.ExitStack() as st:
            nc, S = build_program(NL, NT)
            S.emit(nc, st)
        _CACHE[key] = nc
    nc = _CACHE[key]
    hl = host_layout(NL, params["norm_g"], params["gmlp_ln_g"], params["gmlp_ln_b"], params["gmlp_ws"],
                     params["gmlp_bs"], params["conv_w"], params["conv_b"], params["conv_ln_g"],
                     params["conv_ln_b"], params["pool_scale"], params["mem_norm_g"], params["final_g"])
    shared = dict(hl)
    f = np.float32
    shared["w_in"] = np.asarray(params["w_in"][:NL], f)
    shared["w_kv"] = np.asarray(params["w_kv"][:NL], f)
    shared["w_branch"] = np.asarray(params["w_branch"][:NL], f)
    shared["w_out"] = np.asarray(params["w_out"][:NL], f)
    shared["pool_w"] = np.asarray(params["pool_w"][:NL], f)
    in_maps = []
    for c in range(ncores):
        m = dict(shared)
        m["x"] = np.ascontiguousarray(x[c], dtype=f)
        m["mem"] = np.ascontiguousarray(mem[c], dtype=f)
        in_maps.append(m)
    res = run_bass_kernel_spmd(nc, in_maps, core_ids=list(range(ncores)), trace=trace)
    out = np.stack([np.asarray(r["out"]) for r in res.results], axis=0)
    return out, res


def kernel(x, mem, norm_g, w_in, gmlp_ln_g, gmlp_ln_b, gmlp_ws, gmlp_bs, conv_w, conv_b,
           conv_ln_g, conv_ln_b, pool_w, pool_scale, mem_norm_g, w_kv, w_branch, w_out, final_g):
    params = dict(norm_g=norm_g, w_in=w_in, gmlp_ln_g=gmlp_ln_g, gmlp_ln_b=gmlp_ln_b, gmlp_ws=gmlp_ws,
                  gmlp_bs=gmlp_bs, conv_w=conv_w, conv_b=conv_b, conv_ln_g=conv_ln_g, conv_ln_b=conv_ln_b,
                  pool_w=pool_w, pool_scale=pool_scale, mem_norm_g=mem_norm_g, w_kv=w_kv, w_branch=w_branch,
                  w_out=w_out, final_g=final_g)
    x = np.asarray(x)
    mem = np.asarray(mem)
    out, _ = run_model(x, mem, params, NL=4, ncores=8)
    return out.astype(np.float32)
```
